# Optimizing a Trainium2 kernel written in Bass

```python
import jax
import jax.numpy as jnp
from jax import lax
import numpy as np

D_MODEL = 1024
BATCH = 8
SEQ = 4096
DEPTH = 2

GRID_W = 64
CTX_LEN = 256
N_EVEN = (DEPTH + 1) // 2
N_ODD = DEPTH // 2
NORM_EPS = 1e-6

CONV_WIDTH = D_MODEL // 2
CONV_TAPS = 3
RWKV_HEAD = 64
RWKV_HEADS = (D_MODEL // 2) // RWKV_HEAD
RWKV_WIDTH = RWKV_HEADS * RWKV_HEAD
DECAY_LORA = 64
ICLR_LORA = 64
RWKV_GN_EPS = 64e-5
A_COLS = 4 * CONV_WIDTH
RWKV_SHIFT_COLS = 3 * RWKV_WIDTH + DECAY_LORA + ICLR_LORA
RWKV_SPLITS = (RWKV_WIDTH, 2 * RWKV_WIDTH, 3 * RWKV_WIDTH, 3 * RWKV_WIDTH + DECAY_LORA)
EV_PROJ = A_COLS + RWKV_SHIFT_COLS + RWKV_WIDTH
EV_MIX = CONV_WIDTH + RWKV_WIDTH

MLA_HEADS = 8
QK_NOPE = 128
QK_ROPE = 64
V_HEAD = 128
Q_LORA = 384
KV_LORA = 256
MLA_WIDTH = MLA_HEADS * V_HEAD
Z_OFF = Q_LORA + KV_LORA + QK_ROPE
OD_PROJ = Z_OFF + MLA_WIDTH
SM_SCALE = (QK_NOPE + QK_ROPE) ** -0.5
ROPE_THETA = 10000.0
ROPE_PAIRS = QK_ROPE // 4
Q_BLOCK = 128

kernel_name = 'hybrid_conv_rwkv7_mla_prefix_dit'


def rms_norm(x, w):
    xf = x.astype(jnp.float32)
    y = xf * lax.rsqrt(jnp.mean(xf * xf, axis=-1, keepdims=True) + NORM_EPS)
    return (y * w.astype(jnp.float32)).astype(x.dtype)


def shift_prev(u):
    return jnp.pad(u, ((0, 0), (1, 0), (0, 0)))[:, :-1]


def shift_next(u):
    return jnp.pad(u, ((0, 0), (0, 1), (0, 0)))[:, 1:]


def to_heads(t):
    return t.reshape(t.shape[:-1] + (RWKV_HEADS, RWKV_HEAD))


def axial_rope_tables(n):
    rows = n // GRID_W
    row = jnp.repeat(jnp.arange(rows, dtype=jnp.float32), GRID_W)
    col = jnp.tile(jnp.arange(GRID_W, dtype=jnp.float32), rows)
    inv_freq = ROPE_THETA ** (-jnp.arange(ROPE_PAIRS, dtype=jnp.float32) / ROPE_PAIRS)
    ang = jnp.stack([row[:, None] * inv_freq, col[:, None] * inv_freq], axis=1)
    return jnp.cos(ang), jnp.sin(ang)


def apply_axial_rope(t, cos, sin):
    tf = t.astype(jnp.float32).reshape(t.shape[:-1] + (2, 2, ROPE_PAIRS))
    t1, t2 = tf[..., 0, :], tf[..., 1, :]
    out = jnp.stack([t1 * cos - t2 * sin, t2 * cos + t1 * sin], axis=-2)
    return out.reshape(t.shape).astype(t.dtype)


def short_conv_branch(pa, conv_w):
    u, gate_b, gate_c, z = jnp.split(pa, 4, axis=-1)
    cu = gate_c * u
    y = conv_w[0] * shift_prev(cu) + conv_w[1] * cu + conv_w[2] * shift_next(cu)
    return gate_b * y * jax.nn.silu(z)


def rwkv_prep(pr, mu, k_k):
    pr = pr.astype(jnp.float32)
    pr = pr + mu * (0.5 * (shift_prev(pr) + shift_next(pr)) - pr)
    r, k, v, wl, al = jnp.split(pr, RWKV_SPLITS, axis=-1)
    kk = to_heads(k * k_k)
    kk = kk / jnp.maximum(jnp.linalg.norm(kk, axis=-1, keepdims=True), 1e-12)
    return to_heads(r), k, to_heads(v), wl, al, kk


def rwkv_direction(k, wl, al, kk, w0, w2, a0, a2, k_a):
    w_log = -jax.nn.softplus(-(w0 + jnp.tanh(wl) @ w2)) - 0.5
    decay = jnp.exp(-jnp.exp(w_log))
    iclr = jax.nn.sigmoid(a0 + al @ a2)
    k_dir = k * (1.0 + (iclr - 1.0) * k_a)
    return to_heads(decay), to_heads(k_dir), kk * to_heads(iclr)


def rwkv7_scan(r, decay, k, v, a_vec, b_vec, s0, reverse):
    def step(S, inp):
        r_t, w_t, k_t, v_t, a_t, b_t = inp
        sa = jnp.einsum('bhij,bhj->bhi', S, a_t)
        S = S * w_t[:, :, None, :] + sa[..., None] * b_t[:, :, None, :] + v_t[..., None] * k_t[:, :, None, :]
        return S, jnp.einsum('bhij,bhj->bhi', S, r_t)
    xs = tuple(jnp.moveaxis(t, 1, 0) for t in (r, decay, k, v, a_vec, b_vec))
    S, ys = lax.scan(step, s0, xs, reverse=reverse)
    return jnp.moveaxis(ys, 0, 1), S


def head_group_norm(y, w, b):
    mean = jnp.mean(y, axis=-1, keepdims=True)
    var = jnp.mean(jnp.square(y - mean), axis=-1, keepdims=True)
    yn = (y - mean) * lax.rsqrt(var + RWKV_GN_EPS)
    return yn * w.reshape(RWKV_HEADS, RWKV_HEAD) + b.reshape(RWKV_HEADS, RWKV_HEAD)


def even_output(p, y, bon, conv_w, lnx_w, lnx_b, w_out):
    a_out = short_conv_branch(p[..., :A_COLS], conv_w)
    z_b = p[..., A_COLS + RWKV_SHIFT_COLS:]
    rw = head_group_norm(y, lnx_w, lnx_b) + bon
    b_out = rw.reshape(rw.shape[:2] + (RWKV_WIDTH,)) * jax.nn.silu(z_b.astype(jnp.float32))
    return jnp.concatenate([a_out, b_out.astype(p.dtype)], axis=-1) @ w_out


def even_mixer(h_ctx, h_lat, w_in, conv_w, mu, k_k, k_a, w0, w2, a0, a2, r_k, lnx_w, lnx_b, w_out, need_ctx):
    p_all = (h_ctx @ w_in, h_lat @ w_in)
    streams = [rwkv_prep(p[..., A_COLS:A_COLS + RWKV_SHIFT_COLS], mu, k_k) for p in p_all]
    batch = h_lat.shape[0]
    y_sum = [0.0, 0.0]
    bonus = [0.0, 0.0]
    for d, reverse in enumerate((False, True)):
        state = jnp.zeros((batch, RWKV_HEADS, RWKV_HEAD, RWKV_HEAD), jnp.float32)
        for s, (r, k, v, wl, al, kk) in enumerate(streams):
            decay, k_dir, b_vec = rwkv_direction(k, wl, al, kk, w0[d], w2[d], a0[d], a2[d], k_a)
            y, state = rwkv7_scan(r, decay, k_dir, v, -kk, b_vec, state, reverse)
            y_sum[s] = y_sum[s] + y
            bonus[s] = bonus[s] + jnp.sum(r * k_dir * r_k, axis=-1, keepdims=True) * v
    y_lat = even_output(p_all[1], y_sum[1], bonus[1], conv_w, lnx_w, lnx_b, w_out)
    y_ctx = even_output(p_all[0], y_sum[0], bonus[0], conv_w, lnx_w, lnx_b, w_out) if need_ctx else None
    return y_ctx, y_lat


def block_attention(q_nope, q_rope, k_nope, k_rope, v):
    batch, n_q = q_nope.shape[:2]
    n_blk = n_q // Q_BLOCK

    def to_blocks(t):
        return jnp.swapaxes(t.reshape((batch, n_blk, Q_BLOCK) + t.shape[2:]), 0, 1)

    def attend(qs):
        qn, qr = qs
        s = jnp.einsum('bqhd,bkhd->bhqk', qn, k_nope) + jnp.einsum('bqhr,bkr->bhqk', qr, k_rope)
        p = jax.nn.softmax(s.astype(jnp.float32), axis=-1).astype(v.dtype)
        return jnp.einsum('bhqk,bkhd->bqhd', p, v)

    o = lax.map(attend, (to_blocks(q_nope), to_blocks(q_rope)))
    return jnp.swapaxes(o, 0, 1).reshape((batch, n_q) + o.shape[3:])


def odd_mixer(h_ctx, h_lat, w_in, q_a_norm, kv_a_norm, w_qb, w_kvb, gq_nope, gq_rope, gk_nope, gk_rope, w_o, cos, sin, need_ctx):
    def keys_values(kv_in, rotary):
        kv_a, k_rope = kv_in[..., :KV_LORA], kv_in[..., KV_LORA:]
        kv = rms_norm(kv_a, kv_a_norm) @ w_kvb
        kv = kv.reshape(kv.shape[:2] + (MLA_HEADS, QK_NOPE + V_HEAD))
        k_nope = rms_norm(kv[..., :QK_NOPE], gk_nope)
        k_rope = rms_norm(k_rope, gk_rope)
        if rotary:
            k_rope = apply_axial_rope(k_rope, cos, sin)
        return k_nope, k_rope, kv[..., QK_NOPE:]

    def queries(q_a, rotary):
        q = rms_norm(q_a, q_a_norm) @ w_qb
        q = q.reshape(q.shape[:2] + (MLA_HEADS, QK_NOPE + QK_ROPE))
        q_nope = rms_norm(q[..., :QK_NOPE], gq_nope) * SM_SCALE
        q_rope = rms_norm(q[..., QK_NOPE:], gq_rope) * SM_SCALE
        if rotary:
            q_rope = apply_axial_rope(q_rope, cos[:, None], sin[:, None])
        return q_nope, q_rope

    def out(o, z):
        return (o.reshape(o.shape[:2] + (MLA_WIDTH,)) * jax.nn.silu(z)) @ w_o

    kn_c, kr_c, v_c = keys_values(h_ctx @ w_in[:, Q_LORA:Z_OFF], False)
    p = h_lat @ w_in
    kn_l, kr_l, v_l = keys_values(p[..., Q_LORA:Z_OFF], True)
    qn, qr = queries(p[..., :Q_LORA], True)
    k_nope = jnp.concatenate([kn_c, kn_l], axis=1)
    k_rope = jnp.concatenate([kr_c, kr_l], axis=1)
    v = jnp.concatenate([v_c, v_l], axis=1)
    y_lat = out(block_attention(qn, qr, k_nope, k_rope, v), p[..., Z_OFF:])
    y_ctx = None
    if need_ctx:
        qn_c, qr_c = queries(h_ctx @ w_in[:, :Q_LORA], False)
        y_ctx = out(block_attention(qn_c, qr_c, kn_c, kr_c, v_c), h_ctx @ w_in[:, Z_OFF:])
    return y_ctx, y_lat


def setup_inputs(seed: int = 0) -> dict:
    key = jax.random.key(seed)
    ks = iter(jax.random.split(key, 40))

    def nrm(shape, scale):
        return scale * jax.random.normal(next(ks), shape, jnp.float32)

    D = D_MODEL
    frac = jnp.arange(RWKV_WIDTH, dtype=jnp.float32) / (RWKV_WIDTH - 1)
    decay_base = -6.0 + 5.0 * frac ** 0.9
    return {
        'x': nrm((BATCH, SEQ, D), 1.0),
        'c': nrm((BATCH, D), 1.0),
        'ctx': nrm((BATCH, CTX_LEN, D), 1.0),
        'c_ctx': nrm((D,), 1.0),
        'ada_w': nrm((DEPTH, D, 3 * D), 0.5 * D ** -0.5),
        'ada_b': nrm((DEPTH, 3 * D), 0.02),
        'norm_w': 1.0 + nrm((DEPTH, D), 0.02),
        'ev_w_in': nrm((N_EVEN, D, EV_PROJ), D ** -0.5),
        'ev_conv_w': nrm((N_EVEN, CONV_TAPS, CONV_WIDTH), CONV_TAPS ** -0.5),
        'ev_mu': jax.random.uniform(next(ks), (N_EVEN, RWKV_SHIFT_COLS), jnp.float32),
        'ev_k_k': 0.85 + nrm((N_EVEN, RWKV_WIDTH), 0.02),
        'ev_k_a': 1.0 + nrm((N_EVEN, RWKV_WIDTH), 0.02),
        'ev_w0': decay_base + nrm((N_EVEN, 2, RWKV_WIDTH), 0.1),
        'ev_w2': nrm((N_EVEN, 2, DECAY_LORA, RWKV_WIDTH), 0.5 * DECAY_LORA ** -0.5),
        'ev_a0': nrm((N_EVEN, 2, RWKV_WIDTH), 0.1),
        'ev_a2': nrm((N_EVEN, 2, ICLR_LORA, RWKV_WIDTH), 0.5 * ICLR_LORA ** -0.5),
        'ev_r_k': nrm((N_EVEN, RWKV_HEADS, RWKV_HEAD), 0.1),
        'ev_lnx_w': 1.0 + nrm((N_EVEN, RWKV_WIDTH), 0.02),
        'ev_lnx_b': nrm((N_EVEN, RWKV_WIDTH), 0.02),
        'ev_w_out': nrm((N_EVEN, EV_MIX, D), EV_MIX ** -0.5),
        'od_w_in': nrm((N_ODD, D, OD_PROJ), D ** -0.5),
        'od_q_a_norm': 1.0 + nrm((N_ODD, Q_LORA), 0.02),
        'od_kv_a_norm': 1.0 + nrm((N_ODD, KV_LORA), 0.02),
        'od_w_qb': nrm((N_ODD, Q_LORA, MLA_HEADS * (QK_NOPE + QK_ROPE)), Q_LORA ** -0.5),
        'od_w_kvb': nrm((N_ODD, KV_LORA, MLA_HEADS * (QK_NOPE + V_HEAD)), KV_LORA ** -0.5),
        'od_gq_nope': 1.0 + nrm((N_ODD, QK_NOPE), 0.02),
        'od_gq_rope': 1.0 + nrm((N_ODD, QK_ROPE), 0.02),
        'od_gk_nope': 1.0 + nrm((N_ODD, QK_NOPE), 0.02),
        'od_gk_rope': 1.0 + nrm((N_ODD, QK_ROPE), 0.02),
        'od_w_o': nrm((N_ODD, MLA_WIDTH, D), MLA_WIDTH ** -0.5),
    }


def reference(x, c, ctx, c_ctx, ada_w, ada_b, norm_w, ev_w_in, ev_conv_w, ev_mu, ev_k_k, ev_k_a, ev_w0, ev_w2, ev_a0, ev_a2, ev_r_k, ev_lnx_w, ev_lnx_b, ev_w_out, od_w_in, od_q_a_norm, od_kv_a_norm, od_w_qb, od_w_kvb, od_gq_nope, od_gq_rope, od_gk_nope, od_gk_rope, od_w_o):
    cos, sin = axial_rope_tables(x.shape[1])
    silu_c = jax.nn.silu(c)
    silu_cc = jax.nn.silu(c_ctx)
    for layer in range(DEPTH):
        need_ctx = layer < DEPTH - 1
        shift, scale, gate = jnp.split(silu_c @ ada_w[layer] + ada_b[layer], 3, axis=-1)
        shift_c, scale_c, gate_c = jnp.split(silu_cc @ ada_w[layer] + ada_b[layer], 3, axis=-1)
        h_lat = rms_norm(x, norm_w[layer]) * (1.0 + scale[:, None]) + shift[:, None]
        h_ctx = rms_norm(ctx, norm_w[layer]) * (1.0 + scale_c) + shift_c
        j = layer // 2
        if layer % 2 == 0:
            y_ctx, y_lat = even_mixer(h_ctx, h_lat, ev_w_in[j], ev_conv_w[j], ev_mu[j], ev_k_k[j], ev_k_a[j],
                                      ev_w0[j], ev_w2[j], ev_a0[j], ev_a2[j], ev_r_k[j], ev_lnx_w[j], ev_lnx_b[j],
                                      ev_w_out[j], need_ctx)
        else:
            y_ctx, y_lat = odd_mixer(h_ctx, h_lat, od_w_in[j], od_q_a_norm[j], od_kv_a_norm[j], od_w_qb[j],
                                     od_w_kvb[j], od_gq_nope[j], od_gq_rope[j], od_gk_nope[j], od_gk_rope[j],
                                     od_w_o[j], cos, sin, need_ctx)
        x = x + gate[:, None] * y_lat
        if need_ctx:
            ctx = ctx + gate_c * y_ctx
    return x
```

```python
from contextlib import ExitStack
import numpy as np
import concourse.bass as bass
import concourse.mybir as mybir
from concourse.bass_utils import run_bass_kernel_spmd

F32 = mybir.dt.float32
F32R = mybir.dt.float32r
BF16 = mybir.dt.bfloat16
AF = mybir.ActivationFunctionType
ALU = mybir.AluOpType

T = 4352
NT = 34
NCTX = 256
NEG_E = -float(np.exp(-0.5))


class Buf:
    def __init__(self, t):
        self.t = t
        self.w = None
        self.r = {}

    def __getitem__(self, k):
        return self.t[k]


class Sync:
    def __init__(self, nc, es, ndma=32):
        self.nc = nc
        self.engs = {"pe": nc.tensor, "act": nc.scalar, "dve": nc.vector, "pool": nc.gpsimd, "sp": nc.sync}
        self.sem = {}
        self.cnt = {}
        self.seen = {k: {} for k in self.engs}
        for k in self.engs:
            self.sem[k] = es.enter_context(nc.semaphore("s_" + k))
            self.cnt[k] = 0
        self.dsem = [es.enter_context(nc.semaphore("d_%d" % i)) for i in range(ndma)]
        self.ndma = 0
        self.dma_last = [None] * ndma
        self.qi = 0

    def _need(self, e, evs):
        eng = self.engs[e]
        seen = self.seen[e]
        for ev in evs:
            if ev is None:
                continue
            key, sem, val, src = ev
            if src == e and e == "pe":
                continue
            if seen.get(key, 0) >= val:
                continue
            eng.wait_ge(sem, val)
            seen[key] = val

    @staticmethod
    def _deps(reads, writes):
        evs = []
        for b in reads:
            evs.append(b.w)
        for b in writes:
            evs.append(b.w)
            evs.extend(b.r.values())
        return evs

    @staticmethod
    def _record(ev, reads, writes):
        for b in reads:
            b.r[ev[0]] = ev
        for b in writes:
            b.w = ev
            b.r = {}

    def op(self, e, fn, reads=(), writes=()):
        self._need(e, self._deps(reads, writes))
        ins = fn(self.engs[e])
        self.cnt[e] += 1
        ins.then_inc(self.sem[e], 1)
        self._record((e, self.sem[e], self.cnt[e], e), reads, writes)
        return ins

    def dma(self, out, in_, reads=(), writes=(), q=None, **kw):
        if q is None:
            q = ("sp", "pool")[self.qi % 2] if False else "sp"
            self.qi += 1
        evs = self._deps(reads, writes)
        k = self.ndma % len(self.dsem)
        evs.append(self.dma_last[k])
        self._need(q, evs)
        ins = self.engs[q].dma_start(out=out, in_=in_, **kw)
        val = 16 * (self.ndma // len(self.dsem) + 1)
        ins.then_inc(self.dsem[k], 16)
        ev = ("d%d" % k, self.dsem[k], val, "dma")
        self.dma_last[k] = ev
        self.ndma += 1
        self._record(ev, reads, writes)
        return ins

    def barrier(self):
        evs = [(k, self.sem[k], self.cnt[k], "x") for k in self.engs if self.cnt[k] > 0]
        evs += [ev for ev in self.dma_last if ev is not None]
        for e in self.engs:
            self._need(e, [ev for ev in evs if ev[0] != e])


class Pool:
    def __init__(self, nc, es, name, shape, dtype, n, psum=False):
        mk = nc.psum_tensor if psum else nc.sbuf_tensor
        self.bufs = [Buf(es.enter_context(mk("%s_%d" % (name, i), shape, dtype))) for i in range(n)]
        self.i = 0

    def get(self):
        b = self.bufs[self.i % len(self.bufs)]
        self.i += 1
        return b


def sb(nc, es, name, shape, dtype=F32):
    return Buf(es.enter_context(nc.sbuf_tensor(name, shape, dtype)))


def colify(v):
    v = np.asarray(v, np.float32).reshape(-1)
    if v.size < 128:
        v = np.concatenate([v, np.zeros(128 - v.size, np.float32)])
    return np.ascontiguousarray(v.reshape(-1, 128).T)


COLS = {}


def _layout_cols():
    off = 0
    for name, n in [("c", 16), ("ada_b", 48), ("norm_w", 16), ("conv_w", 12), ("mu", 13), ("k_k", 4), ("k_a", 4),
                    ("w0", 8), ("a0", 8), ("r_k", 4), ("lnx_w", 4), ("lnx_b", 4), ("q_a_norm", 3), ("kv_a_norm", 2),
                    ("gq_nope", 1), ("gq_rope", 1), ("gk_nope", 1), ("gk_rope", 1)]:
        COLS[name] = (off, n)
        off += n
    return off


NCOL = _layout_cols()


def build_cols(inp, b):
    parts = []
    cc = np.stack([colify(inp["c"][b]), colify(inp["c_ctx"])], axis=-1).reshape(128, 16)
    parts.append(cc)
    parts.append(np.concatenate([colify(inp["ada_b"][l]) for l in range(2)], axis=1))
    parts.append(np.concatenate([colify(inp["norm_w"][l]) for l in range(2)], axis=1))
    parts.append(np.concatenate([colify(inp["ev_conv_w"][0][t]) for t in range(3)], axis=1))
    parts.append(colify(inp["ev_mu"][0]))
    parts.append(colify(inp["ev_k_k"][0]))
    parts.append(colify(inp["ev_k_a"][0]))
    parts.append(np.concatenate([colify(inp["ev_w0"][0][d]) for d in range(2)], axis=1))
    parts.append(np.concatenate([colify(inp["ev_a0"][0][d]) for d in range(2)], axis=1))
    parts.append(colify(inp["ev_r_k"][0]))
    parts.append(colify(inp["ev_lnx_w"][0]))
    parts.append(colify(inp["ev_lnx_b"][0]))
    parts.append(colify(inp["od_q_a_norm"][0]))
    parts.append(colify(inp["od_kv_a_norm"][0]))
    parts.append(colify(inp["od_gq_nope"][0]))
    parts.append(colify(inp["od_gq_rope"][0]))
    parts.append(colify(inp["od_gk_nope"][0]))
    parts.append(colify(inp["od_gk_rope"][0]))
    out = np.concatenate(parts, axis=1).astype(np.float32)
    assert out.shape == (128, NCOL), out.shape
    return np.ascontiguousarray(out)


C_M2F, C_M2R, C_NF, C_NR, C_ID2 = 256, 768, 1280, 1536, 1792
C_PERM = 2048
C_BLK = 2112
NCONST = 2112 + 2048


def build_rope():
    t = np.arange(4096)
    pos = np.stack([(t // 64).astype(np.float32), (t % 64).astype(np.float32)], axis=0)
    inv = (np.float32(10000.0) ** (-np.arange(16, dtype=np.float32) / np.float32(16))).astype(np.float32)
    ang = (pos[:, None, :] * inv[None, :, None]).astype(np.float32)
    cos = np.cos(ang).astype(np.float32)
    sin = np.sin(ang).astype(np.float32)
    out = np.zeros((64, 2, 4096), np.float32)
    for ax in range(2):
        for half in range(2):
            r0 = ax * 32 + half * 16
            out[r0:r0 + 16, 0, :] = cos[ax]
            out[r0:r0 + 16, 1, :] = -sin[ax] if half == 0 else sin[ax]
    return out


def build_consts():
    p = np.arange(128)[:, None]
    f = np.arange(128)[None, :]
    c = np.zeros((128, NCONST), np.float32)
    c[:, 0:128] = (p == f)
    c[:, 128:256] = (p // 64 == f // 64)
    lt, le, gt, ge = (p < f), (p <= f), (p > f), (p >= f)
    c[:, C_M2F:C_M2F + 512] = np.concatenate([lt, le, lt, le], axis=1)
    c[:, C_M2R:C_M2R + 512] = np.concatenate([gt, ge, gt, ge], axis=1)
    c[:, C_NF:C_NF + 256] = np.concatenate([gt, gt], axis=1)
    c[:, C_NR:C_NR + 256] = np.concatenate([lt, lt], axis=1)
    c[:, C_ID2:C_ID2 + 256] = np.concatenate([p == f, p == f], axis=1)
    j = np.arange(64)
    partner = np.where((j % 32) < 16, j + 16, j - 16)
    pm = np.zeros((128, 64), np.float32)
    pm[partner, j] = 1.0
    c[:, C_PERM:C_PERM + 64] = pm
    bd = (p // 16 == f // 16)
    mk = [bd & (p < f)]
    nk = [bd & (p > f)]
    for s_ in (16, 32, 64):
        same = (p // (2 * s_) == f // (2 * s_))
        nk.append(same & (p % (2 * s_) >= s_) & (f % (2 * s_) < s_))
        mk.append(same & (f % (2 * s_) >= s_) & (p % (2 * s_) < s_))
    for i, m_ in enumerate(mk + nk):
        c[:, C_BLK + i * 256:C_BLK + (i + 1) * 256] = np.concatenate([m_, m_], axis=1)
    return c


def build(stage=99, dbg=False):
    nc = bass.Bass("TRN2", target_bir_lowering=False)
    dt_in = lambda n, s: nc.dram_tensor(n, s, F32, kind="ExternalInput").ap()
    x_in = dt_in("x", [4096, 1024])
    ctx_in = dt_in("ctx", [NCTX, 1024])
    cols_in = dt_in("cols", [128, NCOL])
    consts_in = dt_in("consts", [128, NCONST])
    ada_w = dt_in("ada_w", [2, 1024, 3072])
    ev_w_in = dt_in("ev_w_in", [1024, 4224])
    ev_lora = dt_in("ev_lora", [128, 2, 512])
    ev_w_out = dt_in("ev_w_out", [1024, 1024])
    od_w_in = dt_in("od_w_in", [1024, 1728])
    od_w_qb = dt_in("od_w_qb", [384, 1536])
    od_w_kvb = dt_in("od_w_kvb", [256, 2048])
    od_w_o = dt_in("od_w_o", [1024, 1024])
    rope_d = dt_in("rope", [64, 2, 4096])
    out_d = nc.dram_tensor("out", [4096, 1024], F32, kind="ExternalOutput").ap()
    outs = [Buf(out_d)]
    scr = {}
    for n in ["rT", "vT", "aT", "lw0", "lw1", "kd0", "kd1", "b0", "b1", "bonT", "szbT", "ys0", "ys1"]:
        kind = "ExternalOutput" if (dbg and n in dbg) else "Internal"
        scr[n] = Buf(nc.dram_tensor(n, [512, T], F32, kind=kind).ap())
    scr["mixT"] = Buf(nc.dram_tensor("mixT", [1024, T], BF16, kind="ExternalOutput" if (dbg and "mixT" in dbg) else "Internal").ap())
    scr["x1"] = Buf(nc.dram_tensor("x1", [T, 1024], F32, kind="ExternalOutput" if (dbg and "x1" in dbg) else "Internal").ap())
    for n, shp, dt_ in [("KTd", [1024, T], BF16), ("KRd", [64, T], BF16), ("Vd", [T, 1024], BF16), ("QNd", [384, 4096], BF16), ("SZd", [1024, 4096], F32)]:
        scr[n] = Buf(nc.dram_tensor(n, shp, dt_, kind="ExternalOutput" if (dbg and n in dbg) else "Internal").ap())
    if dbg and "hT" in dbg:
        scr["hT"] = Buf(nc.dram_tensor("hT", [1024, T], BF16, kind="ExternalOutput").ap())

    with ExitStack() as es0:
        S = Sync(nc, es0)
        colsb = sb(nc, es0, "colsb", [128, NCOL])
        cst = sb(nc, es0, "cst", [128, C_BLK])
        identb = sb(nc, es0, "identb", [128, 128], BF16)
        ones = sb(nc, es0, "ones", [128, 128])
        modT = sb(nc, es0, "modT", [128, 2, 24, 2])
        gcol = sb(nc, es0, "gcol", [128, 2, 8, 2])
        G = [sb(nc, es0, "G%d" % s, [128, 1024]) for s in range(2)]
        der = sb(nc, es0, "der", [128, 32])
        S.dma(colsb[:], cols_in, writes=[colsb])
        S.dma(cst[:], consts_in[:, 0:C_BLK], writes=[cst])
        S.op("dve", lambda e: e.tensor_copy(out=identb[:], in_=cst[:, 0:128]), reads=[cst], writes=[identb])
        S.op("pool", lambda e: e.memset(ones[:], 1.0), writes=[ones])
        ident = cst.t[:, 0:128]
        bdm = cst.t[:, 128:256]

        def col(name, i=0, n=1):
            o, _ = COLS[name]
            return colsb.t[:, o + i:o + i + n]

        mo, _ = COLS["mu"]
        S.op("dve", lambda e: e.tensor_scalar(out=der[:, 0:13], in0=colsb[:, mo:mo + 13], scalar1=-1.0, scalar2=1.0, op0=ALU.mult, op1=ALU.add), reads=[colsb], writes=[der])
        S.op("dve", lambda e: e.tensor_scalar(out=der[:, 13:26], in0=colsb[:, mo:mo + 13], scalar1=0.5, scalar2=None, op0=ALU.mult), reads=[colsb], writes=[der])
        ko, _ = COLS["k_a"]
        S.op("dve", lambda e: e.tensor_scalar(out=der[:, 26:30], in0=colsb[:, ko:ko + 4], scalar1=-1.0, scalar2=1.0, op0=ALU.mult, op1=ALU.add), reads=[colsb], writes=[der])

        with ExitStack() as es:
            sc = sb(nc, es, "sc", [128, 16])
            wpool = Pool(nc, es, "adaw", [128, 8, 512], F32, 2)
            pp = Pool(nc, es, "p0ps", [128, 512], F32, 4, psum=True)
            co, _ = COLS["c"]
            S.op("act", lambda e: e.activation(out=sc[:], in_=colsb[:, co:co + 16], func=AF.Silu), reads=[colsb], writes=[sc])
            abo, _ = COLS["ada_b"]
            for l in range(2):
                wv = ada_w[l].rearrange("(k p) n -> p k n", p=128)
                for piece in range(6):
                    wst = wpool.get()
                    S.dma(wst[:], wv[:, :, piece * 512:(piece + 1) * 512], writes=[wst], q=("sp", "pool")[piece % 2])
                    for dc in range(4):
                        ps = pp.get()
                        for k in range(8):
                            S.op("pe", lambda e: e.matmul(ps[:, 0:2], lhsT=wst[:, k, dc * 128:(dc + 1) * 128], rhs=sc[:, 2 * k:2 * k + 2], start=(k == 0), stop=(k == 7)), reads=[wst, sc], writes=[ps])
                        ch = piece * 4 + dc
                        S.op("dve", lambda e: e.tensor_scalar(out=modT[:, l, ch, :], in0=ps[:, 0:2], scalar1=colsb[:, abo + l * 24 + ch:abo + l * 24 + ch + 1], scalar2=None, op0=ALU.add), reads=[ps, colsb], writes=[modT])
                nwo, _ = COLS["norm_w"]
                S.op("dve", lambda e: e.tensor_scalar(out=gcol[:, l, :, :], in0=modT[:, l, 8:16, :], scalar1=1.0, scalar2=None, op0=ALU.add), reads=[modT], writes=[gcol])
                for s in range(2):
                    S.op("dve", lambda e: e.tensor_tensor(out=gcol[:, l, :, s], in0=gcol[:, l, :, s], in1=colsb[:, nwo + l * 8:nwo + l * 8 + 8], op=ALU.mult), reads=[gcol, colsb], writes=[gcol])
            S.barrier()

        def make_gate_tiles(l):
            with ExitStack() as es:
                gb = Pool(nc, es, "gb%d" % l, [128, 128], F32, 2)
                pp = Pool(nc, es, "gps%d" % l, [128, 512], F32, 2, psum=True)
                for s in range(2):
                    for k in range(8):
                        g = gb.get()
                        S.op("dve", lambda e: e.tensor_scalar(out=g[:], in0=ones[:], scalar1=modT[:, l, 16 + k, s:s + 1], scalar2=None, op0=ALU.mult), reads=[ones, modT], writes=[g])
                        ps = pp.get()
                        S.op("pe", lambda e: e.matmul(ps[:, 0:128], lhsT=g[:], rhs=ident, start=True, stop=True), reads=[g, cst], writes=[ps])
                        S.op("act", lambda e: e.activation(out=G[s][:, k * 128:(k + 1) * 128], in_=ps[:, 0:128], func=AF.Identity), reads=[ps], writes=[G[s]])
                S.barrier()

        def norm_phase(es, l, src_rows):
            hT = es.enter_context(nc.sbuf_tensor("hT%d" % l, [128, 8, T], BF16))
            hTb = [Buf(hT) for _ in range(NT)]
            with ExitStack() as e2:
                xpool = Pool(nc, e2, "xt%d" % l, [128, 1024], F32, 6)
                jpool = Pool(nc, e2, "junk%d" % l, [128, 1024], BF16, 4)
                xnpool = Pool(nc, e2, "xn%d" % l, [128, 1024], BF16, 4)
                sspool = Pool(nc, e2, "ss%d" % l, [128, 1], F32, 8)
                pst = Pool(nc, e2, "pst%d" % l, [128, 8, 128], BF16, 5, psum=True)
                def ntile(i):
                        s = 1 if i < 2 else 0
                        xt = xpool.get()
                        S.dma(xt[:], src_rows(i), writes=[xt], q=("sp", "pool")[i % 2])
                        junk = jpool.get()
                        ss = sspool.get()
                        S.op("act", lambda e: e.activation(out=junk[:], in_=xt[:], func=AF.Square, accum_out=ss[:]), reads=[xt], writes=[junk, ss])
                        yield
                        S.op("act", lambda e: e.activation(out=ss[:], in_=ss[:], func=AF.Sqrt, scale=1.0 / 1024, bias=1e-6), reads=[ss], writes=[ss])
                        yield
                        S.op("dve", lambda e: e.reciprocal(out=ss[:], in_=ss[:]), reads=[ss], writes=[ss])
                        yield
                        xn = xnpool.get()
                        S.op("dve", lambda e: e.tensor_scalar(out=xn[:], in0=xt[:], scalar1=ss[:, 0:1], scalar2=None, op0=ALU.mult), reads=[xt, ss], writes=[xn])
                        yield
                        ps = pst.get()
                        for k in range(8):
                            S.op("pe", lambda e: e.transpose(ps[:, k, :], xn[:, k * 128:(k + 1) * 128], identb[:]), reads=[xn, identb], writes=[ps])
                        for k in range(8):
                            dst = hT[:, k, i * 128:(i + 1) * 128]
                            gsc = gcol[:, l, k, s:s + 1]
                            shf = modT[:, l, k, s:s + 1]
                            if k % 2:
                                S.op("act", lambda e: e.activation(out=dst, in_=ps[:, k, :], func=AF.Identity, scale=gsc, bias=shf), reads=[ps, gcol, modT], writes=[hTb[i]])
                                yield
                            else:
                                S.op("dve", lambda e: e.tensor_scalar(out=dst, in0=ps[:, k, :], scalar1=gsc, scalar2=shf, op0=ALU.mult, op1=ALU.add), reads=[ps, gcol, modT], writes=[hTb[i]])
                                yield
                run_interleaved([ntile(i) for i in range(NT)], 4 if l == 0 else 4)
                S.barrier()
            return hT, hTb

        def src0(i):
            return ctx_in[i * 128:(i + 1) * 128, :] if i < 2 else x_in[(i - 2) * 128:(i - 1) * 128, :]

        make_gate_tiles(0)
        with ExitStack() as esL0:
            hT, hTb = norm_phase(esL0, 0, src0)
            if dbg and "hT" in dbg:
                for k in range(8):
                    S.dma(scr["hT"][k * 128:(k + 1) * 128, :], hT[:, k, :], reads=hTb, writes=[scr["hT"]])
            if stage >= 2:
                layer0_proj(nc, S, esL0, hT, hTb, ev_w_in, ev_lora, colsb, der, cst, scr)
        S.barrier()
        if stage >= 3:
            layer0_scan(nc, S, cst, scr, ones, consts_in, identb)
            S.barrier()
        if stage >= 4:
            layer0_out(nc, S, colsb, cst, scr, ev_w_out, G, src0, identb)
            S.barrier()
        if stage >= 5:
            make_gate_tiles(1)
            with ExitStack() as esL1:
                hT1, hT1b = norm_phase(esL1, 1, lambda i: scr["x1"][i * 128:(i + 1) * 128, :])
                if dbg and "hT" in dbg:
                    for k in range(8):
                        S.dma(scr["hT"][k * 128:(k + 1) * 128, :], hT1[:, k, :], reads=hT1b, writes=[scr["hT"]])
                layer1_kv(nc, S, esL1, hT1, hT1b, od_w_in, od_w_kvb, colsb, cst, scr, rope_d)
            S.barrier()
        if stage >= 6:
            layer1_attn(nc, S, scr, od_w_qb, od_w_o, colsb, cst, G, rope_d, out_d, outs[0])
            S.barrier()
        S.barrier()
    return nc


def token_blocks():
    blks = [(0, NCTX, 0, NCTX)]
    s = NCTX
    while s < T:
        e = min(s + 510, T)
        blks.append((NCTX, T, s, e))
        s = e
    return blks


def layer0_proj(nc, S, es, hT, hTb, w_in_d, lora_d, colsb, der, cst, scr):
    def col(name, i=0, n=1):
        o, _ = COLS[name]
        return colsb.t[:, o + i:o + i + n]

    bdm = cst.t[:, 128:256]
    W = es.enter_context(nc.sbuf_tensor("W0", [128, 8, 4224], BF16))
    Wb = Buf(W)
    lora = Buf(es.enter_context(nc.sbuf_tensor("lora", [128, 2, 512], BF16)))
    with ExitStack() as e2:
        stg = Pool(nc, e2, "wstg", [128, 8, 384], F32, 2)
        wv = w_in_d.rearrange("(k p) n -> p k n", p=128)
        for pc in range(11):
            st = stg.get()
            S.dma(st[:], wv[:, :, pc * 384:(pc + 1) * 384], writes=[st], q=("sp", "pool")[pc % 2])
            for k in range(8):
                eng = ("dve", "act", "pool")[k % 3] if False else ("dve", "act")[k % 2]
                if eng == "act":
                    S.op("act", lambda e: e.activation(out=W[:, k, pc * 384:(pc + 1) * 384], in_=st[:, k, :], func=AF.Identity), reads=[st], writes=[Wb])
                else:
                    S.op("dve", lambda e: e.tensor_copy(out=W[:, k, pc * 384:(pc + 1) * 384], in_=st[:, k, :]), reads=[st], writes=[Wb])
        st = stg.get()
        S.dma(st[:, 0:3, :].rearrange("p a b -> p (a b)")[:, 0:1024], lora_d.rearrange("p a b -> p (a b)"), writes=[st])
        S.op("dve", lambda e: e.tensor_copy(out=lora[:].rearrange("p a b -> p (a b)"), in_=st[:, 0:3, :].rearrange("p a b -> p (a b)")[:, 0:1024]), reads=[st], writes=[lora])
        S.barrier()

    with ExitStack() as e2:
        pp = Pool(nc, e2, "pps", [128, 512], F32, 7, psum=True)
        f32p = Pool(nc, e2, "wk", [128, 514], F32, 10)
        outp = Pool(nc, e2, "wo", [128, 512], F32, 7)
        rpool = Pool(nc, e2, "rp", [128, 512], F32, 2)
        kpool = Pool(nc, e2, "kp", [128, 512], F32, 2)
        vpool = Pool(nc, e2, "vp", [128, 512], F32, 2)
        apool = Pool(nc, e2, "ap", [128, 512], F32, 2)
        outb = Pool(nc, e2, "wob", [128, 512], BF16, 2)
        dq = [0]

        def store(dst, rows, o0, o1, tile, n_out):
            q = ("sp", "pool")[dq[0] % 2]
            dq[0] += 1
            S.dma(dst[rows, o0:o1], tile[:, 0:n_out], reads=[tile], writes=[dst], q=q)

        for (ss, se, o0, o1) in token_blocks():
            cs = max(ss, o0 - 1)
            ce = min(se, o1 + 1)
            n = ce - cs
            n_out = o1 - o0
            oc = o0 - cs + 1
            tiles_rd = [hTb[i] for i in range(cs // 128, (ce - 1) // 128 + 1)]

            def mm(chunk):
                ps = pp.get()
                for k in range(8):
                    S.op("pe", lambda e: e.matmul(ps[:, 0:n], lhsT=W[:, k, chunk * 128:(chunk + 1) * 128], rhs=hT[:, k, cs:ce], start=(k == 0), stop=(k == 7)), reads=[Wb] + tiles_rd, writes=[ps])
                return ps

            def padded(ps, eng="act"):
                t = f32p.get()
                if cs == o0:
                    S.op("pool", lambda e: e.memset(t[:, 0:1], 0.0), writes=[t])
                if ce == o1:
                    S.op("pool", lambda e: e.memset(t[:, n + 1:n + 2], 0.0), writes=[t])
                if eng == "act":
                    S.op("act", lambda e: e.activation(out=t[:, 1:n + 1], in_=ps[:, 0:n], func=AF.Identity), reads=[ps], writes=[t])
                else:
                    S.op("dve", lambda e: e.tensor_copy(out=t[:, 1:n + 1], in_=ps[:, 0:n]), reads=[ps], writes=[t])
                return t

            def tshift(t, mi, eng="dve", dst=None):
                s2 = outp.get()
                o = (dst or outp).get()
                e_ = eng
                S.op(e_, lambda e: e.tensor_tensor(out=s2[:, 0:n_out], in0=t[:, oc - 1:oc - 1 + n_out], in1=t[:, oc + 1:oc + 1 + n_out], op=ALU.add), reads=[t], writes=[s2])
                S.op("act", lambda e: e.activation(out=o[:, 0:n_out], in_=t[:, oc:oc + n_out], func=AF.Copy, scale=der[:, mi:mi + 1]), reads=[t, der], writes=[o])
                S.op("dve", lambda e: e.scalar_tensor_tensor(out=o[:, 0:n_out], in0=s2[:, 0:n_out], scalar=der[:, 13 + mi:14 + mi], in1=o[:, 0:n_out], op0=ALU.mult, op1=ALU.add), reads=[s2, o, der], writes=[o])
                return o

            for q in range(4):
                pu = mm(q)
                pgc = mm(8 + q)
                u_sb = f32p.get()
                S.op("act", lambda e: e.activation(out=u_sb[:, 0:n], in_=pu[:, 0:n], func=AF.Identity), reads=[pu], writes=[u_sb])
                cu = f32p.get()
                if cs == o0:
                    S.op("pool", lambda e: e.memset(cu[:, 0:1], 0.0), writes=[cu])
                if ce == o1:
                    S.op("pool", lambda e: e.memset(cu[:, n + 1:n + 2], 0.0), writes=[cu])
                S.op("dve", lambda e: e.tensor_tensor(out=cu[:, 1:n + 1], in0=pgc[:, 0:n], in1=u_sb[:, 0:n], op=ALU.mult), reads=[pgc, u_sb], writes=[cu])
                y = outp.get()
                S.op("act", lambda e: e.activation(out=y[:, 0:n_out], in_=cu[:, oc:oc + n_out], func=AF.Copy, scale=col("conv_w", 4 + q)), reads=[cu, colsb], writes=[y])
                S.op("dve", lambda e: e.scalar_tensor_tensor(out=y[:, 0:n_out], in0=cu[:, oc - 1:oc - 1 + n_out], scalar=col("conv_w", q), in1=y[:, 0:n_out], op0=ALU.mult, op1=ALU.add), reads=[cu, y, colsb], writes=[y])
                S.op("dve", lambda e: e.scalar_tensor_tensor(out=y[:, 0:n_out], in0=cu[:, oc + 1:oc + 1 + n_out], scalar=col("conv_w", 8 + q), in1=y[:, 0:n_out], op0=ALU.mult, op1=ALU.add), reads=[cu, y, colsb], writes=[y])
                pgb = mm(4 + q)
                pz = mm(12 + q)
                sz = outp.get()
                S.op("act", lambda e: e.activation(out=sz[:, 0:n_out], in_=pz[:, oc - 1:oc - 1 + n_out], func=AF.Silu), reads=[pz], writes=[sz])
                S.op("dve", lambda e: e.tensor_tensor(out=sz[:, 0:n_out], in0=pgb[:, oc - 1:oc - 1 + n_out], in1=sz[:, 0:n_out], op=ALU.mult), reads=[pgb, sz], writes=[sz])
                mo_ = outb.get()
                S.op("dve", lambda e: e.tensor_tensor(out=mo_[:, 0:n_out], in0=sz[:, 0:n_out], in1=y[:, 0:n_out], op=ALU.mult), reads=[sz, y], writes=[mo_])
                store(scr["mixT"], slice(q * 128, (q + 1) * 128), o0, o1, mo_, n_out)

            pwa = mm(28)
            twa = padded(pwa)
            wa = tshift(twa, 12)
            lin = outb.get()
            S.op("act", lambda e: e.activation(out=lin[0:64, 0:n_out], in_=wa[0:64, 0:n_out], func=AF.Tanh), reads=[wa], writes=[lin])
            S.op("dve", lambda e: e.tensor_copy(out=lin[64:128, 0:n_out], in_=wa[64:128, 0:n_out]), reads=[wa], writes=[lin])
            for q in range(4):
                rows = slice(q * 128, (q + 1) * 128)
                pr_ = padded(mm(16 + q))
                pk_ = padded(mm(20 + q), "dve")
                pv_ = padded(mm(24 + q))
                pzb = mm(29 + q)
                r_ = tshift(pr_, q, "dve", rpool)
                store(scr["rT"], rows, o0, o1, r_, n_out)
                k_ = tshift(pk_, 4 + q, "pool", kpool)
                v_ = tshift(pv_, 8 + q, "dve", vpool)
                store(scr["vT"], rows, o0, o1, v_, n_out)
                szb = outp.get()
                S.op("act", lambda e: e.activation(out=szb[:, 0:n_out], in_=pzb[:, oc - 1:oc - 1 + n_out], func=AF.Silu), reads=[pzb], writes=[szb])
                store(scr["szbT"], rows, o0, o1, szb, n_out)
                kk = f32p.get()
                S.op("dve", lambda e: e.tensor_scalar(out=kk[:, 0:n_out], in0=k_[:, 0:n_out], scalar1=col("k_k", q), scalar2=None, op0=ALU.mult), reads=[k_, colsb], writes=[kk])
                sq = f32p.get()
                S.op("act", lambda e: e.activation(out=sq[:, 0:n_out], in_=kk[:, 0:n_out], func=AF.Square), reads=[kk], writes=[sq])
                pss = pp.get()
                S.op("pe", lambda e: e.matmul(pss[:, 0:n_out], lhsT=bdm, rhs=sq[:, 0:n_out], start=True, stop=True), reads=[cst, sq], writes=[pss])
                rn = f32p.get()
                S.op("act", lambda e: e.activation(out=rn[:, 0:n_out], in_=pss[:, 0:n_out], func=AF.Ln, bias=1e-24, scale=1.0), reads=[pss], writes=[rn])
                S.op("act", lambda e: e.activation(out=rn[:, 0:n_out], in_=rn[:, 0:n_out], func=AF.Exp, scale=-0.5), reads=[rn], writes=[rn])
                a_ = apool.get()
                S.op("dve", lambda e: e.scalar_tensor_tensor(out=a_[:, 0:n_out], in0=kk[:, 0:n_out], scalar=-1.0, in1=rn[:, 0:n_out], op0=ALU.mult, op1=ALU.mult), reads=[kk, rn], writes=[a_])
                store(scr["aT"], rows, o0, o1, a_, n_out)
                pbon = pp.get()
                for d in range(2):
                    plw = pp.get()
                    S.op("pe", lambda e: e.matmul(plw[:, 0:n_out], lhsT=lora[0:64, d, q * 128:(q + 1) * 128], rhs=lin[0:64, 0:n_out], start=True, stop=True), reads=[lora, lin], writes=[plw])
                    pla = pp.get()
                    S.op("pe", lambda e: e.matmul(pla[:, 0:n_out], lhsT=lora[64:128, d, q * 128:(q + 1) * 128], rhs=lin[64:128, 0:n_out], start=True, stop=True), reads=[lora, lin], writes=[pla])
                    lw = outp.get()
                    S.op("act", lambda e: e.activation(out=lw[:, 0:n_out], in_=plw[:, 0:n_out], func=AF.Sigmoid, bias=col("w0", d * 4 + q), scale=1.0), reads=[plw, colsb], writes=[lw])
                    S.op("act", lambda e: e.activation(out=lw[:, 0:n_out], in_=lw[:, 0:n_out], func=AF.Copy, scale=NEG_E), reads=[lw], writes=[lw])
                    store(scr["lw%d" % d], rows, o0, o1, lw, n_out)
                    ic = f32p.get()
                    S.op("act", lambda e: e.activation(out=ic[:, 0:n_out], in_=pla[:, 0:n_out], func=AF.Sigmoid, bias=col("a0", d * 4 + q), scale=1.0), reads=[pla, colsb], writes=[ic])
                    kf = f32p.get()
                    S.op("dve", lambda e: e.tensor_scalar(out=kf[:, 0:n_out], in0=ic[:, 0:n_out], scalar1=col("k_a", q), scalar2=der[:, 26 + q:27 + q], op0=ALU.mult, op1=ALU.add), reads=[ic, colsb, der], writes=[kf])
                    kd = outp.get()
                    S.op("dve", lambda e: e.tensor_tensor(out=kd[:, 0:n_out], in0=kf[:, 0:n_out], in1=k_[:, 0:n_out], op=ALU.mult), reads=[kf, k_], writes=[kd])
                    store(scr["kd%d" % d], rows, o0, o1, kd, n_out)
                    b_ = outp.get()
                    S.op("dve", lambda e: e.scalar_tensor_tensor(out=b_[:, 0:n_out], in0=a_[:, 0:n_out], scalar=-1.0, in1=ic[:, 0:n_out], op0=ALU.mult, op1=ALU.mult), reads=[a_, ic], writes=[b_])
                    store(scr["b%d" % d], rows, o0, o1, b_, n_out)
                    rk = f32p.get()
                    S.op("dve", lambda e: e.scalar_tensor_tensor(out=rk[:, 0:n_out], in0=kd[:, 0:n_out], scalar=col("r_k", q), in1=r_[:, 0:n_out], op0=ALU.mult, op1=ALU.mult), reads=[kd, r_, colsb], writes=[rk])
                    S.op("pe", lambda e: e.matmul(pbon[:, 0:n_out], lhsT=bdm, rhs=rk[:, 0:n_out], start=(d == 0), stop=(d == 1)), reads=[cst, rk], writes=[pbon])
                bon = outp.get()
                S.op("dve", lambda e: e.tensor_tensor(out=bon[:, 0:n_out], in0=pbon[:, 0:n_out], in1=v_[:, 0:n_out], op=ALU.mult), reads=[pbon, v_], writes=[bon])
                store(scr["bonT"], rows, o0, o1, bon, n_out)
        S.barrier()


def layer0_scan(nc, S, cst, scr, ones, consts_in, identb):
    import os as _os2
    ident64 = identb.t[0:64, 0:64]
    IDT = BF16
    with ExitStack() as es:
        yop = Pool(nc, es, "syo", [64, 2, 128], F32, 5)
        ldp = {n: Pool(nc, es, "ld_" + n, [64, 2, 128], F32, 6) for n in ("r", "a", "v", "lw", "k", "b")}
        wk = Pool(nc, es, "swk", [64, 2, 128], F32, 30)
        arp = Pool(nc, es, "sar", [64, 2, 256], BF16, 5)
        etp = Pool(nc, es, "set", [64, 2], F32, 10)
        ttp = Pool(nc, es, "stt", [128, 512], BF16, 5)
        mmp = Pool(nc, es, "smm", [128, 256], BF16, 5)
        mkp = Pool(nc, es, "smk", [128, 512], BF16, 5)
        nnp = Pool(nc, es, "snn", [128, 256], BF16, 5)
        mnp = Pool(nc, es, "smn", [128, 512], IDT, 8)
        blk = sb(nc, es, "blkmask", [128, 2048])
        S.dma(blk[:], consts_in[:, C_BLK:C_BLK + 2048], writes=[blk])
        xp = Pool(nc, es, "sx", [128, 256], BF16, 5)
        mbp = Pool(nc, es, "smb", [128, 3, 256], IDT, 5)
        nbp = Pool(nc, es, "snb", [128, 4, 256], IDT, 5)
        dp = Pool(nc, es, "sd", [128, 512], BF16, 8)
        wkr = Pool(nc, es, "swkr", [64, 2, 128], BF16, 20)
        vbp = Pool(nc, es, "svb", [64, 2, 128], BF16, 5)
        hrp = Pool(nc, es, "shr", [64, 2, 64], BF16, 20)
        zero_h = sb(nc, es, "zeroh", [64, 2, 64])
        w0p = Pool(nc, es, "sw0", [128, 128], BF16, 5)
        ahp = Pool(nc, es, "sah", [64, 256], BF16, 5)
        up = Pool(nc, es, "su", [128, 128], BF16, 5)
        hpp = Pool(nc, es, "shp", [64, 256], F32, 5)
        Hs = {(d, g): [sb(nc, es, "H%d%d%d" % (d, g, i), [64, 2, 64]) for i in range(2)] for d in range(2) for g in range(4)}
        banks = [Buf(es.enter_context(nc.psum_tensor("sps%d" % i, [128, 512], F32))) for i in range(8)]
        PSLOT = []
        for sl_ in range(4):
            A_, B_ = banks[2 * sl_:2 * sl_ + 2]
            PSLOT.append((A_, B_, A_, B_, A_, A_, A_, B_, B_, B_, A_))
        S.op("pool", lambda e: e.memset(zero_h[:], 0.0), writes=[zero_h])
        Hr = {}
        for (d, g), hh in Hs.items():
            S.op("pool", lambda e: e.memset(hh[0][:], 0.0), writes=[hh[0]])
            Hr[(d, g)] = hrp.get()
            S.op("act", lambda e: e.activation(out=Hr[(d, g)][:], in_=zero_h[:], func=AF.Identity), reads=[zero_h], writes=[Hr[(d, g)]])
        hcur = {k: 0 for k in Hs}
        orders = [list(range(NT)), [1, 0] + list(range(NT - 1, 1, -1))]
        dq = [0]

        def load(n, g, c):
            t = ldp[n].get()
            q = "sp"
            dq[0] += 1
            src = scr[n][g * 128:(g + 1) * 128, c * 128:(c + 1) * 128].rearrange("(h j) t -> j h t", h=2)
            S.dma(t[:], src, reads=[scr[n]], writes=[t], q=q)
            return t

        import os as _os
        SL = float(_os.environ.get('SCAN_STEP', 99))
        def unit(d, c, g, slot):
            psT, ps1, ps2, ps3, psW, psU, psMN, psXU, psAH, psY, psH = PSLOT[slot]
            m2 = cst.t[:, (C_M2F if d == 0 else C_M2R):(C_M2F if d == 0 else C_M2R) + 512]
            mN = cst.t[:, (C_NF if d == 0 else C_NR):(C_NF if d == 0 else C_NR) + 256]
            id2 = cst.t[:, C_ID2:C_ID2 + 256]
            names = {"r": "rT", "a": "aT", "v": "vT", "lw": "lw%d" % d, "k": "kd%d" % d, "b": "b%d" % d}
            L = {}
            for n, dn in names.items():
                t = ldp[n].get()
                q = "sp"
                dq[0] += 1
                src = scr[dn][g * 128:(g + 1) * 128, c * 128:(c + 1) * 128].rearrange("(h j) t -> j h t", h=2)
                S.dma(t[:], src, reads=[scr[dn]], writes=[t], q=q)
                L[n] = t
            r2, a2, v2, lw2, k2, b2 = L["r"], L["a"], L["v"], L["lw"], L["k"], L["b"]
            if SL < 1:
                return
            yield True
            cum = wk.get()
            for h in range(2):
                S.op("dve", lambda e: e.tensor_tensor_scan(out=cum[:, h, :], data0=ones[0:64, 0:128], data1=lw2[:, h, :], initial=0.0, op0=ALU.mult, op1=ALU.add), reads=[ones, lw2], writes=[cum])
            if d == 1:
                tmp = wk.get()
                S.op("pool", lambda e: e.tensor_tensor(out=tmp[:], in0=cum[:], in1=lw2[:], op=ALU.subtract), reads=[cum, lw2], writes=[tmp])
                cr = wk.get()
                for h in range(2):
                    S.op("dve", lambda e: e.tensor_scalar(out=cr[:, h, :], in0=tmp[:, h, :], scalar1=-1.0, scalar2=cum[:, h, 127:128], op0=ALU.mult, op1=ALU.add), reads=[tmp, cum], writes=[cr])
                cum = cr
                lastc = 0
            else:
                lastc = 127
            if SL < 2:
                return
            ep = wk.get()
            S.op("act", lambda e: e.activation(out=ep[:], in_=cum[:], func=AF.Exp), reads=[cum], writes=[ep])
            en = wk.get()
            S.op("act", lambda e: e.activation(out=en[:], in_=cum[:], func=AF.Exp, scale=-1.0), reads=[cum], writes=[en])
            et = etp.get()
            S.op("act", lambda e: e.activation(out=et[:], in_=cum[:, :, lastc], func=AF.Exp), reads=[cum], writes=[et])
            eh = wk.get()
            for h in range(2):
                S.op("act", lambda e: e.activation(out=eh[:, h, :], in_=cum[:, h, :], func=AF.Exp, scale=-1.0, bias=cum[:, h, lastc:lastc + 1]), reads=[cum], writes=[eh])
            if SL < 3:
                return
            AR = arp.get()
            if d == 0:
                S.op("pool", lambda e: e.tensor_tensor(out=AR[:, :, 1:128], in0=a2[:, :, 1:128], in1=ep[:, :, 0:127], op=ALU.mult), reads=[a2, ep], writes=[AR])
                S.op("act", lambda e: e.activation(out=AR[:, :, 0:1], in_=a2[:, :, 0:1], func=AF.Copy), reads=[a2], writes=[AR])
            else:
                S.op("pool", lambda e: e.tensor_tensor(out=AR[:, :, 0:127], in0=a2[:, :, 0:127], in1=ep[:, :, 1:128], op=ALU.mult), reads=[a2, ep], writes=[AR])
                S.op("act", lambda e: e.activation(out=AR[:, :, 127:128], in_=a2[:, :, 127:128], func=AF.Copy), reads=[a2], writes=[AR])
            S.op("pool", lambda e: e.tensor_tensor(out=AR[:, :, 128:256], in0=r2[:], in1=ep[:], op=ALU.mult), reads=[r2, ep], writes=[AR])
            Bt = wkr.get()
            S.op("dve", lambda e: e.tensor_tensor(out=Bt[:], in0=b2[:], in1=en[:], op=ALU.mult), reads=[b2, en], writes=[Bt])
            Kt = wkr.get()
            S.op("pool", lambda e: e.tensor_tensor(out=Kt[:], in0=k2[:], in1=en[:], op=ALU.mult), reads=[k2, en], writes=[Kt])
            Bh = wkr.get()
            S.op("dve", lambda e: e.tensor_tensor(out=Bh[:], in0=b2[:], in1=eh[:], op=ALU.mult), reads=[b2, eh], writes=[Bh])
            Kh = wkr.get()
            S.op("pool", lambda e: e.tensor_tensor(out=Kh[:], in0=k2[:], in1=eh[:], op=ALU.mult), reads=[k2, eh], writes=[Kh])
            if SL < 4:
                return
            yield True
            vb = vbp.get()
            S.op("act", lambda e: e.activation(out=vb[:], in_=v2[:], func=AF.Copy), reads=[v2], writes=[vb])
            for wi, (src_t, sl) in enumerate([(AR, slice(0, 128)), (Bh, slice(0, 128)), (Kh, slice(0, 128)), (vb, slice(0, 128))]):
                for h in range(2):
                    o_ = (wi * 2 + h) * 64
                    S.op("pe", lambda e: e.transpose(psT[:].bitcast(BF16)[:, o_:o_ + 64], src_t[:, h, sl], ident64), reads=[src_t, identb], writes=[psT])
            TT = ttp.get()
            S.op("act", lambda e: e.activation(out=TT[:], in_=psT[:].bitcast(BF16)[:, 0:512], func=AF.Identity), reads=[psT], writes=[TT])
            AtT = lambda h: TT[:, h * 64:(h + 1) * 64]
            BhT = lambda h: TT[:, (2 + h) * 64:(3 + h) * 64]
            KhT = lambda h: TT[:, (4 + h) * 64:(5 + h) * 64]
            Vt = lambda h: TT[:, (6 + h) * 64:(7 + h) * 64]
            if SL < 5:
                return
            yield True
            v3 = lambda ap: ap.rearrange("p (h x) -> p h x", h=2)
            MKm = lambda typ: blk.t[:, (typ ^ d) * 1024:(typ ^ d) * 1024 + 1024]
            for h in range(2):
                S.op("pe", lambda e: e.matmul(ps1[:, h * 256:(h + 1) * 256], lhsT=Bt[:, h, :], rhs=AR[:, h, :], start=True, stop=True), reads=[Bt, AR], writes=[ps1])
            Mb = mbp.get()
            S.op("dve", lambda e: e.tensor_tensor(out=Mb[:].rearrange("p w (h x) -> p w h x", h=2), in0=v3(ps1[:])[:, :, 0:128].unsqueeze(1).to_broadcast([128, 3, 2, 128]), in1=MKm(0)[:, 0:768].rearrange("p (w h x) -> p w h x", w=3, h=2), op=ALU.mult), reads=[ps1, blk], writes=[Mb])
            Mm = mmp.get()
            S.op("dve", lambda e: e.tensor_tensor(out=v3(Mm[:]), in0=v3(ps1[:])[:, :, 128:256], in1=v3(m2)[:, :, 128:256], op=ALU.mult), reads=[ps1, cst], writes=[Mm])
            yield True
            for h in range(2):
                S.op("pe", lambda e: e.matmul(ps2[:, h * 256:(h + 1) * 256], lhsT=Kt[:, h, :], rhs=AR[:, h, :], start=True, stop=True), reads=[Kt, AR], writes=[ps2])
            Mk = mkp.get()
            S.op("dve", lambda e: e.tensor_tensor(out=Mk[:], in0=ps2[:], in1=m2, op=ALU.mult), reads=[ps2, cst], writes=[Mk])
            yield True
            for h in range(2):
                S.op("pe", lambda e: e.matmul(ps3[:, h * 128:(h + 1) * 128], lhsT=AR[:, h, 0:128], rhs=Bt[:, h, :], start=True, stop=True), reads=[AR, Bt], writes=[ps3])
            Nb = nbp.get()
            S.op("dve", lambda e: e.tensor_tensor(out=Nb[:].rearrange("p w (h x) -> p w h x", h=2), in0=v3(ps3[:, 0:256]).unsqueeze(1).to_broadcast([128, 4, 2, 128]), in1=MKm(1).rearrange("p (w h x) -> p w h x", w=4, h=2), op=ALU.mult), reads=[ps3, blk], writes=[Nb])
            if SL < 6:
                return
            yield True
            v4 = lambda ap: ap.rearrange("p (h m x) -> p h m x", h=2, m=2)
            D = dp.get()
            S.op("pool", lambda e: e.tensor_tensor(out=v4(D[:])[:, :, 0, :], in0=v3(Mb[:, 0, :]), in1=v3(id2), op=ALU.add), reads=[Mb, cst], writes=[D])
            S.op("pool", lambda e: e.tensor_tensor(out=v4(D[:])[:, :, 1, :], in0=v3(Nb[:, 0, :]), in1=v3(id2), op=ALU.add), reads=[Nb, cst], writes=[D])
            Db = D
            yield True
            Mp = lambda h: Mb[:, 0, h * 128:(h + 1) * 128]
            Np = lambda h: Nb[:, 0, h * 128:(h + 1) * 128]
            mpb, npb = Mb, Nb

            def d_update(nparts):
                nonlocal D, Db
                Dn_ = dp.get()
                S.op("dve", lambda e: e.tensor_tensor(out=Dn_[:], in0=psXU[:], in1=D[:], op=ALU.add), reads=[psXU, D], writes=[Dn_])
                D = Db = Dn_

            for lev in range(1, 4):
                for h in range(2):
                    S.op("pe", lambda e: e.matmul(psMN[:, h * 256:h * 256 + 128], lhsT=Np(h), rhs=Mp(h), start=True, stop=True), reads=[mpb, npb], writes=[psMN])
                    S.op("pe", lambda e: e.matmul(psMN[:, h * 256 + 128:h * 256 + 256], lhsT=Mp(h), rhs=Np(h), start=True, stop=True), reads=[mpb, npb], writes=[psMN])
                MN = mnp.get()
                S.op("act", lambda e: e.activation(out=MN[:], in_=psMN[:], func=AF.Identity), reads=[psMN], writes=[MN])
                yield True
                Mp = (lambda MN: (lambda h: MN[:, h * 256:h * 256 + 128]))(MN)
                Np = (lambda MN: (lambda h: MN[:, h * 256 + 128:h * 256 + 256]))(MN)
                mpb = npb = MN
                for h in range(2):
                    S.op("pe", lambda e: e.matmul(psXU[:, h * 256:h * 256 + 128], lhsT=Np(h), rhs=Db[:, h * 256:h * 256 + 128], start=True, stop=True), reads=[MN, Db], writes=[psXU])
                    S.op("pe", lambda e: e.matmul(psXU[:, h * 256 + 128:h * 256 + 256], lhsT=Mp(h), rhs=Db[:, h * 256 + 128:h * 256 + 256], start=True, stop=True), reads=[MN, Db], writes=[psXU])
                d_update(2)
                yield True
            for mi in range(3):
                last = (mi == 2)
                for h in range(2):
                    if not last:
                        S.op("pe", lambda e: e.matmul(psMN[:, h * 256:h * 256 + 128], lhsT=Mb[:, 1 + mi, h * 128:(h + 1) * 128], rhs=Db[:, h * 256 + 128:h * 256 + 256], start=True, stop=True), reads=[Mb, Db], writes=[psMN])
                    S.op("pe", lambda e: e.matmul(psMN[:, h * 256 + 128:h * 256 + 256], lhsT=Nb[:, 1 + mi, h * 128:(h + 1) * 128], rhs=Db[:, h * 256:h * 256 + 128], start=True, stop=True), reads=[Nb, Db], writes=[psMN])
                Z = mnp.get()
                if not last:
                    S.op("act", lambda e: e.activation(out=Z[:], in_=psMN[:], func=AF.Identity), reads=[psMN], writes=[Z])
                else:
                    S.op("act", lambda e: e.activation(out=v4(Z[:])[:, :, 1, :], in_=v4(psMN[:])[:, :, 1, :], func=AF.Identity), reads=[psMN], writes=[Z])
                yield True
                for h in range(2):
                    S.op("pe", lambda e: e.matmul(psXU[:, h * 256:h * 256 + 128], lhsT=Db[:, h * 256 + 128:h * 256 + 256], rhs=Z[:, h * 256 + 128:h * 256 + 256], start=True, stop=True), reads=[Db, Z], writes=[psXU])
                    if not last:
                        S.op("pe", lambda e: e.matmul(psXU[:, h * 256 + 128:h * 256 + 256], lhsT=Db[:, h * 256:h * 256 + 128], rhs=Z[:, h * 256:h * 256 + 128], start=True, stop=True), reads=[Db, Z], writes=[psXU])
                if not last:
                    d_update(2)
                else:
                    X = xp.get()
                    S.op("dve", lambda e: e.tensor_tensor(out=v3(X[:]), in0=v4(psXU[:])[:, :, 0, :], in1=v4(D[:])[:, :, 0, :], op=ALU.add), reads=[psXU, D], writes=[X])
                yield True
            yield True
            for h in range(2):
                S.op("pe", lambda e: e.matmul(psW[:, 256 + h * 64:256 + (h + 1) * 64], lhsT=Mk[:, h * 256:h * 256 + 128], rhs=Vt(h), start=True, stop=True), reads=[Mk, TT], writes=[psW])
            W0 = w0p.get()
            S.op("act", lambda e: e.activation(out=W0[:], in_=psW[:, 256:384], func=AF.Identity), reads=[psW], writes=[W0])
            yield True
            for h in range(2):
                S.op("pe", lambda e: e.matmul(psAH[0:64, 256 + h * 128:256 + (h + 1) * 128], lhsT=AtT(h), rhs=X[:, h * 128:(h + 1) * 128], start=True, stop=True), reads=[TT, X], writes=[psAH])
            AH = ahp.get()
            S.op("act", lambda e: e.activation(out=AH[:], in_=psAH[0:64, 256:512], func=AF.Identity), reads=[psAH], writes=[AH])
            if SL < 9:
                return
            yield True
            Hc = Hs[(d, g)][hcur[(d, g)]]
            Hn = Hs[(d, g)][1 - hcur[(d, g)]]
            Hcr = Hr[(d, g)]
            for h in range(2):
                S.op("pe", lambda e: e.matmul(psU[:, 384 + h * 64:384 + (h + 1) * 64], lhsT=X[:, h * 128:(h + 1) * 128], rhs=W0[:, h * 64:(h + 1) * 64], start=True, stop=False), reads=[X, W0], writes=[psU])
                S.op("pe", lambda e: e.matmul(psU[:, 384 + h * 64:384 + (h + 1) * 64], lhsT=AH[:, h * 128:(h + 1) * 128], rhs=Hcr[:, h, :], start=False, stop=True), reads=[AH, Hcr], writes=[psU])
            U = up.get()
            S.op("dve", lambda e: e.tensor_copy(out=U[:], in_=psU[:, 384:512]), reads=[psU], writes=[U])
            if SL < 10:
                return
            yield True
            for h in range(2):
                yo = psY[0:64, h * 128:(h + 1) * 128]
                S.op("pe", lambda e: e.matmul(yo, lhsT=Hcr[:, h, :], rhs=AR[:, h, 128:256], start=True, stop=False), reads=[Hcr, AR], writes=[psY])
                S.op("pe", lambda e: e.matmul(yo, lhsT=U[:, h * 64:(h + 1) * 64], rhs=Mm[:, h * 128:(h + 1) * 128], start=False, stop=False), reads=[U, Mm], writes=[psY])
                S.op("pe", lambda e: e.matmul(yo, lhsT=Vt(h), rhs=Mk[:, h * 256 + 128:h * 256 + 256], start=False, stop=True), reads=[TT, Mk], writes=[psY])
            if SL < 10.5:
                return
            yo_ = yop.get()
            S.op("act", lambda e: e.activation(out=yo_[:], in_=psY[0:64, 0:256].rearrange("p (h t) -> p h t", h=2), func=AF.Identity), reads=[psY], writes=[yo_])
            if SL < 10.7:
                return
            for h in range(2):
                q = "sp"
                dq[0] += 1
                S.dma(scr["ys%d" % d][g * 128 + h * 64:g * 128 + (h + 1) * 64, c * 128:(c + 1) * 128], yo_[:, h, :], reads=[yo_], writes=[scr["ys%d" % d]], q=q)
            if SL < 11:
                return
            yield True
            for h in range(2):
                ho = psH[0:64, 256 + h * 128:256 + (h + 1) * 128]
                S.op("pe", lambda e: e.matmul(ho, lhsT=BhT(h), rhs=U[:, 0:128], start=True, stop=False), reads=[TT, U], writes=[psH])
                S.op("pe", lambda e: e.matmul(ho, lhsT=KhT(h), rhs=TT[:, 384:512], start=False, stop=True), reads=[TT], writes=[psH])
            if SL < 11.5:
                return
            hps = hpp.get()
            S.op("act", lambda e: e.activation(out=hps[:], in_=psH[0:64, 256:512], func=AF.Identity), reads=[psH], writes=[hps])
            for h in range(2):
                S.op("dve", lambda e: e.scalar_tensor_tensor(out=Hn[:, h, :], in0=Hc[:, h, :], scalar=et[:, h:h + 1], in1=hps[:, h * 192:h * 192 + 64], op0=ALU.mult, op1=ALU.add), reads=[Hc, et, hps], writes=[Hn])
            hcur[(d, g)] = 1 - hcur[(d, g)]
            hr_ = hrp.get()
            S.op("act", lambda e: e.activation(out=hr_[:], in_=Hn[:], func=AF.Identity), reads=[Hn], writes=[hr_])
            Hr[(d, g)] = hr_
            yield True

        from collections import deque
        NCI = int(_os.environ.get('SCAN_NCI', NT))
        todo = deque()
        for ci in range(NCI):
            for d in range(int(_os.environ.get('SCAN_ND', 2))):
                for g in range(int(_os.environ.get('SCAN_NG', 4))):
                    todo.append((d, orders[d][ci], g))
        NSLOT = len(PSLOT)
        active = [None] * NSLOT
        rnd = 0
        STAG = int(_os.environ.get('SCAN_STAG', 8))
        while todo or any(a is not None for a in active):
            rnd += 1
            for k in range(NSLOT):
                if active[k] is None and todo and rnd > k * STAG:
                    d_, c_, g_ = todo.popleft()
                    active[k] = unit(d_, c_, g_, k)
                if active[k] is not None:
                    try:
                        next(active[k])
                    except StopIteration:
                        active[k] = None
        S.barrier()


def run_interleaved(gens, width):
    gens = list(gens)
    active = []
    while gens or active:
        while gens and len(active) < width:
            active.append(gens.pop(0))
        for g_ in list(active):
            try:
                next(g_)
            except StopIteration:
                active.remove(g_)


def out_blocks():
    return [(0, NCTX)] + [(NCTX + 512 * i, NCTX + 512 * (i + 1)) for i in range(8)]


def load_wbf16(nc, S, es, name, w_d, kchunks, ncols, piece=512):
    W = Buf(es.enter_context(nc.sbuf_tensor(name, [128, kchunks, ncols], BF16)))
    wv = w_d.rearrange("(k p) n -> p k n", p=128)
    with ExitStack() as e2:
        stg = Pool(nc, e2, name + "_stg", [128, kchunks, piece], F32, 2)
        i = 0
        for c0 in range(0, ncols, piece):
            c1 = min(ncols, c0 + piece)
            st = stg.get()
            S.dma(st[:, :, 0:c1 - c0], wv[:, :, c0:c1], writes=[st], q=("sp", "pool")[i % 2])
            for k in range(kchunks):
                if (i + k) % 2:
                    S.op("act", lambda e: e.activation(out=W[:, k, c0:c1], in_=st[:, k, 0:c1 - c0], func=AF.Identity), reads=[st], writes=[W])
                else:
                    S.op("dve", lambda e: e.tensor_copy(out=W[:, k, c0:c1], in_=st[:, k, 0:c1 - c0]), reads=[st], writes=[W])
            i += 1
        S.barrier()
    return W


def out_proj_gen(nc, S, pools, Wo, mixblk, G, t0, t1, x_src, x_dst, dst_buf):
    xpool, tpool, pp = pools
    for j in range((t1 - t0) // 128):
        tok = t0 + j * 128
        s = 1 if tok < NCTX else 0
        xt = xpool.get()
        S.dma(xt[:], x_src(tok), writes=[xt], q=("sp", "pool")[j % 2])
        for nh in range(2):
            ps = pp.get()
            for f in range(8):
                S.op("pe", lambda e: e.matmul(ps[:], lhsT=mixblk[:, f, j * 128:(j + 1) * 128], rhs=Wo[:, f, nh * 512:(nh + 1) * 512], start=(f == 0), stop=(f == 7)), reads=[mixblk, Wo], writes=[ps])
            tmp = tpool.get()
            S.op("dve", lambda e: e.tensor_tensor(out=tmp[:], in0=ps[:], in1=G[s][:, nh * 512:(nh + 1) * 512], op=ALU.mult), reads=[ps, G[s]], writes=[tmp])
            S.op("pool", lambda e: e.tensor_tensor(out=xt[:, nh * 512:(nh + 1) * 512], in0=tmp[:], in1=xt[:, nh * 512:(nh + 1) * 512], op=ALU.add), reads=[tmp, xt], writes=[xt])
        dst = x_dst(tok)
        if dst is not None:
            S.dma(dst, xt[:], reads=[xt], writes=[dst_buf], q=("sp", "pool")[(j + 1) % 2])
        yield


def out_proj_block(*a):
    for _ in out_proj_gen(*a):
        pass


def layer0_out(nc, S, colsb, cst, scr, w_out_d, G, src0, identb):
    def col(name, i=0, n=1):
        o, _ = COLS[name]
        return colsb.t[:, o + i:o + i + n]

    bdm = cst.t[:, 128:256]
    with ExitStack() as es:
        Wo = load_wbf16(nc, S, es, "Wo0", w_out_d, 8, 1024)
        mixp = Pool(nc, es, "mixblk", [128, 8, 512], BF16, 2)
        ldp = Pool(nc, es, "o_ld", [128, 512], F32, 16)
        wkp = Pool(nc, es, "o_wk", [128, 512], F32, 12)
        xpool = Pool(nc, es, "o_x", [128, 1024], F32, 3)
        tpool = Pool(nc, es, "o_t", [128, 512], F32, 2)
        pp = Pool(nc, es, "o_ps", [128, 512], F32, 8, psum=True)
        dq = [0]

        def ld(name, g, t0, t1):
            t = ldp.get()
            q = ("sp", "pool")[dq[0] % 2]
            dq[0] += 1
            S.dma(t[:, 0:t1 - t0], scr[name][g * 128:(g + 1) * 128, t0:t1], reads=[scr[name]], writes=[t], q=q)
            return t

        for (t0, t1) in out_blocks():
            n = t1 - t0
            mixblk = mixp.get()
            S.dma(mixblk[:, 0:4, 0:n], scr["mixT"][0:512, t0:t1].rearrange("(k p) t -> p k t", p=128), reads=[scr["mixT"]], writes=[mixblk])
            def gchain(g, mixblk=mixblk, t0=t0, t1=t1, n=n):
                y0 = ld("ys0", g, t0, t1)
                y1 = ld("ys1", g, t0, t1)
                bon = ld("bonT", g, t0, t1)
                szb = ld("szbT", g, t0, t1)
                ysum = wkp.get()
                S.op("pool", lambda e: e.tensor_tensor(out=ysum[:, 0:n], in0=y0[:, 0:n], in1=y1[:, 0:n], op=ALU.add), reads=[y0, y1], writes=[ysum])
                pm = pp.get()
                S.op("pe", lambda e: e.matmul(pm[:, 0:n], lhsT=bdm, rhs=ysum[:, 0:n], start=True, stop=True), reads=[cst, ysum], writes=[pm])
                yield
                xc = wkp.get()
                S.op("dve", lambda e: e.scalar_tensor_tensor(out=xc[:, 0:n], in0=pm[:, 0:n], scalar=-1.0 / 64, in1=ysum[:, 0:n], op0=ALU.mult, op1=ALU.add), reads=[pm, ysum], writes=[xc])
                yield
                sq = wkp.get()
                S.op("act", lambda e: e.activation(out=sq[:, 0:n], in_=xc[:, 0:n], func=AF.Square), reads=[xc], writes=[sq])
                yield
                pv = pp.get()
                S.op("pe", lambda e: e.matmul(pv[:, 0:n], lhsT=bdm, rhs=sq[:, 0:n], start=True, stop=True), reads=[cst, sq], writes=[pv])
                yield
                rs = wkp.get()
                S.op("act", lambda e: e.activation(out=rs[:, 0:n], in_=pv[:, 0:n], func=AF.Ln, scale=1.0 / 64, bias=64e-5), reads=[pv], writes=[rs])
                yield
                S.op("act", lambda e: e.activation(out=rs[:, 0:n], in_=rs[:, 0:n], func=AF.Exp, scale=-0.5), reads=[rs], writes=[rs])
                yield
                S.op("dve", lambda e: e.tensor_tensor(out=xc[:, 0:n], in0=xc[:, 0:n], in1=rs[:, 0:n], op=ALU.mult), reads=[xc, rs], writes=[xc])
                yield
                S.op("act", lambda e: e.activation(out=xc[:, 0:n], in_=xc[:, 0:n], func=AF.Identity, scale=col("lnx_w", g), bias=col("lnx_b", g)), reads=[xc, colsb], writes=[xc])
                yield
                S.op("pool", lambda e: e.tensor_tensor(out=xc[:, 0:n], in0=xc[:, 0:n], in1=bon[:, 0:n], op=ALU.add), reads=[xc, bon], writes=[xc])
                S.op("dve", lambda e: e.tensor_tensor(out=mixblk[:, 4 + g, 0:n], in0=xc[:, 0:n], in1=szb[:, 0:n], op=ALU.mult), reads=[xc, szb], writes=[mixblk])
                yield

            run_interleaved([gchain(g) for g in range(4)], 4)
            out_proj_block(nc, S, (xpool, tpool, pp), Wo, mixblk, G, t0, t1,
                           lambda tok: src0(tok // 128), lambda tok: scr["x1"][tok:tok + 128, :], scr["x1"])
        S.barrier()


def rope_apply(nc, S, pools, kr, n, p0, rope_d, permb):
    ropep, misc, wk, krp = pools
    rt = ropep.get()
    S.dma(rt[:, :, 0:n], rope_d[:, :, p0:p0 + n], writes=[rt])
    pp_ = misc.get()
    S.op("pe", lambda e: e.matmul(pp_[0:64, 0:n], lhsT=permb[0:64, 0:64], rhs=kr[0:64, 0:n], start=True, stop=True), reads=[permb, kr], writes=[pp_])
    t1_ = wk.get()
    S.op("dve", lambda e: e.tensor_tensor(out=t1_[0:64, 0:n], in0=pp_[0:64, 0:n], in1=rt[:, 1, 0:n], op=ALU.mult), reads=[pp_, rt], writes=[t1_])
    t2_ = wk.get()
    S.op("pool", lambda e: e.tensor_tensor(out=t2_[0:64, 0:n], in0=kr[0:64, 0:n], in1=rt[:, 0, 0:n], op=ALU.mult), reads=[kr, rt], writes=[t2_])
    kro = krp.get()
    S.op("dve", lambda e: e.tensor_tensor(out=kro[0:64, 0:n], in0=t1_[0:64, 0:n], in1=t2_[0:64, 0:n], op=ALU.add), reads=[t1_, t2_], writes=[kro])
    return kro


def layer1_kv(nc, S, es, hT, hTb, w_in_d, w_kvb_d, colsb, cst, scr, rope_d):
    def col(name, i=0, n=1):
        o, _ = COLS[name]
        return colsb.t[:, o + i:o + i + n]

    W = load_wbf16(nc, S, es, "W1", w_in_d, 8, 1728, piece=432)
    Wkvb = load_wbf16(nc, S, es, "Wkvb", w_kvb_d, 2, 2048, piece=1024)
    with ExitStack() as e2:
        onesb = sb(nc, e2, "onesb1", [128, 128], BF16)
        permb = sb(nc, e2, "permb1", [64, 64], BF16)
        S.op("pool", lambda e: e.memset(onesb[:], 1.0), writes=[onesb])
        S.op("dve", lambda e: e.tensor_copy(out=permb[:], in_=cst[0:64, C_PERM:C_PERM + 64]), reads=[cst], writes=[permb])
        pp = Pool(nc, e2, "kv_ps", [128, 512], F32, 7, psum=True)
        sqp = Pool(nc, e2, "kv_sq", [128, 512], BF16, 5)
        wk = Pool(nc, e2, "kv_wk", [128, 512], F32, 7)
        kvnp = Pool(nc, e2, "kv_kvn", [128, 2, 512], BF16, 2)
        ktp = Pool(nc, e2, "kv_kt", [128, 512], BF16, 4)
        vtp = Pool(nc, e2, "kv_vt", [128, 512], BF16, 3)
        krp = Pool(nc, e2, "kv_kr", [64, 512], BF16, 3)
        qnp = Pool(nc, e2, "kv_qn", [128, 3, 512], BF16, 2)
        szp = Pool(nc, e2, "kv_sz", [128, 512], F32, 3)
        ropep = Pool(nc, e2, "kv_rope", [64, 2, 512], F32, 2)
        dq = [0]

        def st(dst_buf, dst_ap, src_ap, src_buf):
            q = ("sp", "pool")[dq[0] % 2]
            dq[0] += 1
            S.dma(dst_ap, src_ap, reads=[src_buf], writes=[dst_buf], q=q)

        def rms(pss, rows, n, count):
            rs = wk.get()
            S.op("act", lambda e: e.activation(out=rs[0:rows, 0:n], in_=pss[0:rows, 0:n], func=AF.Ln, scale=1.0 / count, bias=1e-6), reads=[pss], writes=[rs])
            S.op("act", lambda e: e.activation(out=rs[0:rows, 0:n], in_=rs[0:rows, 0:n], func=AF.Exp, scale=-0.5), reads=[rs], writes=[rs])
            return rs

        for (t0, t1) in out_blocks():
            n = t1 - t0
            tiles_rd = hTb[t0 // 128:t1 // 128]

            def mm(c0, c1):
                ps = pp.get()
                for k in range(8):
                    S.op("pe", lambda e: e.matmul(ps[0:c1 - c0, 0:n], lhsT=W[:, k, c0:c1], rhs=hT[:, k, t0:t1], start=(k == 0), stop=(k == 7)), reads=[W] + tiles_rd, writes=[ps])
                return ps

            def sumsq(plist, rows):
                pss = pp.get()
                for i, p_ in enumerate(plist):
                    sq = sqp.get()
                    S.op("act", lambda e: e.activation(out=sq[0:rows, 0:n], in_=p_[0:rows, 0:n], func=AF.Square), reads=[p_], writes=[sq])
                    S.op("pe", lambda e: e.matmul(pss[0:rows, 0:n], lhsT=onesb[0:rows, 0:rows], rhs=sq[0:rows, 0:n], start=(i == 0), stop=(i == len(plist) - 1)), reads=[onesb, sq], writes=[pss])
                return pss

            pkv = [mm(384 + 128 * i, 384 + 128 * (i + 1)) for i in range(2)]
            rs = rms(sumsq(pkv, 128), 128, n, 256)
            kvn = kvnp.get()
            for i in range(2):
                S.op("dve", lambda e: e.scalar_tensor_tensor(out=kvn[:, i, 0:n], in0=pkv[i][:, 0:n], scalar=col("kv_a_norm", i), in1=rs[:, 0:n], op0=ALU.mult, op1=ALU.mult), reads=[pkv[i], rs, colsb], writes=[kvn])
            def kchain(h, kvn=kvn, n=n, t0=t0, t1=t1):
                pk = pp.get()
                for i in range(2):
                    S.op("pe", lambda e: e.matmul(pk[:, 0:n], lhsT=Wkvb[:, i, h * 256:h * 256 + 128], rhs=kvn[:, i, 0:n], start=(i == 0), stop=(i == 1)), reads=[Wkvb, kvn], writes=[pk])
                yield
                sq = sqp.get()
                S.op("act", lambda e: e.activation(out=sq[:, 0:n], in_=pk[:, 0:n], func=AF.Square), reads=[pk], writes=[sq])
                yield
                pss = pp.get()
                S.op("pe", lambda e: e.matmul(pss[:, 0:n], lhsT=onesb[:], rhs=sq[:, 0:n], start=True, stop=True), reads=[onesb, sq], writes=[pss])
                yield
                rs = wk.get()
                S.op("act", lambda e: e.activation(out=rs[:, 0:n], in_=pss[:, 0:n], func=AF.Ln, scale=1.0 / 128, bias=1e-6), reads=[pss], writes=[rs])
                yield
                S.op("act", lambda e: e.activation(out=rs[:, 0:n], in_=rs[:, 0:n], func=AF.Exp, scale=-0.5), reads=[rs], writes=[rs])
                yield
                kt = ktp.get()
                S.op("dve", lambda e: e.scalar_tensor_tensor(out=kt[:, 0:n], in0=pk[:, 0:n], scalar=col("gk_nope"), in1=rs[:, 0:n], op0=ALU.mult, op1=ALU.mult), reads=[pk, rs, colsb], writes=[kt])
                st(scr["KTd"], scr["KTd"][h * 128:(h + 1) * 128, t0:t1], kt[:, 0:n], kt)
                yield

            run_interleaved([kchain(h) for h in range(8)], 3)
            for j in range(n // 128):
                for vh in range(2):
                    pv = pp.get()
                    for i in range(2):
                        rhs = Wkvb[:, i, :].rearrange("p (h x) -> p h x", h=8)[:, vh * 4:(vh + 1) * 4, 128:256]
                        S.op("pe", lambda e: e.matmul(pv[:].rearrange("p (h x) -> p h x", h=4), lhsT=kvn[:, i, j * 128:(j + 1) * 128], rhs=rhs, start=(i == 0), stop=(i == 1)), reads=[Wkvb, kvn], writes=[pv])
                    vt = vtp.get()
                    S.op("act", lambda e: e.activation(out=vt[:], in_=pv[:], func=AF.Identity), reads=[pv], writes=[vt])
                    st(scr["Vd"], scr["Vd"][t0 + j * 128:t0 + (j + 1) * 128, vh * 512:(vh + 1) * 512], vt[:], vt)
            pr = mm(640, 704)
            rs = rms(sumsq([pr], 64), 64, n, 64)
            kr = krp.get()
            S.op("dve", lambda e: e.scalar_tensor_tensor(out=kr[0:64, 0:n], in0=pr[0:64, 0:n], scalar=col("gk_rope")[0:64, :], in1=rs[0:64, 0:n], op0=ALU.mult, op1=ALU.mult), reads=[pr, rs, colsb], writes=[kr])
            if t0 >= NCTX:
                kr = rope_apply(nc, S, (ropep, pp, wk, krp), kr, n, t0 - NCTX, rope_d, permb)
            st(scr["KRd"], scr["KRd"][0:64, t0:t1], kr[0:64, 0:n], kr)
            if t0 >= NCTX:
                l0 = t0 - NCTX
                pq = [mm(128 * i, 128 * (i + 1)) for i in range(3)]
                rs = rms(sumsq(pq, 128), 128, n, 384)
                qn = qnp.get()
                for i in range(3):
                    S.op("dve", lambda e: e.scalar_tensor_tensor(out=qn[:, i, 0:n], in0=pq[i][:, 0:n], scalar=col("q_a_norm", i), in1=rs[:, 0:n], op0=ALU.mult, op1=ALU.mult), reads=[pq[i], rs, colsb], writes=[qn])
                    st(scr["QNd"], scr["QNd"][i * 128:(i + 1) * 128, l0:l0 + n], qn[:, i, 0:n], qn)
                for c in range(8):
                    pz = mm(704 + 128 * c, 704 + 128 * (c + 1))
                    sz = szp.get()
                    S.op("act", lambda e: e.activation(out=sz[:, 0:n], in_=pz[:, 0:n], func=AF.Silu), reads=[pz], writes=[sz])
                    st(scr["SZd"], scr["SZd"][c * 128:(c + 1) * 128, l0:l0 + n], sz[:, 0:n], sz)
        S.barrier()


SM_SCALE = 192.0 ** -0.5


def layer1_attn(nc, S, scr, w_qb_d, w_o_d, colsb, cst, G, rope_d, out_d, out_buf):
    with ExitStack() as es:
        Wqb = load_wbf16(nc, S, es, "Wqb", w_qb_d, 3, 1536, piece=768)
        Wo = load_wbf16(nc, S, es, "Wo1", w_o_d, 8, 1024)
        KR = sb(nc, es, "KR", [128, T], BF16)
        S.op("pool", lambda e: e.memset(KR[64:128, :], 0.0), writes=[KR])
        S.dma(KR[0:64, :], scr["KRd"][:, :], reads=[scr["KRd"]], writes=[KR])
        gq = sb(nc, es, "gq", [128, 2])
        go, _ = COLS["gq_nope"]
        S.op("dve", lambda e: e.tensor_scalar(out=gq[:], in0=colsb[:, go:go + 2], scalar1=SM_SCALE, scalar2=None, op0=ALU.mult), reads=[colsb], writes=[gq])
        onesb = sb(nc, es, "onesb2", [128, 128], BF16)
        permb = sb(nc, es, "permb2", [64, 64], BF16)
        S.op("pool", lambda e: e.memset(onesb[:], 1.0), writes=[onesb])
        S.op("dve", lambda e: e.tensor_copy(out=permb[:], in_=cst[0:64, C_PERM:C_PERM + 64]), reads=[cst], writes=[permb])
        qnp = Pool(nc, es, "at_qn", [128, 3, 512], BF16, 2)
        szp = Pool(nc, es, "at_sz", [128, 8, 512], F32, 2)
        mixp = Pool(nc, es, "at_mix", [128, 8, 512], BF16, 2)
        kthp = Pool(nc, es, "at_kt", [128, T], BF16, 2)
        vhp = Pool(nc, es, "at_v", [128, NT, 128], BF16, 2)
        qntp = Pool(nc, es, "at_QN", [128, 512], BF16, 2)
        krp = Pool(nc, es, "at_QR", [128, 512], BF16, 4)
        for b_ in krp.bufs:
            S.op("pool", lambda e: e.memset(b_[64:128, :], 0.0), writes=[b_])
        ptp = Pool(nc, es, "at_PT", [128, 512], BF16, 6)
        sqp = Pool(nc, es, "at_sq", [128, 512], BF16, 3)
        wk = Pool(nc, es, "at_wk", [128, 512], F32, 8)
        ropep = Pool(nc, es, "at_rope", [64, 2, 512], F32, 2)
        accp = Pool(nc, es, "at_acc", [128, 512], F32, 4)
        xpool = Pool(nc, es, "at_x", [128, 1024], F32, 3)
        tpool = Pool(nc, es, "at_t", [128, 512], F32, 2)
        pS = Pool(nc, es, "at_pS", [128, 512], F32, 3, psum=True)
        pOp = Pool(nc, es, "at_pO", [128, 512], F32, 2, psum=True)
        pRp = Pool(nc, es, "at_pR", [128, 512], F32, 1, psum=True)
        misc = Pool(nc, es, "at_pm", [128, 512], F32, 2, psum=True)

        def rms(pss, rows, count):
            rs = wk.get()
            S.op("act", lambda e: e.activation(out=rs[0:rows, :], in_=pss[0:rows, :], func=AF.Sqrt, scale=1.0 / count, bias=1e-6), reads=[pss], writes=[rs])
            S.op("dve", lambda e: e.reciprocal(out=rs[0:rows, :], in_=rs[0:rows, :]), reads=[rs], writes=[rs])
            return rs

        def sumsq(p_, rows):
            sq = sqp.get()
            S.op("act", lambda e: e.activation(out=sq[0:rows, :], in_=p_[0:rows, :], func=AF.Square), reads=[p_], writes=[sq])
            pss = misc.get()
            S.op("pe", lambda e: e.matmul(pss[0:rows, :], lhsT=onesb[0:rows, 0:rows], rhs=sq[0:rows, :], start=True, stop=True), reads=[onesb, sq], writes=[pss])
            return pss

        import os as _os
        NQB = int(_os.environ.get("ATT_NQB", 8))
        blocks = {}

        def block_setup(qb):
            q0 = qb * 512
            qn = qnp.get()
            for i in range(3):
                S.dma(qn[:, i, :], scr["QNd"][i * 128:(i + 1) * 128, q0:q0 + 512], reads=[scr["QNd"]], writes=[qn], q=("sp", "pool")[i % 2])
            SZ = szp.get()
            S.dma(SZ[:], scr["SZd"][:, q0:q0 + 512].rearrange("(k p) t -> p k t", p=128), reads=[scr["SZd"]], writes=[SZ])
            mixblk = mixp.get()
            blocks[qb] = (qn, SZ, mixblk)

        def prep(qb, h):
            if h == 0:
                block_setup(qb)
            qn = blocks[qb][0]
            q0 = qb * 512
            kth = kthp.get()
            S.dma(kth[:], scr["KTd"][h * 128:(h + 1) * 128, :], reads=[scr["KTd"]], writes=[kth], q="sp")
            vh = vhp.get()
            S.dma(vh[:], scr["Vd"][:, h * 128:(h + 1) * 128].rearrange("(kt p) d -> p kt d", p=128), reads=[scr["Vd"]], writes=[vh], q="pool")
            yield None
            pqn = misc.get()
            for kc in range(3):
                S.op("pe", lambda e: e.matmul(pqn[:], lhsT=Wqb[:, kc, h * 192:h * 192 + 128], rhs=qn[:, kc, :], start=(kc == 0), stop=(kc == 2)), reads=[Wqb, qn], writes=[pqn])
            yield None
            sq = sqp.get()
            S.op("act", lambda e: e.activation(out=sq[:], in_=pqn[:], func=AF.Square), reads=[pqn], writes=[sq])
            yield None
            pss = misc.get()
            S.op("pe", lambda e: e.matmul(pss[:], lhsT=onesb[:], rhs=sq[:], start=True, stop=True), reads=[onesb, sq], writes=[pss])
            yield None
            rs = wk.get()
            S.op("act", lambda e: e.activation(out=rs[:], in_=pss[:], func=AF.Ln, scale=1.0 / 128, bias=1e-6), reads=[pss], writes=[rs])
            S.op("act", lambda e: e.activation(out=rs[:], in_=rs[:], func=AF.Exp, scale=-0.5), reads=[rs], writes=[rs])
            yield None
            QN = qntp.get()
            S.op("dve", lambda e: e.scalar_tensor_tensor(out=QN[:], in0=pqn[:], scalar=gq[:, 0:1], in1=rs[:], op0=ALU.mult, op1=ALU.mult), reads=[pqn, rs, gq], writes=[QN])
            yield None
            pqr = misc.get()
            for kc in range(3):
                S.op("pe", lambda e: e.matmul(pqr[0:64, :], lhsT=Wqb[:, kc, h * 192 + 128:h * 192 + 192], rhs=qn[:, kc, :], start=(kc == 0), stop=(kc == 2)), reads=[Wqb, qn], writes=[pqr])
            yield None
            sq2 = sqp.get()
            S.op("act", lambda e: e.activation(out=sq2[0:64, :], in_=pqr[0:64, :], func=AF.Square), reads=[pqr], writes=[sq2])
            yield None
            pss2 = misc.get()
            S.op("pe", lambda e: e.matmul(pss2[0:64, :], lhsT=onesb[0:64, 0:64], rhs=sq2[0:64, :], start=True, stop=True), reads=[onesb, sq2], writes=[pss2])
            yield None
            rs2 = wk.get()
            S.op("act", lambda e: e.activation(out=rs2[0:64, :], in_=pss2[0:64, :], func=AF.Ln, scale=1.0 / 64, bias=1e-6), reads=[pss2], writes=[rs2])
            S.op("act", lambda e: e.activation(out=rs2[0:64, :], in_=rs2[0:64, :], func=AF.Exp, scale=-0.5), reads=[rs2], writes=[rs2])
            yield None
            qr0 = krp.get()
            S.op("dve", lambda e: e.scalar_tensor_tensor(out=qr0[0:64, :], in0=pqr[0:64, :], scalar=gq[0:64, 1:2], in1=rs2[0:64, :], op0=ALU.mult, op1=ALU.mult), reads=[pqr, rs2, gq], writes=[qr0])
            yield None
            rt = ropep.get()
            S.dma(rt[:], rope_d[:, :, q0:q0 + 512], writes=[rt])
            pp_ = misc.get()
            S.op("pe", lambda e: e.matmul(pp_[0:64, :], lhsT=permb[0:64, 0:64], rhs=qr0[0:64, :], start=True, stop=True), reads=[permb, qr0], writes=[pp_])
            yield None
            t1_ = wk.get()
            S.op("dve", lambda e: e.tensor_tensor(out=t1_[0:64, :], in0=pp_[0:64, :], in1=rt[:, 1, :], op=ALU.mult), reads=[pp_, rt], writes=[t1_])
            t2_ = wk.get()
            S.op("pool", lambda e: e.tensor_tensor(out=t2_[0:64, :], in0=qr0[0:64, :], in1=rt[:, 0, :], op=ALU.mult), reads=[qr0, rt], writes=[t2_])
            yield None
            QR = krp.get()
            S.op("dve", lambda e: e.tensor_tensor(out=QR[0:64, :], in0=t1_[0:64, :], in1=t2_[0:64, :], op=ALU.add), reads=[t1_, t2_], writes=[QR])
            yield (kth, vh, QN, QR)


        def run_all(gen):
            r = None
            for r in gen:
                pass
            return r

        items = [(qb, h) for qb in range(NQB) for h in range(8)]
        pending_out = [None]
        nxt = run_all(prep(*items[0]))
        for idx, (qb, h) in enumerate(items):
            kth, vh, QN, QR = nxt
            qn, SZ, mixblk = blocks[qb]
            tok0 = NCTX + qb * 512
            gen = prep(*items[idx + 1]) if idx + 1 < len(items) else iter(())
            nxt = None
            gen_done = [False]
            pO = pOp.get()
            pR = pRp.get()

            def scores(kt):
                ps = pS.get()
                S.op("pe", lambda e: e.matmul(ps[:], lhsT=kth[:, kt * 128:(kt + 1) * 128], rhs=QN[:], start=True, stop=False), reads=[kth, QN], writes=[ps])
                S.op("pe", lambda e: e.matmul(ps[:], lhsT=KR[:, kt * 128:(kt + 1) * 128], rhs=QR[:, :], start=False, stop=True), reads=[KR, QR], writes=[ps])
                return ps

            AHEAD = 2
            psq = [scores(k_) for k_ in range(AHEAD)]
            for kt in range(NT):
                ps = psq.pop(0)
                if kt + AHEAD < NT:
                    psq.append(scores(kt + AHEAD))
                PT = ptp.get()
                S.op("act", lambda e: e.activation(out=PT[:], in_=ps[:], func=AF.Exp), reads=[ps], writes=[PT])
                S.op("pe", lambda e: e.matmul(pO[:], lhsT=vh[:, kt, :], rhs=PT[:], start=(kt == 0), stop=(kt == NT - 1)), reads=[vh, PT], writes=[pO])
                S.op("pe", lambda e: e.matmul(pR[:], lhsT=onesb[:], rhs=PT[:], start=(kt == 0), stop=(kt == NT - 1)), reads=[onesb, PT], writes=[pR])
                if kt % 2 == 1 and not gen_done[0]:
                    r_ = next(gen, "done")
                    if r_ == "done":
                        gen_done[0] = True
                    elif r_ is not None:
                        nxt = r_
                        gen_done[0] = True
                elif gen_done[0] and pending_out[0] is not None:
                    if next(pending_out[0], "done") == "done":
                        pending_out[0] = None
            for r_ in gen:
                if r_ is not None:
                    nxt = r_
            rinv = wk.get()
            S.op("act", lambda e: e.activation(out=rinv[:], in_=pR[:], func=AF.Ln), reads=[pR], writes=[rinv])
            S.op("act", lambda e: e.activation(out=rinv[:], in_=rinv[:], func=AF.Exp, scale=-1.0), reads=[rinv], writes=[rinv])
            o = wk.get()
            S.op("dve", lambda e: e.tensor_tensor(out=o[:], in0=pO[:], in1=rinv[:], op=ALU.mult), reads=[pO, rinv], writes=[o])
            S.op("pool", lambda e: e.tensor_tensor(out=mixblk[:, h, :], in0=o[:], in1=SZ[:, h, :], op=ALU.mult), reads=[o, SZ], writes=[mixblk])
            if h == 7:
                if pending_out[0] is not None:
                    for _ in pending_out[0]:
                        pass
                pending_out[0] = out_proj_gen(nc, S, (xpool, tpool, misc), Wo, mixblk, G, tok0, tok0 + 512,
                                              lambda tok: scr["x1"][tok:tok + 128, :], lambda tok: out_d[tok - NCTX:tok - NCTX + 128, :], out_buf)
                if _os.environ.get("ATT_DEFER", "1") == "0":
                    for _ in pending_out[0]:
                        pass
                    pending_out[0] = None
        if pending_out[0] is not None:
            for _ in pending_out[0]:
                pass
        S.barrier()


def make_in_maps(inp):
    consts = build_consts()
    rope = build_rope()
    lora = np.ascontiguousarray(np.concatenate([inp["ev_w2"][0], inp["ev_a2"][0]], axis=1).transpose(1, 0, 2))
    maps = []
    for b in range(8):
        maps.append({
            "x": np.ascontiguousarray(inp["x"][b]),
            "ctx": np.ascontiguousarray(inp["ctx"][b]),
            "cols": build_cols(inp, b),
            "consts": consts,
            "ada_w": np.ascontiguousarray(inp["ada_w"]),
            "ev_w_in": np.ascontiguousarray(inp["ev_w_in"][0]),
            "ev_lora": lora,
            "ev_w_out": np.ascontiguousarray(inp["ev_w_out"][0]),
            "od_w_in": np.ascontiguousarray(inp["od_w_in"][0]),
            "od_w_qb": np.ascontiguousarray(inp["od_w_qb"][0]),
            "od_w_kvb": np.ascontiguousarray(inp["od_w_kvb"][0]),
            "od_w_o": np.ascontiguousarray(inp["od_w_o"][0]),
            "rope": rope,
        })
    return maps


def kernel(**inp):
    inp = {k: np.asarray(v) for k, v in inp.items()}
    nc = build()
    res = run_bass_kernel_spmd(nc, make_in_maps(inp), core_ids=list(range(8)))
    return np.stack([res.results[b]["out"] for b in range(8)], axis=0).astype(np.float32)
```

```python
from contextlib import ExitStack
import numpy as np
import concourse.bass as bass
import concourse.mybir as mybir
from concourse.bass_utils import run_bass_kernel_spmd

F32 = mybir.dt.float32
F32R = mybir.dt.float32r
BF16 = mybir.dt.bfloat16
AF = mybir.ActivationFunctionType
ALU = mybir.AluOpType

T = 4352
NT = 34
NCTX = 256
NEG_E = -float(np.exp(-0.5))


class Buf:
    def __init__(self, t):
        self.t = t
        self.w = None
        self.r = {}

    def __getitem__(self, k):
        return self.t[k]


class Sync:
    def __init__(self, nc, es, ndma=32):
        self.nc = nc
        self.engs = {"pe": nc.tensor, "act": nc.scalar, "dve": nc.vector, "pool": nc.gpsimd, "sp": nc.sync}
        self.sem = {}
        self.cnt = {}
        self.seen = {k: {} for k in self.engs}
        for k in self.engs:
            self.sem[k] = es.enter_context(nc.semaphore("s_" + k))
            self.cnt[k] = 0
        self.dsem = [es.enter_context(nc.semaphore("d_%d" % i)) for i in range(ndma)]
        self.ndma = 0
        self.dma_last = [None] * ndma
        self.qi = 0

    def _need(self, e, evs):
        eng = self.engs[e]
        seen = self.seen[e]
        for ev in evs:
            if ev is None:
                continue
            key, sem, val, src = ev
            if src == e and e == "pe":
                continue
            if seen.get(key, 0) >= val:
                continue
            eng.wait_ge(sem, val)
            seen[key] = val

    @staticmethod
    def _deps(reads, writes):
        evs = []
        for b in reads:
            evs.append(b.w)
        for b in writes:
            evs.append(b.w)
            evs.extend(b.r.values())
        return evs

    @staticmethod
    def _record(ev, reads, writes):
        for b in reads:
            b.r[ev[0]] = ev
        for b in writes:
            b.w = ev
            b.r = {}

    def op(self, e, fn, reads=(), writes=()):
        self._need(e, self._deps(reads, writes))
        ins = fn(self.engs[e])
        self.cnt[e] += 1
        ins.then_inc(self.sem[e], 1)
        self._record((e, self.sem[e], self.cnt[e], e), reads, writes)
        return ins

    def dma(self, out, in_, reads=(), writes=(), q=None, **kw):
        if q is None:
            q = ("sp", "pool")[self.qi % 2] if False else "sp"
            self.qi += 1
        evs = self._deps(reads, writes)
        k = self.ndma % len(self.dsem)
        evs.append(self.dma_last[k])
        self._need(q, evs)
        ins = self.engs[q].dma_start(out=out, in_=in_, **kw)
        val = 16 * (self.ndma // len(self.dsem) + 1)
        ins.then_inc(self.dsem[k], 16)
        ev = ("d%d" % k, self.dsem[k], val, "dma")
        self.dma_last[k] = ev
        self.ndma += 1
        self._record(ev, reads, writes)
        return ins

    def barrier(self):
        evs = [(k, self.sem[k], self.cnt[k], "x") for k in self.engs if self.cnt[k] > 0]
        evs += [ev for ev in self.dma_last if ev is not None]
        for e in self.engs:
            self._need(e, [ev for ev in evs if ev[0] != e])


class Pool:
    def __init__(self, nc, es, name, shape, dtype, n, psum=False):
        mk = nc.psum_tensor if psum else nc.sbuf_tensor
        self.bufs = [Buf(es.enter_context(mk("%s_%d" % (name, i), shape, dtype))) for i in range(n)]
        self.i = 0

    def get(self):
        b = self.bufs[self.i % len(self.bufs)]
        self.i += 1
        return b


def sb(nc, es, name, shape, dtype=F32):
    return Buf(es.enter_context(nc.sbuf_tensor(name, shape, dtype)))


def colify(v):
    v = np.asarray(v, np.float32).reshape(-1)
    if v.size < 128:
        v = np.concatenate([v, np.zeros(128 - v.size, np.float32)])
    return np.ascontiguousarray(v.reshape(-1, 128).T)


COLS = {}


def _layout_cols():
    off = 0
    for name, n in [("c", 16), ("ada_b", 48), ("norm_w", 16), ("conv_w", 12), ("mu", 13), ("k_k", 4), ("k_a", 4),
                    ("w0", 8), ("a0", 8), ("r_k", 4), ("lnx_w", 4), ("lnx_b", 4), ("q_a_norm", 3), ("kv_a_norm", 2),
                    ("gq_nope", 1), ("gq_rope", 1), ("gk_nope", 1), ("gk_rope", 1)]:
        COLS[name] = (off, n)
        off += n
    return off


NCOL = _layout_cols()


def build_cols(inp, b):
    parts = []
    cc = np.stack([colify(inp["c"][b]), colify(inp["c_ctx"])], axis=-1).reshape(128, 16)
    parts.append(cc)
    parts.append(np.concatenate([colify(inp["ada_b"][l]) for l in range(2)], axis=1))
    parts.append(np.concatenate([colify(inp["norm_w"][l]) for l in range(2)], axis=1))
    parts.append(np.concatenate([colify(inp["ev_conv_w"][0][t]) for t in range(3)], axis=1))
    parts.append(colify(inp["ev_mu"][0]))
    parts.append(colify(inp["ev_k_k"][0]))
    parts.append(colify(inp["ev_k_a"][0]))
    parts.append(np.concatenate([colify(inp["ev_w0"][0][d]) for d in range(2)], axis=1))
    parts.append(np.concatenate([colify(inp["ev_a0"][0][d]) for d in range(2)], axis=1))
    parts.append(colify(inp["ev_r_k"][0]))
    parts.append(colify(inp["ev_lnx_w"][0]))
    parts.append(colify(inp["ev_lnx_b"][0]))
    parts.append(colify(inp["od_q_a_norm"][0]))
    parts.append(colify(inp["od_kv_a_norm"][0]))
    parts.append(colify(inp["od_gq_nope"][0]))
    parts.append(colify(inp["od_gq_rope"][0]))
    parts.append(colify(inp["od_gk_nope"][0]))
    parts.append(colify(inp["od_gk_rope"][0]))
    out = np.concatenate(parts, axis=1).astype(np.float32)
    assert out.shape == (128, NCOL), out.shape
    return np.ascontiguousarray(out)


C_M2F, C_M2R, C_NF, C_NR, C_ID2 = 256, 768, 1280, 1536, 1792
C_PERM = 2048
C_BLK = 2112
NCONST = 2112 + 2048


def build_rope():
    t = np.arange(4096)
    pos = np.stack([(t // 64).astype(np.float32), (t % 64).astype(np.float32)], axis=0)
    inv = (np.float32(10000.0) ** (-np.arange(16, dtype=np.float32) / np.float32(16))).astype(np.float32)
    ang = (pos[:, None, :] * inv[None, :, None]).astype(np.float32)
    cos = np.cos(ang).astype(np.float32)
    sin = np.sin(ang).astype(np.float32)
    out = np.zeros((64, 2, 4096), np.float32)
    for ax in range(2):
        for half in range(2):
            r0 = ax * 32 + half * 16
            out[r0:r0 + 16, 0, :] = cos[ax]
            out[r0:r0 + 16, 1, :] = -sin[ax] if half == 0 else sin[ax]
    return out


def build_consts():
    p = np.arange(128)[:, None]
    f = np.arange(128)[None, :]
    c = np.zeros((128, NCONST), np.float32)
    c[:, 0:128] = (p == f)
    c[:, 128:256] = (p // 64 == f // 64)
    lt, le, gt, ge = (p < f), (p <= f), (p > f), (p >= f)
    c[:, C_M2F:C_M2F + 512] = np.concatenate([lt, le, lt, le], axis=1)
    c[:, C_M2R:C_M2R + 512] = np.concatenate([gt, ge, gt, ge], axis=1)
    c[:, C_NF:C_NF + 256] = np.concatenate([gt, gt], axis=1)
    c[:, C_NR:C_NR + 256] = np.concatenate([lt, lt], axis=1)
    c[:, C_ID2:C_ID2 + 256] = np.concatenate([p == f, p == f], axis=1)
    j = np.arange(64)
    partner = np.where((j % 32) < 16, j + 16, j - 16)
    pm = np.zeros((128, 64), np.float32)
    pm[partner, j] = 1.0
    c[:, C_PERM:C_PERM + 64] = pm
    bd = (p // 16 == f // 16)
    mk = [bd & (p < f)]
    nk = [bd & (p > f)]
    for s_ in (16, 32, 64):
        same = (p // (2 * s_) == f // (2 * s_))
        nk.append(same & (p % (2 * s_) >= s_) & (f % (2 * s_) < s_))
        mk.append(same & (f % (2 * s_) >= s_) & (p % (2 * s_) < s_))
    for i, m_ in enumerate(mk + nk):
        c[:, C_BLK + i * 256:C_BLK + (i + 1) * 256] = np.concatenate([m_, m_], axis=1)
    return c


def build(stage=99, dbg=False):
    nc = bass.Bass("TRN2", target_bir_lowering=False)
    dt_in = lambda n, s: nc.dram_tensor(n, s, F32, kind="ExternalInput").ap()
    x_in = dt_in("x", [4096, 1024])
    ctx_in = dt_in("ctx", [NCTX, 1024])
    cols_in = dt_in("cols", [128, NCOL])
    consts_in = dt_in("consts", [128, NCONST])
    ada_w = dt_in("ada_w", [2, 1024, 3072])
    ev_w_in = dt_in("ev_w_in", [1024, 4224])
    ev_lora = dt_in("ev_lora", [128, 2, 512])
    ev_w_out = dt_in("ev_w_out", [1024, 1024])
    od_w_in = dt_in("od_w_in", [1024, 1728])
    od_w_qb = dt_in("od_w_qb", [384, 1536])
    od_w_kvb = dt_in("od_w_kvb", [256, 2048])
    od_w_o = dt_in("od_w_o", [1024, 1024])
    rope_d = dt_in("rope", [64, 2, 4096])
    out_d = nc.dram_tensor("out", [4096, 1024], F32, kind="ExternalOutput").ap()
    outs = [Buf(out_d)]
    scr = {}
    for n in ["rT", "vT", "aT", "lw0", "lw1", "kd0", "kd1", "b0", "b1", "bonT", "szbT", "ys0", "ys1"]:
        kind = "ExternalOutput" if (dbg and n in dbg) else "Internal"
        scr[n] = Buf(nc.dram_tensor(n, [512, T], F32, kind=kind).ap())
    scr["mixT"] = Buf(nc.dram_tensor("mixT", [1024, T], BF16, kind="ExternalOutput" if (dbg and "mixT" in dbg) else "Internal").ap())
    scr["x1"] = Buf(nc.dram_tensor("x1", [T, 1024], F32, kind="ExternalOutput" if (dbg and "x1" in dbg) else "Internal").ap())
    for n, shp, dt_ in [("KTd", [1024, T], BF16), ("KRd", [64, T], BF16), ("Vd", [T, 1024], BF16), ("QNd", [384, 4096], BF16), ("SZd", [1024, 4096], F32)]:
        scr[n] = Buf(nc.dram_tensor(n, shp, dt_, kind="ExternalOutput" if (dbg and n in dbg) else "Internal").ap())
    if dbg and "hT" in dbg:
        scr["hT"] = Buf(nc.dram_tensor("hT", [1024, T], BF16, kind="ExternalOutput").ap())

    with ExitStack() as es0:
        S = Sync(nc, es0)
        colsb = sb(nc, es0, "colsb", [128, NCOL])
        cst = sb(nc, es0, "cst", [128, C_BLK])
        identb = sb(nc, es0, "identb", [128, 128], BF16)
        ones = sb(nc, es0, "ones", [128, 128])
        modT = sb(nc, es0, "modT", [128, 2, 24, 2])
        gcol = sb(nc, es0, "gcol", [128, 2, 8, 2])
        G = [sb(nc, es0, "G%d" % s, [128, 1024]) for s in range(2)]
        der = sb(nc, es0, "der", [128, 32])
        S.dma(colsb[:], cols_in, writes=[colsb])
        S.dma(cst[:], consts_in[:, 0:C_BLK], writes=[cst])
        S.op("dve", lambda e: e.tensor_copy(out=identb[:], in_=cst[:, 0:128]), reads=[cst], writes=[identb])
        S.op("pool", lambda e: e.memset(ones[:], 1.0), writes=[ones])
        ident = cst.t[:, 0:128]
        bdm = cst.t[:, 128:256]

        def col(name, i=0, n=1):
            o, _ = COLS[name]
            return colsb.t[:, o + i:o + i + n]

        mo, _ = COLS["mu"]
        S.op("dve", lambda e: e.tensor_scalar(out=der[:, 0:13], in0=colsb[:, mo:mo + 13], scalar1=-1.0, scalar2=1.0, op0=ALU.mult, op1=ALU.add), reads=[colsb], writes=[der])
        S.op("dve", lambda e: e.tensor_scalar(out=der[:, 13:26], in0=colsb[:, mo:mo + 13], scalar1=0.5, scalar2=None, op0=ALU.mult), reads=[colsb], writes=[der])
        ko, _ = COLS["k_a"]
        S.op("dve", lambda e: e.tensor_scalar(out=der[:, 26:30], in0=colsb[:, ko:ko + 4], scalar1=-1.0, scalar2=1.0, op0=ALU.mult, op1=ALU.add), reads=[colsb], writes=[der])

        with ExitStack() as es:
            sc = sb(nc, es, "sc", [128, 16])
            wpool = Pool(nc, es, "adaw", [128, 8, 512], F32, 2)
            pp = Pool(nc, es, "p0ps", [128, 512], F32, 4, psum=True)
            co, _ = COLS["c"]
            S.op("act", lambda e: e.activation(out=sc[:], in_=colsb[:, co:co + 16], func=AF.Silu), reads=[colsb], writes=[sc])
            abo, _ = COLS["ada_b"]
            for l in range(2):
                wv = ada_w[l].rearrange("(k p) n -> p k n", p=128)
                for piece in range(6):
                    wst = wpool.get()
                    S.dma(wst[:], wv[:, :, piece * 512:(piece + 1) * 512], writes=[wst], q=("sp", "pool")[piece % 2])
                    for dc in range(4):
                        ps = pp.get()
                        for k in range(8):
                            S.op("pe", lambda e: e.matmul(ps[:, 0:2], lhsT=wst[:, k, dc * 128:(dc + 1) * 128], rhs=sc[:, 2 * k:2 * k + 2], start=(k == 0), stop=(k == 7)), reads=[wst, sc], writes=[ps])
                        ch = piece * 4 + dc
                        S.op("dve", lambda e: e.tensor_scalar(out=modT[:, l, ch, :], in0=ps[:, 0:2], scalar1=colsb[:, abo + l * 24 + ch:abo + l * 24 + ch + 1], scalar2=None, op0=ALU.add), reads=[ps, colsb], writes=[modT])
                nwo, _ = COLS["norm_w"]
                S.op("dve", lambda e: e.tensor_scalar(out=gcol[:, l, :, :], in0=modT[:, l, 8:16, :], scalar1=1.0, scalar2=None, op0=ALU.add), reads=[modT], writes=[gcol])
                for s in range(2):
                    S.op("dve", lambda e: e.tensor_tensor(out=gcol[:, l, :, s], in0=gcol[:, l, :, s], in1=colsb[:, nwo + l * 8:nwo + l * 8 + 8], op=ALU.mult), reads=[gcol, colsb], writes=[gcol])
            S.barrier()

        def make_gate_tiles(l):
            with ExitStack() as es:
                gb = Pool(nc, es, "gb%d" % l, [128, 128], F32, 2)
                pp = Pool(nc, es, "gps%d" % l, [128, 512], F32, 2, psum=True)
                for s in range(2):
                    for k in range(8):
                        g = gb.get()
                        S.op("dve", lambda e: e.tensor_scalar(out=g[:], in0=ones[:], scalar1=modT[:, l, 16 + k, s:s + 1], scalar2=None, op0=ALU.mult), reads=[ones, modT], writes=[g])
                        ps = pp.get()
                        S.op("pe", lambda e: e.matmul(ps[:, 0:128], lhsT=g[:], rhs=ident, start=True, stop=True), reads=[g, cst], writes=[ps])
                        S.op("act", lambda e: e.activation(out=G[s][:, k * 128:(k + 1) * 128], in_=ps[:, 0:128], func=AF.Identity), reads=[ps], writes=[G[s]])
                S.barrier()

        def norm_phase(es, l, src_rows):
            hT = es.enter_context(nc.sbuf_tensor("hT%d" % l, [128, 8, T], BF16))
            hTb = [Buf(hT) for _ in range(NT)]
            with ExitStack() as e2:
                xpool = Pool(nc, e2, "xt%d" % l, [128, 1024], F32, 6)
                jpool = Pool(nc, e2, "junk%d" % l, [128, 1024], BF16, 4)
                xnpool = Pool(nc, e2, "xn%d" % l, [128, 1024], BF16, 4)
                sspool = Pool(nc, e2, "ss%d" % l, [128, 1], F32, 8)
                pst = Pool(nc, e2, "pst%d" % l, [128, 8, 128], BF16, 5, psum=True)
                def ntile(i):
                        s = 1 if i < 2 else 0
                        xt = xpool.get()
                        S.dma(xt[:], src_rows(i), writes=[xt], q=("sp", "pool")[i % 2])
                        junk = jpool.get()
                        ss = sspool.get()
                        S.op("act", lambda e: e.activation(out=junk[:], in_=xt[:], func=AF.Square, accum_out=ss[:]), reads=[xt], writes=[junk, ss])
                        yield
                        S.op("act", lambda e: e.activation(out=ss[:], in_=ss[:], func=AF.Sqrt, scale=1.0 / 1024, bias=1e-6), reads=[ss], writes=[ss])
                        yield
                        S.op("dve", lambda e: e.reciprocal(out=ss[:], in_=ss[:]), reads=[ss], writes=[ss])
                        yield
                        xn = xnpool.get()
                        S.op("dve", lambda e: e.tensor_scalar(out=xn[:], in0=xt[:], scalar1=ss[:, 0:1], scalar2=None, op0=ALU.mult), reads=[xt, ss], writes=[xn])
                        yield
                        ps = pst.get()
                        for k in range(8):
                            S.op("pe", lambda e: e.transpose(ps[:, k, :], xn[:, k * 128:(k + 1) * 128], identb[:]), reads=[xn, identb], writes=[ps])
                        for k in range(8):
                            dst = hT[:, k, i * 128:(i + 1) * 128]
                            gsc = gcol[:, l, k, s:s + 1]
                            shf = modT[:, l, k, s:s + 1]
                            if k % 2:
                                S.op("act", lambda e: e.activation(out=dst, in_=ps[:, k, :], func=AF.Identity, scale=gsc, bias=shf), reads=[ps, gcol, modT], writes=[hTb[i]])
                                yield
                            else:
                                S.op("dve", lambda e: e.tensor_scalar(out=dst, in0=ps[:, k, :], scalar1=gsc, scalar2=shf, op0=ALU.mult, op1=ALU.add), reads=[ps, gcol, modT], writes=[hTb[i]])
                                yield
                run_interleaved([ntile(i) for i in range(NT)], 4 if l == 0 else 4)
                S.barrier()
            return hT, hTb

        def src0(i):
            return ctx_in[i * 128:(i + 1) * 128, :] if i < 2 else x_in[(i - 2) * 128:(i - 1) * 128, :]

        make_gate_tiles(0)
        with ExitStack() as esL0:
            hT, hTb = norm_phase(esL0, 0, src0)
            if dbg and "hT" in dbg:
                for k in range(8):
                    S.dma(scr["hT"][k * 128:(k + 1) * 128, :], hT[:, k, :], reads=hTb, writes=[scr["hT"]])
            if stage >= 2:
                layer0_proj(nc, S, esL0, hT, hTb, ev_w_in, ev_lora, colsb, der, cst, scr)
        S.barrier()
        if stage >= 3:
            layer0_scan(nc, S, cst, scr, ones, consts_in, identb)
            S.barrier()
        if stage >= 4:
            layer0_out(nc, S, colsb, cst, scr, ev_w_out, G, src0, identb)
            S.barrier()
        if stage >= 5:
            make_gate_tiles(1)
            with ExitStack() as esL1:
                hT1, hT1b = norm_phase(esL1, 1, lambda i: scr["x1"][i * 128:(i + 1) * 128, :])
                if dbg and "hT" in dbg:
                    for k in range(8):
                        S.dma(scr["hT"][k * 128:(k + 1) * 128, :], hT1[:, k, :], reads=hT1b, writes=[scr["hT"]])
                layer1_kv(nc, S, esL1, hT1, hT1b, od_w_in, od_w_kvb, colsb, cst, scr, rope_d)
            S.barrier()
        if stage >= 6:
            layer1_attn(nc, S, scr, od_w_qb, od_w_o, colsb, cst, G, rope_d, out_d, outs[0])
            S.barrier()
        S.barrier()
    return nc


def token_blocks():
    blks = [(0, NCTX, 0, NCTX)]
    s = NCTX
    while s < T:
        e = min(s + 510, T)
        blks.append((NCTX, T, s, e))
        s = e
    return blks


def layer0_proj(nc, S, es, hT, hTb, w_in_d, lora_d, colsb, der, cst, scr):
    def col(name, i=0, n=1):
        o, _ = COLS[name]
        return colsb.t[:, o + i:o + i + n]

    bdm = cst.t[:, 128:256]
    W = es.enter_context(nc.sbuf_tensor("W0", [128, 8, 4224], BF16))
    Wb = Buf(W)
    lora = Buf(es.enter_context(nc.sbuf_tensor("lora", [128, 2, 512], BF16)))
    with ExitStack() as e2:
        stg = Pool(nc, e2, "wstg", [128, 8, 384], F32, 2)
        wv = w_in_d.rearrange("(k p) n -> p k n", p=128)
        for pc in range(11):
            st = stg.get()
            S.dma(st[:], wv[:, :, pc * 384:(pc + 1) * 384], writes=[st], q=("sp", "pool")[pc % 2])
            for k in range(8):
                eng = ("dve", "act", "pool")[k % 3] if False else ("dve", "act")[k % 2]
                if eng == "act":
                    S.op("act", lambda e: e.activation(out=W[:, k, pc * 384:(pc + 1) * 384], in_=st[:, k, :], func=AF.Identity), reads=[st], writes=[Wb])
                else:
                    S.op("dve", lambda e: e.tensor_copy(out=W[:, k, pc * 384:(pc + 1) * 384], in_=st[:, k, :]), reads=[st], writes=[Wb])
        st = stg.get()
        S.dma(st[:, 0:3, :].rearrange("p a b -> p (a b)")[:, 0:1024], lora_d.rearrange("p a b -> p (a b)"), writes=[st])
        S.op("dve", lambda e: e.tensor_copy(out=lora[:].rearrange("p a b -> p (a b)"), in_=st[:, 0:3, :].rearrange("p a b -> p (a b)")[:, 0:1024]), reads=[st], writes=[lora])
        S.barrier()

    with ExitStack() as e2:
        pp = Pool(nc, e2, "pps", [128, 512], F32, 7, psum=True)
        f32p = Pool(nc, e2, "wk", [128, 514], F32, 10)
        outp = Pool(nc, e2, "wo", [128, 512], F32, 7)
        rpool = Pool(nc, e2, "rp", [128, 512], F32, 2)
        kpool = Pool(nc, e2, "kp", [128, 512], F32, 2)
        vpool = Pool(nc, e2, "vp", [128, 512], F32, 2)
        apool = Pool(nc, e2, "ap", [128, 512], F32, 2)
        outb = Pool(nc, e2, "wob", [128, 512], BF16, 2)
        dq = [0]

        def store(dst, rows, o0, o1, tile, n_out):
            q = ("sp", "pool")[dq[0] % 2]
            dq[0] += 1
            S.dma(dst[rows, o0:o1], tile[:, 0:n_out], reads=[tile], writes=[dst], q=q)

        for (ss, se, o0, o1) in token_blocks():
            cs = max(ss, o0 - 1)
            ce = min(se, o1 + 1)
            n = ce - cs
            n_out = o1 - o0
            oc = o0 - cs + 1
            tiles_rd = [hTb[i] for i in range(cs // 128, (ce - 1) // 128 + 1)]

            def mm(chunk):
                ps = pp.get()
                for k in range(8):
                    S.op("pe", lambda e: e.matmul(ps[:, 0:n], lhsT=W[:, k, chunk * 128:(chunk + 1) * 128], rhs=hT[:, k, cs:ce], start=(k == 0), stop=(k == 7)), reads=[Wb] + tiles_rd, writes=[ps])
                return ps

            def padded(ps, eng="act"):
                t = f32p.get()
                if cs == o0:
                    S.op("pool", lambda e: e.memset(t[:, 0:1], 0.0), writes=[t])
                if ce == o1:
                    S.op("pool", lambda e: e.memset(t[:, n + 1:n + 2], 0.0), writes=[t])
                if eng == "act":
                    S.op("act", lambda e: e.activation(out=t[:, 1:n + 1], in_=ps[:, 0:n], func=AF.Identity), reads=[ps], writes=[t])
                else:
                    S.op("dve", lambda e: e.tensor_copy(out=t[:, 1:n + 1], in_=ps[:, 0:n]), reads=[ps], writes=[t])
                return t

            def tshift(t, mi, eng="dve", dst=None):
                s2 = outp.get()
                o = (dst or outp).get()
                e_ = eng
                S.op(e_, lambda e: e.tensor_tensor(out=s2[:, 0:n_out], in0=t[:, oc - 1:oc - 1 + n_out], in1=t[:, oc + 1:oc + 1 + n_out], op=ALU.add), reads=[t], writes=[s2])
                S.op("act", lambda e: e.activation(out=o[:, 0:n_out], in_=t[:, oc:oc + n_out], func=AF.Copy, scale=der[:, mi:mi + 1]), reads=[t, der], writes=[o])
                S.op("dve", lambda e: e.scalar_tensor_tensor(out=o[:, 0:n_out], in0=s2[:, 0:n_out], scalar=der[:, 13 + mi:14 + mi], in1=o[:, 0:n_out], op0=ALU.mult, op1=ALU.add), reads=[s2, o, der], writes=[o])
                return o

            for q in range(4):
                pu = mm(q)
                pgc = mm(8 + q)
                u_sb = f32p.get()
                S.op("act", lambda e: e.activation(out=u_sb[:, 0:n], in_=pu[:, 0:n], func=AF.Identity), reads=[pu], writes=[u_sb])
                cu = f32p.get()
                if cs == o0:
                    S.op("pool", lambda e: e.memset(cu[:, 0:1], 0.0), writes=[cu])
                if ce == o1:
                    S.op("pool", lambda e: e.memset(cu[:, n + 1:n + 2], 0.0), writes=[cu])
                S.op("dve", lambda e: e.tensor_tensor(out=cu[:, 1:n + 1], in0=pgc[:, 0:n], in1=u_sb[:, 0:n], op=ALU.mult), reads=[pgc, u_sb], writes=[cu])
                y = outp.get()
                S.op("act", lambda e: e.activation(out=y[:, 0:n_out], in_=cu[:, oc:oc + n_out], func=AF.Copy, scale=col("conv_w", 4 + q)), reads=[cu, colsb], writes=[y])
                S.op("dve", lambda e: e.scalar_tensor_tensor(out=y[:, 0:n_out], in0=cu[:, oc - 1:oc - 1 + n_out], scalar=col("conv_w", q), in1=y[:, 0:n_out], op0=ALU.mult, op1=ALU.add), reads=[cu, y, colsb], writes=[y])
                S.op("dve", lambda e: e.scalar_tensor_tensor(out=y[:, 0:n_out], in0=cu[:, oc + 1:oc + 1 + n_out], scalar=col("conv_w", 8 + q), in1=y[:, 0:n_out], op0=ALU.mult, op1=ALU.add), reads=[cu, y, colsb], writes=[y])
                pgb = mm(4 + q)
                pz = mm(12 + q)
                sz = outp.get()
                S.op("act", lambda e: e.activation(out=sz[:, 0:n_out], in_=pz[:, oc - 1:oc - 1 + n_out], func=AF.Silu), reads=[pz], writes=[sz])
                S.op("dve", lambda e: e.tensor_tensor(out=sz[:, 0:n_out], in0=pgb[:, oc - 1:oc - 1 + n_out], in1=sz[:, 0:n_out], op=ALU.mult), reads=[pgb, sz], writes=[sz])
                mo_ = outb.get()
                S.op("dve", lambda e: e.tensor_tensor(out=mo_[:, 0:n_out], in0=sz[:, 0:n_out], in1=y[:, 0:n_out], op=ALU.mult), reads=[sz, y], writes=[mo_])
                store(scr["mixT"], slice(q * 128, (q + 1) * 128), o0, o1, mo_, n_out)

            pwa = mm(28)
            twa = padded(pwa)
            wa = tshift(twa, 12)
            lin = outb.get()
            S.op("act", lambda e: e.activation(out=lin[0:64, 0:n_out], in_=wa[0:64, 0:n_out], func=AF.Tanh), reads=[wa], writes=[lin])
            S.op("dve", lambda e: e.tensor_copy(out=lin[64:128, 0:n_out], in_=wa[64:128, 0:n_out]), reads=[wa], writes=[lin])
            for q in range(4):
                rows = slice(q * 128, (q + 1) * 128)
                pr_ = padded(mm(16 + q))
                pk_ = padded(mm(20 + q), "dve")
                pv_ = padded(mm(24 + q))
                pzb = mm(29 + q)
                r_ = tshift(pr_, q, "dve", rpool)
                store(scr["rT"], rows, o0, o1, r_, n_out)
                k_ = tshift(pk_, 4 + q, "pool", kpool)
                v_ = tshift(pv_, 8 + q, "dve", vpool)
                store(scr["vT"], rows, o0, o1, v_, n_out)
                szb = outp.get()
                S.op("act", lambda e: e.activation(out=szb[:, 0:n_out], in_=pzb[:, oc - 1:oc - 1 + n_out], func=AF.Silu), reads=[pzb], writes=[szb])
                store(scr["szbT"], rows, o0, o1, szb, n_out)
                kk = f32p.get()
                S.op("dve", lambda e: e.tensor_scalar(out=kk[:, 0:n_out], in0=k_[:, 0:n_out], scalar1=col("k_k", q), scalar2=None, op0=ALU.mult), reads=[k_, colsb], writes=[kk])
                sq = f32p.get()
                S.op("act", lambda e: e.activation(out=sq[:, 0:n_out], in_=kk[:, 0:n_out], func=AF.Square), reads=[kk], writes=[sq])
                pss = pp.get()
                S.op("pe", lambda e: e.matmul(pss[:, 0:n_out], lhsT=bdm, rhs=sq[:, 0:n_out], start=True, stop=True), reads=[cst, sq], writes=[pss])
                rn = f32p.get()
                S.op("act", lambda e: e.activation(out=rn[:, 0:n_out], in_=pss[:, 0:n_out], func=AF.Ln, bias=1e-24, scale=1.0), reads=[pss], writes=[rn])
                S.op("act", lambda e: e.activation(out=rn[:, 0:n_out], in_=rn[:, 0:n_out], func=AF.Exp, scale=-0.5), reads=[rn], writes=[rn])
                a_ = apool.get()
                S.op("dve", lambda e: e.scalar_tensor_tensor(out=a_[:, 0:n_out], in0=kk[:, 0:n_out], scalar=-1.0, in1=rn[:, 0:n_out], op0=ALU.mult, op1=ALU.mult), reads=[kk, rn], writes=[a_])
                store(scr["aT"], rows, o0, o1, a_, n_out)
                pbon = pp.get()
                for d in range(2):
                    plw = pp.get()
                    S.op("pe", lambda e: e.matmul(plw[:, 0:n_out], lhsT=lora[0:64, d, q * 128:(q + 1) * 128], rhs=lin[0:64, 0:n_out], start=True, stop=True), reads=[lora, lin], writes=[plw])
                    pla = pp.get()
                    S.op("pe", lambda e: e.matmul(pla[:, 0:n_out], lhsT=lora[64:128, d, q * 128:(q + 1) * 128], rhs=lin[64:128, 0:n_out], start=True, stop=True), reads=[lora, lin], writes=[pla])
                    lw = outp.get()
                    S.op("act", lambda e: e.activation(out=lw[:, 0:n_out], in_=plw[:, 0:n_out], func=AF.Sigmoid, bias=col("w0", d * 4 + q), scale=1.0), reads=[plw, colsb], writes=[lw])
                    S.op("act", lambda e: e.activation(out=lw[:, 0:n_out], in_=lw[:, 0:n_out], func=AF.Copy, scale=NEG_E), reads=[lw], writes=[lw])
                    store(scr["lw%d" % d], rows, o0, o1, lw, n_out)
                    ic = f32p.get()
                    S.op("act", lambda e: e.activation(out=ic[:, 0:n_out], in_=pla[:, 0:n_out], func=AF.Sigmoid, bias=col("a0", d * 4 + q), scale=1.0), reads=[pla, colsb], writes=[ic])
                    kf = f32p.get()
                    S.op("dve", lambda e: e.tensor_scalar(out=kf[:, 0:n_out], in0=ic[:, 0:n_out], scalar1=col("k_a", q), scalar2=der[:, 26 + q:27 + q], op0=ALU.mult, op1=ALU.add), reads=[ic, colsb, der], writes=[kf])
                    kd = outp.get()
                    S.op("dve", lambda e: e.tensor_tensor(out=kd[:, 0:n_out], in0=kf[:, 0:n_out], in1=k_[:, 0:n_out], op=ALU.mult), reads=[kf, k_], writes=[kd])
                    store(scr["kd%d" % d], rows, o0, o1, kd, n_out)
                    b_ = outp.get()
                    S.op("dve", lambda e: e.scalar_tensor_tensor(out=b_[:, 0:n_out], in0=a_[:, 0:n_out], scalar=-1.0, in1=ic[:, 0:n_out], op0=ALU.mult, op1=ALU.mult), reads=[a_, ic], writes=[b_])
                    store(scr["b%d" % d], rows, o0, o1, b_, n_out)
                    rk = f32p.get()
                    S.op("dve", lambda e: e.scalar_tensor_tensor(out=rk[:, 0:n_out], in0=kd[:, 0:n_out], scalar=col("r_k", q), in1=r_[:, 0:n_out], op0=ALU.mult, op1=ALU.mult), reads=[kd, r_, colsb], writes=[rk])
                    S.op("pe", lambda e: e.matmul(pbon[:, 0:n_out], lhsT=bdm, rhs=rk[:, 0:n_out], start=(d == 0), stop=(d == 1)), reads=[cst, rk], writes=[pbon])
                bon = outp.get()
                S.op("dve", lambda e: e.tensor_tensor(out=bon[:, 0:n_out], in0=pbon[:, 0:n_out], in1=v_[:, 0:n_out], op=ALU.mult), reads=[pbon, v_], writes=[bon])
                store(scr["bonT"], rows, o0, o1, bon, n_out)
        S.barrier()


def layer0_scan(nc, S, cst, scr, ones, consts_in, identb):
    import os as _os2
    ident64 = identb.t[0:64, 0:64]
    IDT = BF16
    with ExitStack() as es:
        blk = sb(nc, es, "blkmask", [128, 2048])
        S.dma(blk[:], consts_in[:, C_BLK:C_BLK + 2048], writes=[blk])
        SP = []
        for sl_ in range(4):
            def mk(nm, shape, dt_, n):
                return Pool(nc, es, "s%d_%s" % (sl_, nm), shape, dt_, n)
            SP.append(dict(
                yop=mk("yo", [64, 2, 128], F32, 1),
                ldp={n: mk("ld_" + n, [64, 2, 128], F32, 1) for n in ("r", "a", "v", "lw", "k", "b")},
                wk=mk("wk", [64, 2, 128], F32, 7),
                arp=mk("ar", [64, 2, 256], BF16, 1),
                etp=mk("et", [64, 2], F32, 1),
                ttp=mk("tt", [128, 512], BF16, 1),
                mmp=mk("mm", [128, 256], BF16, 1),
                mkp=mk("mk", [128, 512], BF16, 1),
                mnp=mk("mn", [128, 512], BF16, 2),
                xp=mk("x", [128, 256], BF16, 1),
                mbp=mk("mb", [128, 3, 256], BF16, 1),
                nbp=mk("nb", [128, 4, 256], BF16, 1),
                dp=mk("d", [128, 512], BF16, 2),
                wkr=mk("wkr", [64, 2, 128], BF16, 4),
                vbp=mk("vb", [64, 2, 128], BF16, 1),
                w0p=mk("w0", [128, 128], BF16, 1),
                ahp=mk("ah", [64, 256], BF16, 1),
                up=mk("u", [128, 128], BF16, 1),
                hpp=mk("hp", [64, 256], F32, 1),
            ))
        zero_h = sb(nc, es, "zeroh", [64, 2, 64])
        Hs = {(d, g): [sb(nc, es, "H%d%d%d" % (d, g, i), [64, 2, 64]) for i in range(2)] for d in range(2) for g in range(4)}
        banks = [Buf(es.enter_context(nc.psum_tensor("sps%d" % i, [128, 512], F32))) for i in range(8)]
        PSLOT = []
        for sl_ in range(4):
            A_, B_ = banks[2 * sl_:2 * sl_ + 2]
            PSLOT.append((A_, B_, A_, B_, A_, A_, A_, B_, B_, B_, A_))
        S.op("pool", lambda e: e.memset(zero_h[:], 0.0), writes=[zero_h])
        Hr = {}
        Hrs = {k: [Buf(es.enter_context(nc.sbuf_tensor("Hr%d%d%d" % (k[0], k[1], i), [64, 2, 64], BF16))) for i in range(2)] for k in Hs}
        for (d, g), hh in Hs.items():
            S.op("pool", lambda e: e.memset(hh[0][:], 0.0), writes=[hh[0]])
            Hr[(d, g)] = Hrs[(d, g)][0]
            S.op("act", lambda e: e.activation(out=Hr[(d, g)][:], in_=zero_h[:], func=AF.Identity), reads=[zero_h], writes=[Hr[(d, g)]])
        hcur = {k: 0 for k in Hs}
        orders = [list(range(NT)), [1, 0] + list(range(NT - 1, 1, -1))]
        dq = [0]

        def load(n, g, c):
            t = ldp[n].get()
            q = "sp"
            dq[0] += 1
            src = scr[n][g * 128:(g + 1) * 128, c * 128:(c + 1) * 128].rearrange("(h j) t -> j h t", h=2)
            S.dma(t[:], src, reads=[scr[n]], writes=[t], q=q)
            return t

        import os as _os
        SL = float(_os.environ.get('SCAN_STEP', 99))
        FINE = _os.environ.get('SCAN_FINE', '1') == '1'
        def unit(d, c, g, slot):
            psT, ps1, ps2, ps3, psW, psU, psMN, psXU, psAH, psY, psH = PSLOT[slot]
            P_ = SP[slot]
            yop, ldp, wk, arp, etp, ttp, mmp, mkp, mnp, xp = (P_[k_] for k_ in ("yop", "ldp", "wk", "arp", "etp", "ttp", "mmp", "mkp", "mnp", "xp"))
            mbp, nbp, dp, wkr, vbp, w0p, ahp, up, hpp = (P_[k_] for k_ in ("mbp", "nbp", "dp", "wkr", "vbp", "w0p", "ahp", "up", "hpp"))
            m2 = cst.t[:, (C_M2F if d == 0 else C_M2R):(C_M2F if d == 0 else C_M2R) + 512]
            mN = cst.t[:, (C_NF if d == 0 else C_NR):(C_NF if d == 0 else C_NR) + 256]
            id2 = cst.t[:, C_ID2:C_ID2 + 256]
            names = {"r": "rT", "a": "aT", "v": "vT", "lw": "lw%d" % d, "k": "kd%d" % d, "b": "b%d" % d}
            L = {}
            for n, dn in names.items():
                t = ldp[n].get()
                q = "sp"
                dq[0] += 1
                src = scr[dn][g * 128:(g + 1) * 128, c * 128:(c + 1) * 128].rearrange("(h j) t -> j h t", h=2)
                S.dma(t[:], src, reads=[scr[dn]], writes=[t], q=q)
                L[n] = t
            r2, a2, v2, lw2, k2, b2 = L["r"], L["a"], L["v"], L["lw"], L["k"], L["b"]
            if SL < 1:
                return
            yield True
            cum = wk.get()
            for h in range(2):
                S.op("dve", lambda e: e.tensor_tensor_scan(out=cum[:, h, :], data0=ones[0:64, 0:128], data1=lw2[:, h, :], initial=0.0, op0=ALU.mult, op1=ALU.add), reads=[ones, lw2], writes=[cum])
                if FINE: yield True
            if d == 1:
                tmp = wk.get()
                S.op("pool", lambda e: e.tensor_tensor(out=tmp[:], in0=cum[:], in1=lw2[:], op=ALU.subtract), reads=[cum, lw2], writes=[tmp])
                if FINE: yield True
                cr = wk.get()
                for h in range(2):
                    S.op("dve", lambda e: e.tensor_scalar(out=cr[:, h, :], in0=tmp[:, h, :], scalar1=-1.0, scalar2=cum[:, h, 127:128], op0=ALU.mult, op1=ALU.add), reads=[tmp, cum], writes=[cr])
                    if FINE: yield True
                cum = cr
                lastc = 0
            else:
                lastc = 127
            if SL < 2:
                return
            ep = wk.get()
            S.op("act", lambda e: e.activation(out=ep[:], in_=cum[:], func=AF.Exp), reads=[cum], writes=[ep])
            if FINE: yield True
            en = wk.get()
            S.op("act", lambda e: e.activation(out=en[:], in_=cum[:], func=AF.Exp, scale=-1.0), reads=[cum], writes=[en])
            if FINE: yield True
            et = etp.get()
            S.op("act", lambda e: e.activation(out=et[:], in_=cum[:, :, lastc], func=AF.Exp), reads=[cum], writes=[et])
            if FINE: yield True
            eh = wk.get()
            for h in range(2):
                S.op("act", lambda e: e.activation(out=eh[:, h, :], in_=cum[:, h, :], func=AF.Exp, scale=-1.0, bias=cum[:, h, lastc:lastc + 1]), reads=[cum], writes=[eh])
                if FINE: yield True
            if SL < 3:
                return
            AR = arp.get()
            if d == 0:
                S.op("pool", lambda e: e.tensor_tensor(out=AR[:, :, 1:128], in0=a2[:, :, 1:128], in1=ep[:, :, 0:127], op=ALU.mult), reads=[a2, ep], writes=[AR])
                if FINE: yield True
                S.op("act", lambda e: e.activation(out=AR[:, :, 0:1], in_=a2[:, :, 0:1], func=AF.Copy), reads=[a2], writes=[AR])
                if FINE: yield True
            else:
                S.op("pool", lambda e: e.tensor_tensor(out=AR[:, :, 0:127], in0=a2[:, :, 0:127], in1=ep[:, :, 1:128], op=ALU.mult), reads=[a2, ep], writes=[AR])
                if FINE: yield True
                S.op("act", lambda e: e.activation(out=AR[:, :, 127:128], in_=a2[:, :, 127:128], func=AF.Copy), reads=[a2], writes=[AR])
                if FINE: yield True
            S.op("pool", lambda e: e.tensor_tensor(out=AR[:, :, 128:256], in0=r2[:], in1=ep[:], op=ALU.mult), reads=[r2, ep], writes=[AR])
            if FINE: yield True
            Bt = wkr.get()
            S.op("dve", lambda e: e.tensor_tensor(out=Bt[:], in0=b2[:], in1=en[:], op=ALU.mult), reads=[b2, en], writes=[Bt])
            if FINE: yield True
            Kt = wkr.get()
            S.op("pool", lambda e: e.tensor_tensor(out=Kt[:], in0=k2[:], in1=en[:], op=ALU.mult), reads=[k2, en], writes=[Kt])
            if FINE: yield True
            Bh = wkr.get()
            S.op("dve", lambda e: e.tensor_tensor(out=Bh[:], in0=b2[:], in1=eh[:], op=ALU.mult), reads=[b2, eh], writes=[Bh])
            if FINE: yield True
            Kh = wkr.get()
            S.op("pool", lambda e: e.tensor_tensor(out=Kh[:], in0=k2[:], in1=eh[:], op=ALU.mult), reads=[k2, eh], writes=[Kh])
            if FINE: yield True
            if SL < 4:
                return
            yield True
            vb = vbp.get()
            S.op("act", lambda e: e.activation(out=vb[:], in_=v2[:], func=AF.Copy), reads=[v2], writes=[vb])
            if FINE: yield True
            for wi, (src_t, sl) in enumerate([(AR, slice(0, 128)), (Bh, slice(0, 128)), (Kh, slice(0, 128)), (vb, slice(0, 128))]):
                for h in range(2):
                    o_ = (wi * 2 + h) * 64
                    S.op("pe", lambda e: e.transpose(psT[:].bitcast(BF16)[:, o_:o_ + 64], src_t[:, h, sl], ident64), reads=[src_t, identb], writes=[psT])
            TT = ttp.get()
            S.op("act", lambda e: e.activation(out=TT[:], in_=psT[:].bitcast(BF16)[:, 0:512], func=AF.Identity), reads=[psT], writes=[TT])
            if FINE: yield True
            AtT = lambda h: TT[:, h * 64:(h + 1) * 64]
            BhT = lambda h: TT[:, (2 + h) * 64:(3 + h) * 64]
            KhT = lambda h: TT[:, (4 + h) * 64:(5 + h) * 64]
            Vt = lambda h: TT[:, (6 + h) * 64:(7 + h) * 64]
            if SL < 5:
                return
            yield True
            v3 = lambda ap: ap.rearrange("p (h x) -> p h x", h=2)
            MKm = lambda typ: blk.t[:, (typ ^ d) * 1024:(typ ^ d) * 1024 + 1024]
            for h in range(2):
                S.op("pe", lambda e: e.matmul(ps1[:, h * 256:(h + 1) * 256], lhsT=Bt[:, h, :], rhs=AR[:, h, :], start=True, stop=True), reads=[Bt, AR], writes=[ps1])
            Mb = mbp.get()
            S.op("dve", lambda e: e.tensor_tensor(out=Mb[:].rearrange("p w (h x) -> p w h x", h=2), in0=v3(ps1[:])[:, :, 0:128].unsqueeze(1).to_broadcast([128, 3, 2, 128]), in1=MKm(0)[:, 0:768].rearrange("p (w h x) -> p w h x", w=3, h=2), op=ALU.mult), reads=[ps1, blk], writes=[Mb])
            if FINE: yield True
            Mm = mmp.get()
            S.op("dve", lambda e: e.tensor_tensor(out=v3(Mm[:]), in0=v3(ps1[:])[:, :, 128:256], in1=v3(m2)[:, :, 128:256], op=ALU.mult), reads=[ps1, cst], writes=[Mm])
            yield True
            for h in range(2):
                S.op("pe", lambda e: e.matmul(ps2[:, h * 256:(h + 1) * 256], lhsT=Kt[:, h, :], rhs=AR[:, h, :], start=True, stop=True), reads=[Kt, AR], writes=[ps2])
            Mk = mkp.get()
            S.op("dve", lambda e: e.tensor_tensor(out=Mk[:], in0=ps2[:], in1=m2, op=ALU.mult), reads=[ps2, cst], writes=[Mk])
            yield True
            for h in range(2):
                S.op("pe", lambda e: e.matmul(ps3[:, h * 128:(h + 1) * 128], lhsT=AR[:, h, 0:128], rhs=Bt[:, h, :], start=True, stop=True), reads=[AR, Bt], writes=[ps3])
            Nb = nbp.get()
            S.op("dve", lambda e: e.tensor_tensor(out=Nb[:].rearrange("p w (h x) -> p w h x", h=2), in0=v3(ps3[:, 0:256]).unsqueeze(1).to_broadcast([128, 4, 2, 128]), in1=MKm(1).rearrange("p (w h x) -> p w h x", w=4, h=2), op=ALU.mult), reads=[ps3, blk], writes=[Nb])
            if FINE: yield True
            if SL < 6:
                return
            yield True
            v4 = lambda ap: ap.rearrange("p (h m x) -> p h m x", h=2, m=2)
            D = dp.get()
            S.op("pool", lambda e: e.tensor_tensor(out=v4(D[:])[:, :, 0, :], in0=v3(Mb[:, 0, :]), in1=v3(id2), op=ALU.add), reads=[Mb, cst], writes=[D])
            if FINE: yield True
            S.op("pool", lambda e: e.tensor_tensor(out=v4(D[:])[:, :, 1, :], in0=v3(Nb[:, 0, :]), in1=v3(id2), op=ALU.add), reads=[Nb, cst], writes=[D])
            if FINE: yield True
            Db = D
            yield True
            Mp = lambda h: Mb[:, 0, h * 128:(h + 1) * 128]
            Np = lambda h: Nb[:, 0, h * 128:(h + 1) * 128]
            mpb, npb = Mb, Nb

            def d_update(nparts):
                nonlocal D, Db
                Dn_ = dp.get()
                S.op("dve", lambda e: e.tensor_tensor(out=Dn_[:], in0=psXU[:], in1=D[:], op=ALU.add), reads=[psXU, D], writes=[Dn_])
                D = Db = Dn_

            for lev in range(1, 4):
                for h in range(2):
                    S.op("pe", lambda e: e.matmul(psMN[:, h * 256:h * 256 + 128], lhsT=Np(h), rhs=Mp(h), start=True, stop=True), reads=[mpb, npb], writes=[psMN])
                    S.op("pe", lambda e: e.matmul(psMN[:, h * 256 + 128:h * 256 + 256], lhsT=Mp(h), rhs=Np(h), start=True, stop=True), reads=[mpb, npb], writes=[psMN])
                MN = mnp.get()
                S.op("act", lambda e: e.activation(out=MN[:], in_=psMN[:], func=AF.Identity), reads=[psMN], writes=[MN])
                if _os.environ.get("SCAN_Y1", "1") == "1":
                    yield True
                Mp = (lambda MN: (lambda h: MN[:, h * 256:h * 256 + 128]))(MN)
                Np = (lambda MN: (lambda h: MN[:, h * 256 + 128:h * 256 + 256]))(MN)
                mpb = npb = MN
                for h in range(2):
                    S.op("pe", lambda e: e.matmul(psXU[:, h * 256:h * 256 + 128], lhsT=Np(h), rhs=Db[:, h * 256:h * 256 + 128], start=True, stop=True), reads=[MN, Db], writes=[psXU])
                    S.op("pe", lambda e: e.matmul(psXU[:, h * 256 + 128:h * 256 + 256], lhsT=Mp(h), rhs=Db[:, h * 256 + 128:h * 256 + 256], start=True, stop=True), reads=[MN, Db], writes=[psXU])
                d_update(2)
                yield True
            for mi in range(3):
                last = (mi == 2)
                for h in range(2):
                    if not last:
                        S.op("pe", lambda e: e.matmul(psMN[:, h * 256:h * 256 + 128], lhsT=Mb[:, 1 + mi, h * 128:(h + 1) * 128], rhs=Db[:, h * 256 + 128:h * 256 + 256], start=True, stop=True), reads=[Mb, Db], writes=[psMN])
                    S.op("pe", lambda e: e.matmul(psMN[:, h * 256 + 128:h * 256 + 256], lhsT=Nb[:, 1 + mi, h * 128:(h + 1) * 128], rhs=Db[:, h * 256:h * 256 + 128], start=True, stop=True), reads=[Nb, Db], writes=[psMN])
                Z = mnp.get()
                if not last:
                    S.op("act", lambda e: e.activation(out=Z[:], in_=psMN[:], func=AF.Identity), reads=[psMN], writes=[Z])
                    if FINE: yield True
                else:
                    S.op("act", lambda e: e.activation(out=v4(Z[:])[:, :, 1, :], in_=v4(psMN[:])[:, :, 1, :], func=AF.Identity), reads=[psMN], writes=[Z])
                yield True
                for h in range(2):
                    S.op("pe", lambda e: e.matmul(psXU[:, h * 256:h * 256 + 128], lhsT=Db[:, h * 256 + 128:h * 256 + 256], rhs=Z[:, h * 256 + 128:h * 256 + 256], start=True, stop=True), reads=[Db, Z], writes=[psXU])
                    if not last:
                        S.op("pe", lambda e: e.matmul(psXU[:, h * 256 + 128:h * 256 + 256], lhsT=Db[:, h * 256:h * 256 + 128], rhs=Z[:, h * 256:h * 256 + 128], start=True, stop=True), reads=[Db, Z], writes=[psXU])
                if not last:
                    d_update(2)
                else:
                    X = xp.get()
                    S.op("dve", lambda e: e.tensor_tensor(out=v3(X[:]), in0=v4(psXU[:])[:, :, 0, :], in1=v4(D[:])[:, :, 0, :], op=ALU.add), reads=[psXU, D], writes=[X])
                yield True
            yield True
            for h in range(2):
                S.op("pe", lambda e: e.matmul(psW[:, 256 + h * 64:256 + (h + 1) * 64], lhsT=Mk[:, h * 256:h * 256 + 128], rhs=Vt(h), start=True, stop=True), reads=[Mk, TT], writes=[psW])
            W0 = w0p.get()
            S.op("act", lambda e: e.activation(out=W0[:], in_=psW[:, 256:384], func=AF.Identity), reads=[psW], writes=[W0])
            yield True
            for h in range(2):
                S.op("pe", lambda e: e.matmul(psAH[0:64, 256 + h * 128:256 + (h + 1) * 128], lhsT=AtT(h), rhs=X[:, h * 128:(h + 1) * 128], start=True, stop=True), reads=[TT, X], writes=[psAH])
            AH = ahp.get()
            S.op("act", lambda e: e.activation(out=AH[:], in_=psAH[0:64, 256:512], func=AF.Identity), reads=[psAH], writes=[AH])
            if FINE: yield True
            if SL < 9:
                return
            yield True
            Hc = Hs[(d, g)][hcur[(d, g)]]
            Hn = Hs[(d, g)][1 - hcur[(d, g)]]
            Hcr = Hr[(d, g)]
            for h in range(2):
                S.op("pe", lambda e: e.matmul(psU[:, 384 + h * 64:384 + (h + 1) * 64], lhsT=X[:, h * 128:(h + 1) * 128], rhs=W0[:, h * 64:(h + 1) * 64], start=True, stop=False), reads=[X, W0], writes=[psU])
                S.op("pe", lambda e: e.matmul(psU[:, 384 + h * 64:384 + (h + 1) * 64], lhsT=AH[:, h * 128:(h + 1) * 128], rhs=Hcr[:, h, :], start=False, stop=True), reads=[AH, Hcr], writes=[psU])
            U = up.get()
            S.op("dve", lambda e: e.tensor_copy(out=U[:], in_=psU[:, 384:512]), reads=[psU], writes=[U])
            if FINE: yield True
            if SL < 10:
                return
            yield True
            for h in range(2):
                yo = psY[0:64, h * 128:(h + 1) * 128]
                S.op("pe", lambda e: e.matmul(yo, lhsT=Hcr[:, h, :], rhs=AR[:, h, 128:256], start=True, stop=False), reads=[Hcr, AR], writes=[psY])
                S.op("pe", lambda e: e.matmul(yo, lhsT=U[:, h * 64:(h + 1) * 64], rhs=Mm[:, h * 128:(h + 1) * 128], start=False, stop=False), reads=[U, Mm], writes=[psY])
                S.op("pe", lambda e: e.matmul(yo, lhsT=Vt(h), rhs=Mk[:, h * 256 + 128:h * 256 + 256], start=False, stop=True), reads=[TT, Mk], writes=[psY])
            if SL < 10.5:
                return
            yo_ = yop.get()
            S.op("act", lambda e: e.activation(out=yo_[:], in_=psY[0:64, 0:256].rearrange("p (h t) -> p h t", h=2), func=AF.Identity), reads=[psY], writes=[yo_])
            if FINE: yield True
            if SL < 10.7:
                return
            for h in range(2):
                q = "sp"
                dq[0] += 1
                S.dma(scr["ys%d" % d][g * 128 + h * 64:g * 128 + (h + 1) * 64, c * 128:(c + 1) * 128], yo_[:, h, :], reads=[yo_], writes=[scr["ys%d" % d]], q=q)
            if SL < 11:
                return
            yield True
            for h in range(2):
                ho = psH[0:64, 256 + h * 128:256 + (h + 1) * 128]
                S.op("pe", lambda e: e.matmul(ho, lhsT=BhT(h), rhs=U[:, 0:128], start=True, stop=False), reads=[TT, U], writes=[psH])
                S.op("pe", lambda e: e.matmul(ho, lhsT=KhT(h), rhs=TT[:, 384:512], start=False, stop=True), reads=[TT], writes=[psH])
            if SL < 11.5:
                return
            hps = hpp.get()
            S.op("act", lambda e: e.activation(out=hps[:], in_=psH[0:64, 256:512], func=AF.Identity), reads=[psH], writes=[hps])
            if FINE: yield True
            for h in range(2):
                S.op("dve", lambda e: e.scalar_tensor_tensor(out=Hn[:, h, :], in0=Hc[:, h, :], scalar=et[:, h:h + 1], in1=hps[:, h * 192:h * 192 + 64], op0=ALU.mult, op1=ALU.add), reads=[Hc, et, hps], writes=[Hn])
                if FINE: yield True
            hcur[(d, g)] = 1 - hcur[(d, g)]
            hr_ = Hrs[(d, g)][hcur[(d, g)]]
            S.op("act", lambda e: e.activation(out=hr_[:], in_=Hn[:], func=AF.Identity), reads=[Hn], writes=[hr_])
            if FINE: yield True
            Hr[(d, g)] = hr_
            yield True

        from collections import deque
        NCI = int(_os.environ.get('SCAN_NCI', NT))
        todo = deque()
        for ci in range(NCI):
            for d in range(int(_os.environ.get('SCAN_ND', 2))):
                for g in range(int(_os.environ.get('SCAN_NG', 4))):
                    todo.append((d, orders[d][ci], g))
        NSLOT = len(PSLOT)
        active = [None] * NSLOT
        rnd = 0
        STAG = int(_os.environ.get('SCAN_STAG', 22))
        while todo or any(a is not None for a in active):
            rnd += 1
            for k in range(NSLOT):
                if active[k] is None and todo and rnd > k * STAG:
                    d_, c_, g_ = todo.popleft()
                    active[k] = unit(d_, c_, g_, k)
                if active[k] is not None:
                    try:
                        next(active[k])
                    except StopIteration:
                        active[k] = None
        S.barrier()


def run_interleaved(gens, width):
    gens = list(gens)
    active = []
    while gens or active:
        while gens and len(active) < width:
            active.append(gens.pop(0))
        for g_ in list(active):
            try:
                next(g_)
            except StopIteration:
                active.remove(g_)


def out_blocks():
    return [(0, NCTX)] + [(NCTX + 512 * i, NCTX + 512 * (i + 1)) for i in range(8)]


def load_wbf16(nc, S, es, name, w_d, kchunks, ncols, piece=512):
    W = Buf(es.enter_context(nc.sbuf_tensor(name, [128, kchunks, ncols], BF16)))
    wv = w_d.rearrange("(k p) n -> p k n", p=128)
    with ExitStack() as e2:
        stg = Pool(nc, e2, name + "_stg", [128, kchunks, piece], F32, 2)
        i = 0
        for c0 in range(0, ncols, piece):
            c1 = min(ncols, c0 + piece)
            st = stg.get()
            S.dma(st[:, :, 0:c1 - c0], wv[:, :, c0:c1], writes=[st], q=("sp", "pool")[i % 2])
            for k in range(kchunks):
                if (i + k) % 2:
                    S.op("act", lambda e: e.activation(out=W[:, k, c0:c1], in_=st[:, k, 0:c1 - c0], func=AF.Identity), reads=[st], writes=[W])
                else:
                    S.op("dve", lambda e: e.tensor_copy(out=W[:, k, c0:c1], in_=st[:, k, 0:c1 - c0]), reads=[st], writes=[W])
            i += 1
        S.barrier()
    return W


def out_proj_gen(nc, S, pools, Wo, mixblk, G, t0, t1, x_src, x_dst, dst_buf):
    xpool, tpool, pp = pools
    for j in range((t1 - t0) // 128):
        tok = t0 + j * 128
        s = 1 if tok < NCTX else 0
        xt = xpool.get()
        S.dma(xt[:], x_src(tok), writes=[xt], q=("sp", "pool")[j % 2])
        for nh in range(2):
            ps = pp.get()
            for f in range(8):
                S.op("pe", lambda e: e.matmul(ps[:], lhsT=mixblk[:, f, j * 128:(j + 1) * 128], rhs=Wo[:, f, nh * 512:(nh + 1) * 512], start=(f == 0), stop=(f == 7)), reads=[mixblk, Wo], writes=[ps])
            tmp = tpool.get()
            S.op("dve", lambda e: e.tensor_tensor(out=tmp[:], in0=ps[:], in1=G[s][:, nh * 512:(nh + 1) * 512], op=ALU.mult), reads=[ps, G[s]], writes=[tmp])
            S.op("pool", lambda e: e.tensor_tensor(out=xt[:, nh * 512:(nh + 1) * 512], in0=tmp[:], in1=xt[:, nh * 512:(nh + 1) * 512], op=ALU.add), reads=[tmp, xt], writes=[xt])
        dst = x_dst(tok)
        if dst is not None:
            S.dma(dst, xt[:], reads=[xt], writes=[dst_buf], q=("sp", "pool")[(j + 1) % 2])
        yield


def out_proj_block(*a):
    for _ in out_proj_gen(*a):
        pass


def layer0_out(nc, S, colsb, cst, scr, w_out_d, G, src0, identb):
    def col(name, i=0, n=1):
        o, _ = COLS[name]
        return colsb.t[:, o + i:o + i + n]

    bdm = cst.t[:, 128:256]
    with ExitStack() as es:
        Wo = load_wbf16(nc, S, es, "Wo0", w_out_d, 8, 1024)
        mixp = Pool(nc, es, "mixblk", [128, 8, 512], BF16, 2)
        ldp = Pool(nc, es, "o_ld", [128, 512], F32, 16)
        wkp = Pool(nc, es, "o_wk", [128, 512], F32, 12)
        xpool = Pool(nc, es, "o_x", [128, 1024], F32, 3)
        tpool = Pool(nc, es, "o_t", [128, 512], F32, 2)
        pp = Pool(nc, es, "o_ps", [128, 512], F32, 8, psum=True)
        dq = [0]

        def ld(name, g, t0, t1):
            t = ldp.get()
            q = ("sp", "pool")[dq[0] % 2]
            dq[0] += 1
            S.dma(t[:, 0:t1 - t0], scr[name][g * 128:(g + 1) * 128, t0:t1], reads=[scr[name]], writes=[t], q=q)
            return t

        for (t0, t1) in out_blocks():
            n = t1 - t0
            mixblk = mixp.get()
            S.dma(mixblk[:, 0:4, 0:n], scr["mixT"][0:512, t0:t1].rearrange("(k p) t -> p k t", p=128), reads=[scr["mixT"]], writes=[mixblk])
            def gchain(g, mixblk=mixblk, t0=t0, t1=t1, n=n):
                y0 = ld("ys0", g, t0, t1)
                y1 = ld("ys1", g, t0, t1)
                bon = ld("bonT", g, t0, t1)
                szb = ld("szbT", g, t0, t1)
                ysum = wkp.get()
                S.op("pool", lambda e: e.tensor_tensor(out=ysum[:, 0:n], in0=y0[:, 0:n], in1=y1[:, 0:n], op=ALU.add), reads=[y0, y1], writes=[ysum])
                pm = pp.get()
                S.op("pe", lambda e: e.matmul(pm[:, 0:n], lhsT=bdm, rhs=ysum[:, 0:n], start=True, stop=True), reads=[cst, ysum], writes=[pm])
                yield
                xc = wkp.get()
                S.op("dve", lambda e: e.scalar_tensor_tensor(out=xc[:, 0:n], in0=pm[:, 0:n], scalar=-1.0 / 64, in1=ysum[:, 0:n], op0=ALU.mult, op1=ALU.add), reads=[pm, ysum], writes=[xc])
                yield
                sq = wkp.get()
                S.op("act", lambda e: e.activation(out=sq[:, 0:n], in_=xc[:, 0:n], func=AF.Square), reads=[xc], writes=[sq])
                yield
                pv = pp.get()
                S.op("pe", lambda e: e.matmul(pv[:, 0:n], lhsT=bdm, rhs=sq[:, 0:n], start=True, stop=True), reads=[cst, sq], writes=[pv])
                yield
                rs = wkp.get()
                S.op("act", lambda e: e.activation(out=rs[:, 0:n], in_=pv[:, 0:n], func=AF.Ln, scale=1.0 / 64, bias=64e-5), reads=[pv], writes=[rs])
                yield
                S.op("act", lambda e: e.activation(out=rs[:, 0:n], in_=rs[:, 0:n], func=AF.Exp, scale=-0.5), reads=[rs], writes=[rs])
                yield
                S.op("dve", lambda e: e.tensor_tensor(out=xc[:, 0:n], in0=xc[:, 0:n], in1=rs[:, 0:n], op=ALU.mult), reads=[xc, rs], writes=[xc])
                yield
                S.op("act", lambda e: e.activation(out=xc[:, 0:n], in_=xc[:, 0:n], func=AF.Identity, scale=col("lnx_w", g), bias=col("lnx_b", g)), reads=[xc, colsb], writes=[xc])
                yield
                S.op("pool", lambda e: e.tensor_tensor(out=xc[:, 0:n], in0=xc[:, 0:n], in1=bon[:, 0:n], op=ALU.add), reads=[xc, bon], writes=[xc])
                S.op("dve", lambda e: e.tensor_tensor(out=mixblk[:, 4 + g, 0:n], in0=xc[:, 0:n], in1=szb[:, 0:n], op=ALU.mult), reads=[xc, szb], writes=[mixblk])
                yield

            run_interleaved([gchain(g) for g in range(4)], 4)
            out_proj_block(nc, S, (xpool, tpool, pp), Wo, mixblk, G, t0, t1,
                           lambda tok: src0(tok // 128), lambda tok: scr["x1"][tok:tok + 128, :], scr["x1"])
        S.barrier()


def rope_apply(nc, S, pools, kr, n, p0, rope_d, permb):
    ropep, misc, wk, krp = pools
    rt = ropep.get()
    S.dma(rt[:, :, 0:n], rope_d[:, :, p0:p0 + n], writes=[rt])
    pp_ = misc.get()
    S.op("pe", lambda e: e.matmul(pp_[0:64, 0:n], lhsT=permb[0:64, 0:64], rhs=kr[0:64, 0:n], start=True, stop=True), reads=[permb, kr], writes=[pp_])
    t1_ = wk.get()
    S.op("dve", lambda e: e.tensor_tensor(out=t1_[0:64, 0:n], in0=pp_[0:64, 0:n], in1=rt[:, 1, 0:n], op=ALU.mult), reads=[pp_, rt], writes=[t1_])
    t2_ = wk.get()
    S.op("pool", lambda e: e.tensor_tensor(out=t2_[0:64, 0:n], in0=kr[0:64, 0:n], in1=rt[:, 0, 0:n], op=ALU.mult), reads=[kr, rt], writes=[t2_])
    kro = krp.get()
    S.op("dve", lambda e: e.tensor_tensor(out=kro[0:64, 0:n], in0=t1_[0:64, 0:n], in1=t2_[0:64, 0:n], op=ALU.add), reads=[t1_, t2_], writes=[kro])
    return kro


def layer1_kv(nc, S, es, hT, hTb, w_in_d, w_kvb_d, colsb, cst, scr, rope_d):
    def col(name, i=0, n=1):
        o, _ = COLS[name]
        return colsb.t[:, o + i:o + i + n]

    W = load_wbf16(nc, S, es, "W1", w_in_d, 8, 1728, piece=432)
    Wkvb = load_wbf16(nc, S, es, "Wkvb", w_kvb_d, 2, 2048, piece=1024)
    with ExitStack() as e2:
        onesb = sb(nc, e2, "onesb1", [128, 128], BF16)
        permb = sb(nc, e2, "permb1", [64, 64], BF16)
        S.op("pool", lambda e: e.memset(onesb[:], 1.0), writes=[onesb])
        S.op("dve", lambda e: e.tensor_copy(out=permb[:], in_=cst[0:64, C_PERM:C_PERM + 64]), reads=[cst], writes=[permb])
        pp = Pool(nc, e2, "kv_ps", [128, 512], F32, 7, psum=True)
        sqp = Pool(nc, e2, "kv_sq", [128, 512], BF16, 5)
        wk = Pool(nc, e2, "kv_wk", [128, 512], F32, 7)
        kvnp = Pool(nc, e2, "kv_kvn", [128, 2, 512], BF16, 2)
        ktp = Pool(nc, e2, "kv_kt", [128, 512], BF16, 4)
        vtp = Pool(nc, e2, "kv_vt", [128, 512], BF16, 3)
        krp = Pool(nc, e2, "kv_kr", [64, 512], BF16, 3)
        qnp = Pool(nc, e2, "kv_qn", [128, 3, 512], BF16, 2)
        szp = Pool(nc, e2, "kv_sz", [128, 512], F32, 3)
        ropep = Pool(nc, e2, "kv_rope", [64, 2, 512], F32, 2)
        dq = [0]

        def st(dst_buf, dst_ap, src_ap, src_buf):
            q = ("sp", "pool")[dq[0] % 2]
            dq[0] += 1
            S.dma(dst_ap, src_ap, reads=[src_buf], writes=[dst_buf], q=q)

        def rms(pss, rows, n, count):
            rs = wk.get()
            S.op("act", lambda e: e.activation(out=rs[0:rows, 0:n], in_=pss[0:rows, 0:n], func=AF.Ln, scale=1.0 / count, bias=1e-6), reads=[pss], writes=[rs])
            S.op("act", lambda e: e.activation(out=rs[0:rows, 0:n], in_=rs[0:rows, 0:n], func=AF.Exp, scale=-0.5), reads=[rs], writes=[rs])
            return rs

        for (t0, t1) in out_blocks():
            n = t1 - t0
            tiles_rd = hTb[t0 // 128:t1 // 128]

            def mm(c0, c1):
                ps = pp.get()
                for k in range(8):
                    S.op("pe", lambda e: e.matmul(ps[0:c1 - c0, 0:n], lhsT=W[:, k, c0:c1], rhs=hT[:, k, t0:t1], start=(k == 0), stop=(k == 7)), reads=[W] + tiles_rd, writes=[ps])
                return ps

            def sumsq(plist, rows):
                pss = pp.get()
                for i, p_ in enumerate(plist):
                    sq = sqp.get()
                    S.op("act", lambda e: e.activation(out=sq[0:rows, 0:n], in_=p_[0:rows, 0:n], func=AF.Square), reads=[p_], writes=[sq])
                    S.op("pe", lambda e: e.matmul(pss[0:rows, 0:n], lhsT=onesb[0:rows, 0:rows], rhs=sq[0:rows, 0:n], start=(i == 0), stop=(i == len(plist) - 1)), reads=[onesb, sq], writes=[pss])
                return pss

            pkv = [mm(384 + 128 * i, 384 + 128 * (i + 1)) for i in range(2)]
            rs = rms(sumsq(pkv, 128), 128, n, 256)
            kvn = kvnp.get()
            for i in range(2):
                S.op("dve", lambda e: e.scalar_tensor_tensor(out=kvn[:, i, 0:n], in0=pkv[i][:, 0:n], scalar=col("kv_a_norm", i), in1=rs[:, 0:n], op0=ALU.mult, op1=ALU.mult), reads=[pkv[i], rs, colsb], writes=[kvn])
            def kchain(h, kvn=kvn, n=n, t0=t0, t1=t1):
                pk = pp.get()
                for i in range(2):
                    S.op("pe", lambda e: e.matmul(pk[:, 0:n], lhsT=Wkvb[:, i, h * 256:h * 256 + 128], rhs=kvn[:, i, 0:n], start=(i == 0), stop=(i == 1)), reads=[Wkvb, kvn], writes=[pk])
                yield
                sq = sqp.get()
                S.op("act", lambda e: e.activation(out=sq[:, 0:n], in_=pk[:, 0:n], func=AF.Square), reads=[pk], writes=[sq])
                yield
                pss = pp.get()
                S.op("pe", lambda e: e.matmul(pss[:, 0:n], lhsT=onesb[:], rhs=sq[:, 0:n], start=True, stop=True), reads=[onesb, sq], writes=[pss])
                yield
                rs = wk.get()
                S.op("act", lambda e: e.activation(out=rs[:, 0:n], in_=pss[:, 0:n], func=AF.Ln, scale=1.0 / 128, bias=1e-6), reads=[pss], writes=[rs])
                yield
                S.op("act", lambda e: e.activation(out=rs[:, 0:n], in_=rs[:, 0:n], func=AF.Exp, scale=-0.5), reads=[rs], writes=[rs])
                yield
                kt = ktp.get()
                S.op("dve", lambda e: e.scalar_tensor_tensor(out=kt[:, 0:n], in0=pk[:, 0:n], scalar=col("gk_nope"), in1=rs[:, 0:n], op0=ALU.mult, op1=ALU.mult), reads=[pk, rs, colsb], writes=[kt])
                st(scr["KTd"], scr["KTd"][h * 128:(h + 1) * 128, t0:t1], kt[:, 0:n], kt)
                yield

            run_interleaved([kchain(h) for h in range(8)], 3)
            for j in range(n // 128):
                for vh in range(2):
                    pv = pp.get()
                    for i in range(2):
                        rhs = Wkvb[:, i, :].rearrange("p (h x) -> p h x", h=8)[:, vh * 4:(vh + 1) * 4, 128:256]
                        S.op("pe", lambda e: e.matmul(pv[:].rearrange("p (h x) -> p h x", h=4), lhsT=kvn[:, i, j * 128:(j + 1) * 128], rhs=rhs, start=(i == 0), stop=(i == 1)), reads=[Wkvb, kvn], writes=[pv])
                    vt = vtp.get()
                    S.op("act", lambda e: e.activation(out=vt[:], in_=pv[:], func=AF.Identity), reads=[pv], writes=[vt])
                    st(scr["Vd"], scr["Vd"][t0 + j * 128:t0 + (j + 1) * 128, vh * 512:(vh + 1) * 512], vt[:], vt)
            pr = mm(640, 704)
            rs = rms(sumsq([pr], 64), 64, n, 64)
            kr = krp.get()
            S.op("dve", lambda e: e.scalar_tensor_tensor(out=kr[0:64, 0:n], in0=pr[0:64, 0:n], scalar=col("gk_rope")[0:64, :], in1=rs[0:64, 0:n], op0=ALU.mult, op1=ALU.mult), reads=[pr, rs, colsb], writes=[kr])
            if t0 >= NCTX:
                kr = rope_apply(nc, S, (ropep, pp, wk, krp), kr, n, t0 - NCTX, rope_d, permb)
            st(scr["KRd"], scr["KRd"][0:64, t0:t1], kr[0:64, 0:n], kr)
            if t0 >= NCTX:
                l0 = t0 - NCTX
                pq = [mm(128 * i, 128 * (i + 1)) for i in range(3)]
                rs = rms(sumsq(pq, 128), 128, n, 384)
                qn = qnp.get()
                for i in range(3):
                    S.op("dve", lambda e: e.scalar_tensor_tensor(out=qn[:, i, 0:n], in0=pq[i][:, 0:n], scalar=col("q_a_norm", i), in1=rs[:, 0:n], op0=ALU.mult, op1=ALU.mult), reads=[pq[i], rs, colsb], writes=[qn])
                    st(scr["QNd"], scr["QNd"][i * 128:(i + 1) * 128, l0:l0 + n], qn[:, i, 0:n], qn)
                for c in range(8):
                    pz = mm(704 + 128 * c, 704 + 128 * (c + 1))
                    sz = szp.get()
                    S.op("act", lambda e: e.activation(out=sz[:, 0:n], in_=pz[:, 0:n], func=AF.Silu), reads=[pz], writes=[sz])
                    st(scr["SZd"], scr["SZd"][c * 128:(c + 1) * 128, l0:l0 + n], sz[:, 0:n], sz)
        S.barrier()


SM_SCALE = 192.0 ** -0.5


def layer1_attn(nc, S, scr, w_qb_d, w_o_d, colsb, cst, G, rope_d, out_d, out_buf):
    with ExitStack() as es:
        Wqb = load_wbf16(nc, S, es, "Wqb", w_qb_d, 3, 1536, piece=768)
        Wo = load_wbf16(nc, S, es, "Wo1", w_o_d, 8, 1024)
        KR = sb(nc, es, "KR", [128, T], BF16)
        S.op("pool", lambda e: e.memset(KR[64:128, :], 0.0), writes=[KR])
        S.dma(KR[0:64, :], scr["KRd"][:, :], reads=[scr["KRd"]], writes=[KR])
        gq = sb(nc, es, "gq", [128, 2])
        go, _ = COLS["gq_nope"]
        S.op("dve", lambda e: e.tensor_scalar(out=gq[:], in0=colsb[:, go:go + 2], scalar1=SM_SCALE, scalar2=None, op0=ALU.mult), reads=[colsb], writes=[gq])
        onesb = sb(nc, es, "onesb2", [128, 128], BF16)
        permb = sb(nc, es, "permb2", [64, 64], BF16)
        S.op("pool", lambda e: e.memset(onesb[:], 1.0), writes=[onesb])
        S.op("dve", lambda e: e.tensor_copy(out=permb[:], in_=cst[0:64, C_PERM:C_PERM + 64]), reads=[cst], writes=[permb])
        qnp = Pool(nc, es, "at_qn", [128, 3, 512], BF16, 2)
        szp = Pool(nc, es, "at_sz", [128, 8, 512], F32, 2)
        mixp = Pool(nc, es, "at_mix", [128, 8, 512], BF16, 2)
        kthp = Pool(nc, es, "at_kt", [128, T], BF16, 2)
        vhp = Pool(nc, es, "at_v", [128, NT, 128], BF16, 2)
        qntp = Pool(nc, es, "at_QN", [128, 512], BF16, 2)
        krp = Pool(nc, es, "at_QR", [128, 512], BF16, 4)
        for b_ in krp.bufs:
            S.op("pool", lambda e: e.memset(b_[64:128, :], 0.0), writes=[b_])
        ptp = Pool(nc, es, "at_PT", [128, 512], BF16, 6)
        sqp = Pool(nc, es, "at_sq", [128, 512], BF16, 3)
        wk = Pool(nc, es, "at_wk", [128, 512], F32, 8)
        ropep = Pool(nc, es, "at_rope", [64, 2, 512], F32, 2)
        accp = Pool(nc, es, "at_acc", [128, 512], F32, 4)
        xpool = Pool(nc, es, "at_x", [128, 1024], F32, 3)
        tpool = Pool(nc, es, "at_t", [128, 512], F32, 2)
        pS = Pool(nc, es, "at_pS", [128, 512], F32, 3, psum=True)
        pOp = Pool(nc, es, "at_pO", [128, 512], F32, 2, psum=True)
        pRp = Pool(nc, es, "at_pR", [128, 512], F32, 1, psum=True)
        misc = Pool(nc, es, "at_pm", [128, 512], F32, 2, psum=True)

        def rms(pss, rows, count):
            rs = wk.get()
            S.op("act", lambda e: e.activation(out=rs[0:rows, :], in_=pss[0:rows, :], func=AF.Sqrt, scale=1.0 / count, bias=1e-6), reads=[pss], writes=[rs])
            S.op("dve", lambda e: e.reciprocal(out=rs[0:rows, :], in_=rs[0:rows, :]), reads=[rs], writes=[rs])
            return rs

        def sumsq(p_, rows):
            sq = sqp.get()
            S.op("act", lambda e: e.activation(out=sq[0:rows, :], in_=p_[0:rows, :], func=AF.Square), reads=[p_], writes=[sq])
            pss = misc.get()
            S.op("pe", lambda e: e.matmul(pss[0:rows, :], lhsT=onesb[0:rows, 0:rows], rhs=sq[0:rows, :], start=True, stop=True), reads=[onesb, sq], writes=[pss])
            return pss

        import os as _os
        NQB = int(_os.environ.get("ATT_NQB", 8))
        blocks = {}

        def block_setup(qb):
            q0 = qb * 512
            qn = qnp.get()
            for i in range(3):
                S.dma(qn[:, i, :], scr["QNd"][i * 128:(i + 1) * 128, q0:q0 + 512], reads=[scr["QNd"]], writes=[qn], q=("sp", "pool")[i % 2])
            SZ = szp.get()
            S.dma(SZ[:], scr["SZd"][:, q0:q0 + 512].rearrange("(k p) t -> p k t", p=128), reads=[scr["SZd"]], writes=[SZ])
            mixblk = mixp.get()
            blocks[qb] = (qn, SZ, mixblk)

        def prep(qb, h):
            if h == 0:
                block_setup(qb)
            qn = blocks[qb][0]
            q0 = qb * 512
            kth = kthp.get()
            S.dma(kth[:], scr["KTd"][h * 128:(h + 1) * 128, :], reads=[scr["KTd"]], writes=[kth], q="sp")
            vh = vhp.get()
            S.dma(vh[:], scr["Vd"][:, h * 128:(h + 1) * 128].rearrange("(kt p) d -> p kt d", p=128), reads=[scr["Vd"]], writes=[vh], q="pool")
            yield None
            pqn = misc.get()
            for kc in range(3):
                S.op("pe", lambda e: e.matmul(pqn[:], lhsT=Wqb[:, kc, h * 192:h * 192 + 128], rhs=qn[:, kc, :], start=(kc == 0), stop=(kc == 2)), reads=[Wqb, qn], writes=[pqn])
            yield None
            sq = sqp.get()
            S.op("act", lambda e: e.activation(out=sq[:], in_=pqn[:], func=AF.Square), reads=[pqn], writes=[sq])
            yield None
            pss = misc.get()
            S.op("pe", lambda e: e.matmul(pss[:], lhsT=onesb[:], rhs=sq[:], start=True, stop=True), reads=[onesb, sq], writes=[pss])
            yield None
            rs = wk.get()
            S.op("act", lambda e: e.activation(out=rs[:], in_=pss[:], func=AF.Ln, scale=1.0 / 128, bias=1e-6), reads=[pss], writes=[rs])
            S.op("act", lambda e: e.activation(out=rs[:], in_=rs[:], func=AF.Exp, scale=-0.5), reads=[rs], writes=[rs])
            yield None
            QN = qntp.get()
            S.op("dve", lambda e: e.scalar_tensor_tensor(out=QN[:], in0=pqn[:], scalar=gq[:, 0:1], in1=rs[:], op0=ALU.mult, op1=ALU.mult), reads=[pqn, rs, gq], writes=[QN])
            yield None
            pqr = misc.get()
            for kc in range(3):
                S.op("pe", lambda e: e.matmul(pqr[0:64, :], lhsT=Wqb[:, kc, h * 192 + 128:h * 192 + 192], rhs=qn[:, kc, :], start=(kc == 0), stop=(kc == 2)), reads=[Wqb, qn], writes=[pqr])
            yield None
            sq2 = sqp.get()
            S.op("act", lambda e: e.activation(out=sq2[0:64, :], in_=pqr[0:64, :], func=AF.Square), reads=[pqr], writes=[sq2])
            yield None
            pss2 = misc.get()
            S.op("pe", lambda e: e.matmul(pss2[0:64, :], lhsT=onesb[0:64, 0:64], rhs=sq2[0:64, :], start=True, stop=True), reads=[onesb, sq2], writes=[pss2])
            yield None
            rs2 = wk.get()
            S.op("act", lambda e: e.activation(out=rs2[0:64, :], in_=pss2[0:64, :], func=AF.Ln, scale=1.0 / 64, bias=1e-6), reads=[pss2], writes=[rs2])
            S.op("act", lambda e: e.activation(out=rs2[0:64, :], in_=rs2[0:64, :], func=AF.Exp, scale=-0.5), reads=[rs2], writes=[rs2])
            yield None
            qr0 = krp.get()
            S.op("dve", lambda e: e.scalar_tensor_tensor(out=qr0[0:64, :], in0=pqr[0:64, :], scalar=gq[0:64, 1:2], in1=rs2[0:64, :], op0=ALU.mult, op1=ALU.mult), reads=[pqr, rs2, gq], writes=[qr0])
            yield None
            rt = ropep.get()
            S.dma(rt[:], rope_d[:, :, q0:q0 + 512], writes=[rt])
            pp_ = misc.get()
            S.op("pe", lambda e: e.matmul(pp_[0:64, :], lhsT=permb[0:64, 0:64], rhs=qr0[0:64, :], start=True, stop=True), reads=[permb, qr0], writes=[pp_])
            yield None
            t1_ = wk.get()
            S.op("dve", lambda e: e.tensor_tensor(out=t1_[0:64, :], in0=pp_[0:64, :], in1=rt[:, 1, :], op=ALU.mult), reads=[pp_, rt], writes=[t1_])
            t2_ = wk.get()
            S.op("pool", lambda e: e.tensor_tensor(out=t2_[0:64, :], in0=qr0[0:64, :], in1=rt[:, 0, :], op=ALU.mult), reads=[qr0, rt], writes=[t2_])
            yield None
            QR = krp.get()
            S.op("dve", lambda e: e.tensor_tensor(out=QR[0:64, :], in0=t1_[0:64, :], in1=t2_[0:64, :], op=ALU.add), reads=[t1_, t2_], writes=[QR])
            yield (kth, vh, QN, QR)


        def run_all(gen):
            r = None
            for r in gen:
                pass
            return r

        items = [(qb, h) for qb in range(NQB) for h in range(8)]
        pending_out = [None]
        nxt = run_all(prep(*items[0]))
        for idx, (qb, h) in enumerate(items):
            kth, vh, QN, QR = nxt
            qn, SZ, mixblk = blocks[qb]
            tok0 = NCTX + qb * 512
            gen = prep(*items[idx + 1]) if idx + 1 < len(items) else iter(())
            nxt = None
            gen_done = [False]
            pO = pOp.get()
            pR = pRp.get()

            def scores(kt):
                ps = pS.get()
                S.op("pe", lambda e: e.matmul(ps[:], lhsT=kth[:, kt * 128:(kt + 1) * 128], rhs=QN[:], start=True, stop=False), reads=[kth, QN], writes=[ps])
                S.op("pe", lambda e: e.matmul(ps[:], lhsT=KR[:, kt * 128:(kt + 1) * 128], rhs=QR[:, :], start=False, stop=True), reads=[KR, QR], writes=[ps])
                return ps

            AHEAD = 2
            psq = [scores(k_) for k_ in range(AHEAD)]
            for kt in range(NT):
                ps = psq.pop(0)
                if kt + AHEAD < NT:
                    psq.append(scores(kt + AHEAD))
                PT = ptp.get()
                S.op("act", lambda e: e.activation(out=PT[:], in_=ps[:], func=AF.Exp), reads=[ps], writes=[PT])
                S.op("pe", lambda e: e.matmul(pO[:], lhsT=vh[:, kt, :], rhs=PT[:], start=(kt == 0), stop=(kt == NT - 1)), reads=[vh, PT], writes=[pO])
                S.op("pe", lambda e: e.matmul(pR[:], lhsT=onesb[:], rhs=PT[:], start=(kt == 0), stop=(kt == NT - 1)), reads=[onesb, PT], writes=[pR])
                if kt % 2 == 1 and not gen_done[0]:
                    r_ = next(gen, "done")
                    if r_ == "done":
                        gen_done[0] = True
                    elif r_ is not None:
                        nxt = r_
                        gen_done[0] = True
                elif gen_done[0] and pending_out[0] is not None:
                    if next(pending_out[0], "done") == "done":
                        pending_out[0] = None
            for r_ in gen:
                if r_ is not None:
                    nxt = r_
            rinv = wk.get()
            S.op("act", lambda e: e.activation(out=rinv[:], in_=pR[:], func=AF.Ln), reads=[pR], writes=[rinv])
            S.op("act", lambda e: e.activation(out=rinv[:], in_=rinv[:], func=AF.Exp, scale=-1.0), reads=[rinv], writes=[rinv])
            o = wk.get()
            S.op("dve", lambda e: e.tensor_tensor(out=o[:], in0=pO[:], in1=rinv[:], op=ALU.mult), reads=[pO, rinv], writes=[o])
            S.op("pool", lambda e: e.tensor_tensor(out=mixblk[:, h, :], in0=o[:], in1=SZ[:, h, :], op=ALU.mult), reads=[o, SZ], writes=[mixblk])
            if h == 7:
                if pending_out[0] is not None:
                    for _ in pending_out[0]:
                        pass
                pending_out[0] = out_proj_gen(nc, S, (xpool, tpool, misc), Wo, mixblk, G, tok0, tok0 + 512,
                                              lambda tok: scr["x1"][tok:tok + 128, :], lambda tok: out_d[tok - NCTX:tok - NCTX + 128, :], out_buf)
                if _os.environ.get("ATT_DEFER", "1") == "0":
                    for _ in pending_out[0]:
                        pass
                    pending_out[0] = None
        if pending_out[0] is not None:
            for _ in pending_out[0]:
                pass
        S.barrier()


def make_in_maps(inp):
    consts = build_consts()
    rope = build_rope()
    lora = np.ascontiguousarray(np.concatenate([inp["ev_w2"][0], inp["ev_a2"][0]], axis=1).transpose(1, 0, 2))
    maps = []
    for b in range(8):
        maps.append({
            "x": np.ascontiguousarray(inp["x"][b]),
            "ctx": np.ascontiguousarray(inp["ctx"][b]),
            "cols": build_cols(inp, b),
            "consts": consts,
            "ada_w": np.ascontiguousarray(inp["ada_w"]),
            "ev_w_in": np.ascontiguousarray(inp["ev_w_in"][0]),
            "ev_lora": lora,
            "ev_w_out": np.ascontiguousarray(inp["ev_w_out"][0]),
            "od_w_in": np.ascontiguousarray(inp["od_w_in"][0]),
            "od_w_qb": np.ascontiguousarray(inp["od_w_qb"][0]),
            "od_w_kvb": np.ascontiguousarray(inp["od_w_kvb"][0]),
            "od_w_o": np.ascontiguousarray(inp["od_w_o"][0]),
            "rope": rope,
        })
    return maps


def kernel(**inp):
    inp = {k: np.asarray(v) for k, v in inp.items()}
    nc = build()
    res = run_bass_kernel_spmd(nc, make_in_maps(inp), core_ids=list(range(8)))
    return np.stack([res.results[b]["out"] for b in range(8)], axis=0).astype(np.float32)
```

```python
from contextlib import ExitStack
import numpy as np
import concourse.bass as bass
import concourse.mybir as mybir
from concourse.bass_utils import run_bass_kernel_spmd

F32 = mybir.dt.float32
F32R = mybir.dt.float32r
BF16 = mybir.dt.bfloat16
AF = mybir.ActivationFunctionType
ALU = mybir.AluOpType

T = 4352
NT = 34
NCTX = 256
NEG_E = -float(np.exp(-0.5))


class Buf:
    def __init__(self, t):
        self.t = t
        self.w = None
        self.r = {}

    def __getitem__(self, k):
        return self.t[k]


class Sync:
    def __init__(self, nc, es, ndma=32):
        self.nc = nc
        self.engs = {"pe": nc.tensor, "act": nc.scalar, "dve": nc.vector, "pool": nc.gpsimd, "sp": nc.sync}
        self.sem = {}
        self.cnt = {}
        self.seen = {k: {} for k in self.engs}
        for k in self.engs:
            self.sem[k] = es.enter_context(nc.semaphore("s_" + k))
            self.cnt[k] = 0
        self.dsem = [es.enter_context(nc.semaphore("d_%d" % i)) for i in range(ndma)]
        self.ndma = 0
        self.dma_last = [None] * ndma
        self.qi = 0

    def _need(self, e, evs):
        eng = self.engs[e]
        seen = self.seen[e]
        for ev in evs:
            if ev is None:
                continue
            key, sem, val, src = ev
            if src == e and e == "pe":
                continue
            if seen.get(key, 0) >= val:
                continue
            eng.wait_ge(sem, val)
            seen[key] = val

    @staticmethod
    def _deps(reads, writes):
        evs = []
        for b in reads:
            evs.append(b.w)
        for b in writes:
            evs.append(b.w)
            evs.extend(b.r.values())
        return evs

    @staticmethod
    def _record(ev, reads, writes):
        for b in reads:
            b.r[ev[0]] = ev
        for b in writes:
            b.w = ev
            b.r = {}

    def op(self, e, fn, reads=(), writes=()):
        self._need(e, self._deps(reads, writes))
        ins = fn(self.engs[e])
        self.cnt[e] += 1
        ins.then_inc(self.sem[e], 1)
        self._record((e, self.sem[e], self.cnt[e], e), reads, writes)
        return ins

    def dma(self, out, in_, reads=(), writes=(), q=None, **kw):
        if q is None:
            q = ("sp", "pool")[self.qi % 2] if False else "sp"
            self.qi += 1
        evs = self._deps(reads, writes)
        k = self.ndma % len(self.dsem)
        evs.append(self.dma_last[k])
        self._need(q, evs)
        ins = self.engs[q].dma_start(out=out, in_=in_, **kw)
        val = 16 * (self.ndma // len(self.dsem) + 1)
        ins.then_inc(self.dsem[k], 16)
        ev = ("d%d" % k, self.dsem[k], val, "dma")
        self.dma_last[k] = ev
        self.ndma += 1
        self._record(ev, reads, writes)
        return ins

    def barrier(self):
        evs = [(k, self.sem[k], self.cnt[k], "x") for k in self.engs if self.cnt[k] > 0]
        evs += [ev for ev in self.dma_last if ev is not None]
        for e in self.engs:
            self._need(e, [ev for ev in evs if ev[0] != e])


class Pool:
    def __init__(self, nc, es, name, shape, dtype, n, psum=False):
        mk = nc.psum_tensor if psum else nc.sbuf_tensor
        self.bufs = [Buf(es.enter_context(mk("%s_%d" % (name, i), shape, dtype))) for i in range(n)]
        self.i = 0

    def get(self):
        b = self.bufs[self.i % len(self.bufs)]
        self.i += 1
        return b


def sb(nc, es, name, shape, dtype=F32):
    return Buf(es.enter_context(nc.sbuf_tensor(name, shape, dtype)))


def colify(v):
    v = np.asarray(v, np.float32).reshape(-1)
    if v.size < 128:
        v = np.concatenate([v, np.zeros(128 - v.size, np.float32)])
    return np.ascontiguousarray(v.reshape(-1, 128).T)


COLS = {}


def _layout_cols():
    off = 0
    for name, n in [("c", 16), ("ada_b", 48), ("norm_w", 16), ("conv_w", 12), ("mu", 13), ("k_k", 4), ("k_a", 4),
                    ("w0", 8), ("a0", 8), ("r_k", 4), ("lnx_w", 4), ("lnx_b", 4), ("q_a_norm", 3), ("kv_a_norm", 2),
                    ("gq_nope", 1), ("gq_rope", 1), ("gk_nope", 1), ("gk_rope", 1)]:
        COLS[name] = (off, n)
        off += n
    return off


NCOL = _layout_cols()


def build_cols(inp, b):
    parts = []
    cc = np.stack([colify(inp["c"][b]), colify(inp["c_ctx"])], axis=-1).reshape(128, 16)
    parts.append(cc)
    parts.append(np.concatenate([colify(inp["ada_b"][l]) for l in range(2)], axis=1))
    parts.append(np.concatenate([colify(inp["norm_w"][l]) for l in range(2)], axis=1))
    parts.append(np.concatenate([colify(inp["ev_conv_w"][0][t]) for t in range(3)], axis=1))
    parts.append(colify(inp["ev_mu"][0]))
    parts.append(colify(inp["ev_k_k"][0]))
    parts.append(colify(inp["ev_k_a"][0]))
    parts.append(np.concatenate([colify(inp["ev_w0"][0][d]) for d in range(2)], axis=1))
    parts.append(np.concatenate([colify(inp["ev_a0"][0][d]) for d in range(2)], axis=1))
    parts.append(colify(inp["ev_r_k"][0]))
    parts.append(colify(inp["ev_lnx_w"][0]))
    parts.append(colify(inp["ev_lnx_b"][0]))
    parts.append(colify(inp["od_q_a_norm"][0]))
    parts.append(colify(inp["od_kv_a_norm"][0]))
    parts.append(colify(inp["od_gq_nope"][0]))
    parts.append(colify(inp["od_gq_rope"][0]))
    parts.append(colify(inp["od_gk_nope"][0]))
    parts.append(colify(inp["od_gk_rope"][0]))
    out = np.concatenate(parts, axis=1).astype(np.float32)
    assert out.shape == (128, NCOL), out.shape
    return np.ascontiguousarray(out)


C_M2F, C_M2R, C_NF, C_NR, C_ID2 = 256, 768, 1280, 1536, 1792
C_PERM = 2048
C_BLK = 2112
NCONST = 2112 + 2048


def build_rope():
    t = np.arange(4096)
    pos = np.stack([(t // 64).astype(np.float32), (t % 64).astype(np.float32)], axis=0)
    inv = (np.float32(10000.0) ** (-np.arange(16, dtype=np.float32) / np.float32(16))).astype(np.float32)
    ang = (pos[:, None, :] * inv[None, :, None]).astype(np.float32)
    cos = np.cos(ang).astype(np.float32)
    sin = np.sin(ang).astype(np.float32)
    out = np.zeros((64, 2, 4096), np.float32)
    for ax in range(2):
        for half in range(2):
            r0 = ax * 32 + half * 16
            out[r0:r0 + 16, 0, :] = cos[ax]
            out[r0:r0 + 16, 1, :] = -sin[ax] if half == 0 else sin[ax]
    return out


def build_consts():
    p = np.arange(128)[:, None]
    f = np.arange(128)[None, :]
    c = np.zeros((128, NCONST), np.float32)
    c[:, 0:128] = (p == f)
    c[:, 128:256] = (p // 64 == f // 64)
    lt, le, gt, ge = (p < f), (p <= f), (p > f), (p >= f)
    c[:, C_M2F:C_M2F + 512] = np.concatenate([lt, le, lt, le], axis=1)
    c[:, C_M2R:C_M2R + 512] = np.concatenate([gt, ge, gt, ge], axis=1)
    c[:, C_NF:C_NF + 256] = np.concatenate([gt, gt], axis=1)
    c[:, C_NR:C_NR + 256] = np.concatenate([lt, lt], axis=1)
    c[:, C_ID2:C_ID2 + 256] = np.concatenate([p == f, p == f], axis=1)
    j = np.arange(64)
    partner = np.where((j % 32) < 16, j + 16, j - 16)
    pm = np.zeros((128, 64), np.float32)
    pm[partner, j] = 1.0
    c[:, C_PERM:C_PERM + 64] = pm
    bd = (p // 16 == f // 16)
    mk = [bd & (p < f)]
    nk = [bd & (p > f)]
    for s_ in (16, 32, 64):
        same = (p // (2 * s_) == f // (2 * s_))
        nk.append(same & (p % (2 * s_) >= s_) & (f % (2 * s_) < s_))
        mk.append(same & (f % (2 * s_) >= s_) & (p % (2 * s_) < s_))
    for i, m_ in enumerate(mk + nk):
        c[:, C_BLK + i * 256:C_BLK + (i + 1) * 256] = np.concatenate([m_, m_], axis=1)
    return c


def build(stage=99, dbg=False):
    nc = bass.Bass("TRN2", target_bir_lowering=False)
    dt_in = lambda n, s: nc.dram_tensor(n, s, F32, kind="ExternalInput").ap()
    x_in = dt_in("x", [4096, 1024])
    ctx_in = dt_in("ctx", [NCTX, 1024])
    cols_in = dt_in("cols", [128, NCOL])
    consts_in = dt_in("consts", [128, NCONST])
    ada_w = dt_in("ada_w", [2, 1024, 3072])
    ev_w_in = dt_in("ev_w_in", [1024, 4224])
    ev_lora = dt_in("ev_lora", [128, 2, 512])
    ev_w_out = dt_in("ev_w_out", [1024, 1024])
    od_w_in = dt_in("od_w_in", [1024, 1728])
    od_w_qb = dt_in("od_w_qb", [384, 1536])
    od_w_kvb = dt_in("od_w_kvb", [256, 2048])
    od_w_o = dt_in("od_w_o", [1024, 1024])
    rope_d = dt_in("rope", [64, 2, 4096])
    out_d = nc.dram_tensor("out", [4096, 1024], F32, kind="ExternalOutput").ap()
    outs = [Buf(out_d)]
    scr = {}
    for n in ["rT", "vT", "aT", "lw0", "lw1", "kd0", "kd1", "b0", "b1", "bonT", "szbT", "ys0", "ys1"]:
        kind = "ExternalOutput" if (dbg and n in dbg) else "Internal"
        scr[n] = Buf(nc.dram_tensor(n, [512, T], F32, kind=kind).ap())
    scr["mixT"] = Buf(nc.dram_tensor("mixT", [1024, T], BF16, kind="ExternalOutput" if (dbg and "mixT" in dbg) else "Internal").ap())
    scr["x1"] = Buf(nc.dram_tensor("x1", [T, 1024], F32, kind="ExternalOutput" if (dbg and "x1" in dbg) else "Internal").ap())
    for n, shp, dt_ in [("KTd", [1024, T], BF16), ("KRd", [64, T], BF16), ("Vd", [T, 1024], BF16), ("QNd", [384, 4096], BF16), ("SZd", [1024, 4096], F32)]:
        scr[n] = Buf(nc.dram_tensor(n, shp, dt_, kind="ExternalOutput" if (dbg and n in dbg) else "Internal").ap())
    if dbg and "hT" in dbg:
        scr["hT"] = Buf(nc.dram_tensor("hT", [1024, T], BF16, kind="ExternalOutput").ap())

    with ExitStack() as es0:
        S = Sync(nc, es0)
        colsb = sb(nc, es0, "colsb", [128, NCOL])
        cst = sb(nc, es0, "cst", [128, C_BLK])
        identb = sb(nc, es0, "identb", [128, 128], BF16)
        ones = sb(nc, es0, "ones", [128, 128])
        modT = sb(nc, es0, "modT", [128, 2, 24, 2])
        gcol = sb(nc, es0, "gcol", [128, 2, 8, 2])
        G = [sb(nc, es0, "G%d" % s, [128, 1024]) for s in range(2)]
        der = sb(nc, es0, "der", [128, 32])
        S.dma(colsb[:], cols_in, writes=[colsb])
        S.dma(cst[:], consts_in[:, 0:C_BLK], writes=[cst])
        S.op("dve", lambda e: e.tensor_copy(out=identb[:], in_=cst[:, 0:128]), reads=[cst], writes=[identb])
        bdb = sb(nc, es0, "bdb", [128, 128], BF16)
        S.op("dve", lambda e: e.tensor_copy(out=bdb[:], in_=cst[:, 128:256]), reads=[cst], writes=[bdb])
        S.op("pool", lambda e: e.memset(ones[:], 1.0), writes=[ones])
        ident = cst.t[:, 0:128]
        bdm = cst.t[:, 128:256]

        def col(name, i=0, n=1):
            o, _ = COLS[name]
            return colsb.t[:, o + i:o + i + n]

        mo, _ = COLS["mu"]
        S.op("dve", lambda e: e.tensor_scalar(out=der[:, 0:13], in0=colsb[:, mo:mo + 13], scalar1=-1.0, scalar2=1.0, op0=ALU.mult, op1=ALU.add), reads=[colsb], writes=[der])
        S.op("dve", lambda e: e.tensor_scalar(out=der[:, 13:26], in0=colsb[:, mo:mo + 13], scalar1=0.5, scalar2=None, op0=ALU.mult), reads=[colsb], writes=[der])
        ko, _ = COLS["k_a"]
        S.op("dve", lambda e: e.tensor_scalar(out=der[:, 26:30], in0=colsb[:, ko:ko + 4], scalar1=-1.0, scalar2=1.0, op0=ALU.mult, op1=ALU.add), reads=[colsb], writes=[der])

        with ExitStack() as es:
            sc = sb(nc, es, "sc", [128, 16])
            wpool = Pool(nc, es, "adaw", [128, 8, 512], F32, 2)
            pp = Pool(nc, es, "p0ps", [128, 512], F32, 4, psum=True)
            co, _ = COLS["c"]
            S.op("act", lambda e: e.activation(out=sc[:], in_=colsb[:, co:co + 16], func=AF.Silu), reads=[colsb], writes=[sc])
            abo, _ = COLS["ada_b"]
            for l in range(2):
                wv = ada_w[l].rearrange("(k p) n -> p k n", p=128)
                for piece in range(6):
                    wst = wpool.get()
                    S.dma(wst[:], wv[:, :, piece * 512:(piece + 1) * 512], writes=[wst], q=("sp", "pool")[piece % 2])
                    for dc in range(4):
                        ps = pp.get()
                        for k in range(8):
                            S.op("pe", lambda e: e.matmul(ps[:, 0:2], lhsT=wst[:, k, dc * 128:(dc + 1) * 128], rhs=sc[:, 2 * k:2 * k + 2], start=(k == 0), stop=(k == 7)), reads=[wst, sc], writes=[ps])
                        ch = piece * 4 + dc
                        S.op("dve", lambda e: e.tensor_scalar(out=modT[:, l, ch, :], in0=ps[:, 0:2], scalar1=colsb[:, abo + l * 24 + ch:abo + l * 24 + ch + 1], scalar2=None, op0=ALU.add), reads=[ps, colsb], writes=[modT])
                nwo, _ = COLS["norm_w"]
                S.op("dve", lambda e: e.tensor_scalar(out=gcol[:, l, :, :], in0=modT[:, l, 8:16, :], scalar1=1.0, scalar2=None, op0=ALU.add), reads=[modT], writes=[gcol])
                for s in range(2):
                    S.op("dve", lambda e: e.tensor_tensor(out=gcol[:, l, :, s], in0=gcol[:, l, :, s], in1=colsb[:, nwo + l * 8:nwo + l * 8 + 8], op=ALU.mult), reads=[gcol, colsb], writes=[gcol])
            S.barrier()

        def make_gate_tiles(l):
            with ExitStack() as es:
                gb = Pool(nc, es, "gb%d" % l, [128, 128], F32, 2)
                pp = Pool(nc, es, "gps%d" % l, [128, 512], F32, 2, psum=True)
                for s in range(2):
                    for k in range(8):
                        g = gb.get()
                        S.op("dve", lambda e: e.tensor_scalar(out=g[:], in0=ones[:], scalar1=modT[:, l, 16 + k, s:s + 1], scalar2=None, op0=ALU.mult), reads=[ones, modT], writes=[g])
                        ps = pp.get()
                        S.op("pe", lambda e: e.matmul(ps[:, 0:128], lhsT=g[:], rhs=ident, start=True, stop=True), reads=[g, cst], writes=[ps])
                        S.op("act", lambda e: e.activation(out=G[s][:, k * 128:(k + 1) * 128], in_=ps[:, 0:128], func=AF.Identity), reads=[ps], writes=[G[s]])
                S.barrier()

        def norm_phase(es, l, src_rows):
            hT = es.enter_context(nc.sbuf_tensor("hT%d" % l, [128, 8, T], BF16))
            hTb = [Buf(hT) for _ in range(NT)]
            with ExitStack() as e2:
                xpool = Pool(nc, e2, "xt%d" % l, [128, 1024], F32, 6)
                jpool = Pool(nc, e2, "junk%d" % l, [128, 1024], BF16, 4)
                xnpool = Pool(nc, e2, "xn%d" % l, [128, 1024], BF16, 4)
                sspool = Pool(nc, e2, "ss%d" % l, [128, 1], F32, 8)
                pst = Pool(nc, e2, "pst%d" % l, [128, 8, 128], BF16, 5, psum=True)
                def ntile(i):
                        s = 1 if i < 2 else 0
                        xt = xpool.get()
                        S.dma(xt[:], src_rows(i), writes=[xt], q=("sp", "pool")[i % 2])
                        junk = jpool.get()
                        ss = sspool.get()
                        S.op("act", lambda e: e.activation(out=junk[:], in_=xt[:], func=AF.Square, accum_out=ss[:]), reads=[xt], writes=[junk, ss])
                        yield
                        S.op("act", lambda e: e.activation(out=ss[:], in_=ss[:], func=AF.Sqrt, scale=1.0 / 1024, bias=1e-6), reads=[ss], writes=[ss])
                        yield
                        S.op("dve", lambda e: e.reciprocal(out=ss[:], in_=ss[:]), reads=[ss], writes=[ss])
                        yield
                        xn = xnpool.get()
                        S.op("dve", lambda e: e.tensor_scalar(out=xn[:], in0=xt[:], scalar1=ss[:, 0:1], scalar2=None, op0=ALU.mult), reads=[xt, ss], writes=[xn])
                        yield
                        ps = pst.get()
                        for k in range(8):
                            S.op("pe", lambda e: e.transpose(ps[:, k, :], xn[:, k * 128:(k + 1) * 128], identb[:]), reads=[xn, identb], writes=[ps])
                        for k in range(8):
                            dst = hT[:, k, i * 128:(i + 1) * 128]
                            gsc = gcol[:, l, k, s:s + 1]
                            shf = modT[:, l, k, s:s + 1]
                            if k % 2:
                                S.op("act", lambda e: e.activation(out=dst, in_=ps[:, k, :], func=AF.Identity, scale=gsc, bias=shf), reads=[ps, gcol, modT], writes=[hTb[i]])
                                yield
                            else:
                                S.op("dve", lambda e: e.tensor_scalar(out=dst, in0=ps[:, k, :], scalar1=gsc, scalar2=shf, op0=ALU.mult, op1=ALU.add), reads=[ps, gcol, modT], writes=[hTb[i]])
                                yield
                run_interleaved([ntile(i) for i in range(NT)], 4 if l == 0 else 4)
                S.barrier()
            return hT, hTb

        def src0(i):
            return ctx_in[i * 128:(i + 1) * 128, :] if i < 2 else x_in[(i - 2) * 128:(i - 1) * 128, :]

        make_gate_tiles(0)
        with ExitStack() as esL0:
            hT, hTb = norm_phase(esL0, 0, src0)
            if dbg and "hT" in dbg:
                for k in range(8):
                    S.dma(scr["hT"][k * 128:(k + 1) * 128, :], hT[:, k, :], reads=hTb, writes=[scr["hT"]])
            if stage >= 2:
                layer0_proj(nc, S, esL0, hT, hTb, ev_w_in, ev_lora, colsb, der, cst, scr, bdb)
        S.barrier()
        if stage >= 3:
            layer0_scan(nc, S, cst, scr, ones, consts_in, identb)
            S.barrier()
        if stage >= 4:
            layer0_out(nc, S, colsb, cst, scr, ev_w_out, G, src0, bdb)
            S.barrier()
        if stage >= 5:
            make_gate_tiles(1)
            with ExitStack() as esL1:
                hT1, hT1b = norm_phase(esL1, 1, lambda i: scr["x1"][i * 128:(i + 1) * 128, :])
                if dbg and "hT" in dbg:
                    for k in range(8):
                        S.dma(scr["hT"][k * 128:(k + 1) * 128, :], hT1[:, k, :], reads=hT1b, writes=[scr["hT"]])
                layer1_kv(nc, S, esL1, hT1, hT1b, od_w_in, od_w_kvb, colsb, cst, scr, rope_d)
            S.barrier()
        if stage >= 6:
            layer1_attn(nc, S, scr, od_w_qb, od_w_o, colsb, cst, G, rope_d, out_d, outs[0])
            S.barrier()
        S.barrier()
    return nc


def token_blocks():
    blks = [(0, NCTX, 0, NCTX)]
    s = NCTX
    while s < T:
        e = min(s + 510, T)
        blks.append((NCTX, T, s, e))
        s = e
    return blks


def layer0_proj(nc, S, es, hT, hTb, w_in_d, lora_d, colsb, der, cst, scr, bdb):
    def col(name, i=0, n=1):
        o, _ = COLS[name]
        return colsb.t[:, o + i:o + i + n]

    bdm = cst.t[:, 128:256]
    W = es.enter_context(nc.sbuf_tensor("W0", [128, 8, 4224], BF16))
    Wb = Buf(W)
    lora = Buf(es.enter_context(nc.sbuf_tensor("lora", [128, 2, 512], BF16)))
    with ExitStack() as e2:
        stg = Pool(nc, e2, "wstg", [128, 8, 384], F32, 2)
        wv = w_in_d.rearrange("(k p) n -> p k n", p=128)
        for pc in range(11):
            st = stg.get()
            S.dma(st[:], wv[:, :, pc * 384:(pc + 1) * 384], writes=[st], q=("sp", "pool")[pc % 2])
            for k in range(8):
                eng = ("dve", "act", "pool")[k % 3] if False else ("dve", "act")[k % 2]
                if eng == "act":
                    S.op("act", lambda e: e.activation(out=W[:, k, pc * 384:(pc + 1) * 384], in_=st[:, k, :], func=AF.Identity), reads=[st], writes=[Wb])
                else:
                    S.op("dve", lambda e: e.tensor_copy(out=W[:, k, pc * 384:(pc + 1) * 384], in_=st[:, k, :]), reads=[st], writes=[Wb])
        st = stg.get()
        S.dma(st[:, 0:3, :].rearrange("p a b -> p (a b)")[:, 0:1024], lora_d.rearrange("p a b -> p (a b)"), writes=[st])
        S.op("dve", lambda e: e.tensor_copy(out=lora[:].rearrange("p a b -> p (a b)"), in_=st[:, 0:3, :].rearrange("p a b -> p (a b)")[:, 0:1024]), reads=[st], writes=[lora])
        S.barrier()

    with ExitStack() as e2:
        pp = Pool(nc, e2, "pps", [128, 512], F32, 7, psum=True)
        f32p = Pool(nc, e2, "wk", [128, 514], F32, 10)
        outp = Pool(nc, e2, "wo", [128, 512], F32, 5)
        rpool = Pool(nc, e2, "rp", [128, 512], F32, 2)
        kpool = Pool(nc, e2, "kp", [128, 512], F32, 2)
        vpool = Pool(nc, e2, "vp", [128, 512], F32, 2)
        apool = Pool(nc, e2, "ap", [128, 512], F32, 1)
        outb = Pool(nc, e2, "wob", [128, 512], BF16, 2)
        bfp = Pool(nc, e2, "wbf", [128, 512], BF16, 3)
        dq = [0]

        def store(dst, rows, o0, o1, tile, n_out):
            q = ("sp", "pool")[dq[0] % 2]
            dq[0] += 1
            S.dma(dst[rows, o0:o1], tile[:, 0:n_out], reads=[tile], writes=[dst], q=q)

        for (ss, se, o0, o1) in token_blocks():
            cs = max(ss, o0 - 1)
            ce = min(se, o1 + 1)
            n = ce - cs
            n_out = o1 - o0
            oc = o0 - cs + 1
            tiles_rd = [hTb[i] for i in range(cs // 128, (ce - 1) // 128 + 1)]

            def mm(chunk):
                ps = pp.get()
                for k in range(8):
                    S.op("pe", lambda e: e.matmul(ps[:, 0:n], lhsT=W[:, k, chunk * 128:(chunk + 1) * 128], rhs=hT[:, k, cs:ce], start=(k == 0), stop=(k == 7)), reads=[Wb] + tiles_rd, writes=[ps])
                return ps

            def padded(ps, eng="act"):
                t = f32p.get()
                if cs == o0:
                    S.op("pool", lambda e: e.memset(t[:, 0:1], 0.0), writes=[t])
                if ce == o1:
                    S.op("pool", lambda e: e.memset(t[:, n + 1:n + 2], 0.0), writes=[t])
                if eng == "act":
                    S.op("act", lambda e: e.activation(out=t[:, 1:n + 1], in_=ps[:, 0:n], func=AF.Identity), reads=[ps], writes=[t])
                else:
                    S.op("dve", lambda e: e.tensor_copy(out=t[:, 1:n + 1], in_=ps[:, 0:n]), reads=[ps], writes=[t])
                return t

            def tshift(t, mi, eng="dve", dst=None):
                s2 = outp.get()
                o = (dst or outp).get()
                e_ = eng
                S.op(e_, lambda e: e.tensor_tensor(out=s2[:, 0:n_out], in0=t[:, oc - 1:oc - 1 + n_out], in1=t[:, oc + 1:oc + 1 + n_out], op=ALU.add), reads=[t], writes=[s2])
                S.op("act", lambda e: e.activation(out=o[:, 0:n_out], in_=t[:, oc:oc + n_out], func=AF.Copy, scale=der[:, mi:mi + 1]), reads=[t, der], writes=[o])
                S.op("dve", lambda e: e.scalar_tensor_tensor(out=o[:, 0:n_out], in0=s2[:, 0:n_out], scalar=der[:, 13 + mi:14 + mi], in1=o[:, 0:n_out], op0=ALU.mult, op1=ALU.add), reads=[s2, o, der], writes=[o])
                return o

            for q in range(4):
                pu = mm(q)
                pgc = mm(8 + q)
                u_sb = f32p.get()
                S.op("act", lambda e: e.activation(out=u_sb[:, 0:n], in_=pu[:, 0:n], func=AF.Identity), reads=[pu], writes=[u_sb])
                cu = f32p.get()
                if cs == o0:
                    S.op("pool", lambda e: e.memset(cu[:, 0:1], 0.0), writes=[cu])
                if ce == o1:
                    S.op("pool", lambda e: e.memset(cu[:, n + 1:n + 2], 0.0), writes=[cu])
                S.op("dve", lambda e: e.tensor_tensor(out=cu[:, 1:n + 1], in0=pgc[:, 0:n], in1=u_sb[:, 0:n], op=ALU.mult), reads=[pgc, u_sb], writes=[cu])
                y = outp.get()
                S.op("act", lambda e: e.activation(out=y[:, 0:n_out], in_=cu[:, oc:oc + n_out], func=AF.Copy, scale=col("conv_w", 4 + q)), reads=[cu, colsb], writes=[y])
                S.op("dve", lambda e: e.scalar_tensor_tensor(out=y[:, 0:n_out], in0=cu[:, oc - 1:oc - 1 + n_out], scalar=col("conv_w", q), in1=y[:, 0:n_out], op0=ALU.mult, op1=ALU.add), reads=[cu, y, colsb], writes=[y])
                S.op("dve", lambda e: e.scalar_tensor_tensor(out=y[:, 0:n_out], in0=cu[:, oc + 1:oc + 1 + n_out], scalar=col("conv_w", 8 + q), in1=y[:, 0:n_out], op0=ALU.mult, op1=ALU.add), reads=[cu, y, colsb], writes=[y])
                pgb = mm(4 + q)
                pz = mm(12 + q)
                sz = outp.get()
                S.op("act", lambda e: e.activation(out=sz[:, 0:n_out], in_=pz[:, oc - 1:oc - 1 + n_out], func=AF.Silu), reads=[pz], writes=[sz])
                S.op("dve", lambda e: e.tensor_tensor(out=sz[:, 0:n_out], in0=pgb[:, oc - 1:oc - 1 + n_out], in1=sz[:, 0:n_out], op=ALU.mult), reads=[pgb, sz], writes=[sz])
                mo_ = outb.get()
                S.op("dve", lambda e: e.tensor_tensor(out=mo_[:, 0:n_out], in0=sz[:, 0:n_out], in1=y[:, 0:n_out], op=ALU.mult), reads=[sz, y], writes=[mo_])
                store(scr["mixT"], slice(q * 128, (q + 1) * 128), o0, o1, mo_, n_out)

            pwa = mm(28)
            twa = padded(pwa)
            wa = tshift(twa, 12)
            lin = outb.get()
            S.op("act", lambda e: e.activation(out=lin[0:64, 0:n_out], in_=wa[0:64, 0:n_out], func=AF.Tanh), reads=[wa], writes=[lin])
            S.op("dve", lambda e: e.tensor_copy(out=lin[64:128, 0:n_out], in_=wa[64:128, 0:n_out]), reads=[wa], writes=[lin])
            for q in range(4):
                rows = slice(q * 128, (q + 1) * 128)
                pr_ = padded(mm(16 + q))
                pk_ = padded(mm(20 + q), "dve")
                pv_ = padded(mm(24 + q))
                pzb = mm(29 + q)
                r_ = tshift(pr_, q, "dve", rpool)
                store(scr["rT"], rows, o0, o1, r_, n_out)
                k_ = tshift(pk_, 4 + q, "pool", kpool)
                v_ = tshift(pv_, 8 + q, "dve", vpool)
                store(scr["vT"], rows, o0, o1, v_, n_out)
                szb = outp.get()
                S.op("act", lambda e: e.activation(out=szb[:, 0:n_out], in_=pzb[:, oc - 1:oc - 1 + n_out], func=AF.Silu), reads=[pzb], writes=[szb])
                store(scr["szbT"], rows, o0, o1, szb, n_out)
                kk = f32p.get()
                S.op("dve", lambda e: e.tensor_scalar(out=kk[:, 0:n_out], in0=k_[:, 0:n_out], scalar1=col("k_k", q), scalar2=None, op0=ALU.mult), reads=[k_, colsb], writes=[kk])
                sq = bfp.get()
                S.op("act", lambda e: e.activation(out=sq[:, 0:n_out], in_=kk[:, 0:n_out], func=AF.Square), reads=[kk], writes=[sq])
                pss = pp.get()
                S.op("pe", lambda e: e.matmul(pss[:, 0:n_out], lhsT=bdb[:], rhs=sq[:, 0:n_out], start=True, stop=True), reads=[bdb, sq], writes=[pss])
                rn = f32p.get()
                S.op("act", lambda e: e.activation(out=rn[:, 0:n_out], in_=pss[:, 0:n_out], func=AF.Ln, bias=1e-24, scale=1.0), reads=[pss], writes=[rn])
                S.op("act", lambda e: e.activation(out=rn[:, 0:n_out], in_=rn[:, 0:n_out], func=AF.Exp, scale=-0.5), reads=[rn], writes=[rn])
                a_ = apool.get()
                S.op("dve", lambda e: e.scalar_tensor_tensor(out=a_[:, 0:n_out], in0=kk[:, 0:n_out], scalar=-1.0, in1=rn[:, 0:n_out], op0=ALU.mult, op1=ALU.mult), reads=[kk, rn], writes=[a_])
                store(scr["aT"], rows, o0, o1, a_, n_out)
                pbon = pp.get()
                for d in range(2):
                    plw = pp.get()
                    S.op("pe", lambda e: e.matmul(plw[:, 0:n_out], lhsT=lora[0:64, d, q * 128:(q + 1) * 128], rhs=lin[0:64, 0:n_out], start=True, stop=True), reads=[lora, lin], writes=[plw])
                    pla = pp.get()
                    S.op("pe", lambda e: e.matmul(pla[:, 0:n_out], lhsT=lora[64:128, d, q * 128:(q + 1) * 128], rhs=lin[64:128, 0:n_out], start=True, stop=True), reads=[lora, lin], writes=[pla])
                    lw = outp.get()
                    S.op("act", lambda e: e.activation(out=lw[:, 0:n_out], in_=plw[:, 0:n_out], func=AF.Sigmoid, bias=col("w0", d * 4 + q), scale=1.0), reads=[plw, colsb], writes=[lw])
                    S.op("act", lambda e: e.activation(out=lw[:, 0:n_out], in_=lw[:, 0:n_out], func=AF.Copy, scale=NEG_E), reads=[lw], writes=[lw])
                    store(scr["lw%d" % d], rows, o0, o1, lw, n_out)
                    ic = f32p.get()
                    S.op("act", lambda e: e.activation(out=ic[:, 0:n_out], in_=pla[:, 0:n_out], func=AF.Sigmoid, bias=col("a0", d * 4 + q), scale=1.0), reads=[pla, colsb], writes=[ic])
                    kf = f32p.get()
                    S.op("dve", lambda e: e.tensor_scalar(out=kf[:, 0:n_out], in0=ic[:, 0:n_out], scalar1=col("k_a", q), scalar2=der[:, 26 + q:27 + q], op0=ALU.mult, op1=ALU.add), reads=[ic, colsb, der], writes=[kf])
                    kd = outp.get()
                    S.op("dve", lambda e: e.tensor_tensor(out=kd[:, 0:n_out], in0=kf[:, 0:n_out], in1=k_[:, 0:n_out], op=ALU.mult), reads=[kf, k_], writes=[kd])
                    store(scr["kd%d" % d], rows, o0, o1, kd, n_out)
                    b_ = outp.get()
                    S.op("dve", lambda e: e.scalar_tensor_tensor(out=b_[:, 0:n_out], in0=a_[:, 0:n_out], scalar=-1.0, in1=ic[:, 0:n_out], op0=ALU.mult, op1=ALU.mult), reads=[a_, ic], writes=[b_])
                    store(scr["b%d" % d], rows, o0, o1, b_, n_out)
                    rk = bfp.get()
                    S.op("dve", lambda e: e.scalar_tensor_tensor(out=rk[:, 0:n_out], in0=kd[:, 0:n_out], scalar=col("r_k", q), in1=r_[:, 0:n_out], op0=ALU.mult, op1=ALU.mult), reads=[kd, r_, colsb], writes=[rk])
                    S.op("pe", lambda e: e.matmul(pbon[:, 0:n_out], lhsT=bdb[:], rhs=rk[:, 0:n_out], start=(d == 0), stop=(d == 1)), reads=[bdb, rk], writes=[pbon])
                bon = outp.get()
                S.op("dve", lambda e: e.tensor_tensor(out=bon[:, 0:n_out], in0=pbon[:, 0:n_out], in1=v_[:, 0:n_out], op=ALU.mult), reads=[pbon, v_], writes=[bon])
                store(scr["bonT"], rows, o0, o1, bon, n_out)
        S.barrier()


def layer0_scan(nc, S, cst, scr, ones, consts_in, identb):
    import os as _os2
    ident64 = identb.t[0:64, 0:64]
    IDT = BF16
    with ExitStack() as es:
        blk = sb(nc, es, "blkmask", [128, 2048])
        S.dma(blk[:], consts_in[:, C_BLK:C_BLK + 2048], writes=[blk])
        SP = []
        for sl_ in range(4):
            def mk(nm, shape, dt_, n):
                return Pool(nc, es, "s%d_%s" % (sl_, nm), shape, dt_, n)
            SP.append(dict(
                yop=mk("yo", [64, 2, 128], F32, 1),
                ldp={n: mk("ld_" + n, [64, 2, 128], F32, 1) for n in ("r", "a", "v", "lw", "k", "b")},
                wk=mk("wk", [64, 2, 128], F32, 7),
                arp=mk("ar", [64, 2, 256], BF16, 1),
                etp=mk("et", [64, 2], F32, 1),
                ttp=mk("tt", [128, 512], BF16, 1),
                mmp=mk("mm", [128, 256], BF16, 1),
                mkp=mk("mk", [128, 512], BF16, 1),
                mnp=mk("mn", [128, 512], BF16, 2),
                xp=mk("x", [128, 256], BF16, 1),
                mbp=mk("mb", [128, 3, 256], BF16, 1),
                nbp=mk("nb", [128, 4, 256], BF16, 1),
                dp=mk("d", [128, 512], BF16, 2),
                wkr=mk("wkr", [64, 2, 128], BF16, 4),
                vbp=mk("vb", [64, 2, 128], BF16, 1),
                w0p=mk("w0", [128, 128], BF16, 1),
                ahp=mk("ah", [64, 256], BF16, 1),
                up=mk("u", [128, 128], BF16, 1),
                hpp=mk("hp", [64, 256], F32, 1),
            ))
        zero_h = sb(nc, es, "zeroh", [64, 2, 64])
        Hs = {(d, g): [sb(nc, es, "H%d%d%d" % (d, g, i), [64, 2, 64]) for i in range(2)] for d in range(2) for g in range(4)}
        banks = [Buf(es.enter_context(nc.psum_tensor("sps%d" % i, [128, 512], F32))) for i in range(8)]
        PSLOT = []
        for sl_ in range(4):
            A_, B_ = banks[2 * sl_:2 * sl_ + 2]
            PSLOT.append((A_, B_, A_, B_, A_, A_, A_, B_, B_, B_, A_))
        S.op("pool", lambda e: e.memset(zero_h[:], 0.0), writes=[zero_h])
        Hr = {}
        Hrs = {k: [Buf(es.enter_context(nc.sbuf_tensor("Hr%d%d%d" % (k[0], k[1], i), [64, 2, 64], BF16))) for i in range(2)] for k in Hs}
        for (d, g), hh in Hs.items():
            S.op("pool", lambda e: e.memset(hh[0][:], 0.0), writes=[hh[0]])
            Hr[(d, g)] = Hrs[(d, g)][0]
            S.op("act", lambda e: e.activation(out=Hr[(d, g)][:], in_=zero_h[:], func=AF.Identity), reads=[zero_h], writes=[Hr[(d, g)]])
        hcur = {k: 0 for k in Hs}
        orders = [list(range(NT)), [1, 0] + list(range(NT - 1, 1, -1))]
        dq = [0]

        def load(n, g, c):
            t = ldp[n].get()
            q = "sp"
            dq[0] += 1
            src = scr[n][g * 128:(g + 1) * 128, c * 128:(c + 1) * 128].rearrange("(h j) t -> j h t", h=2)
            S.dma(t[:], src, reads=[scr[n]], writes=[t], q=q)
            return t

        import os as _os
        SL = float(_os.environ.get('SCAN_STEP', 99))
        FINE = _os.environ.get('SCAN_FINE', '1') == '1'
        def unit(d, c, g, slot):
            psT, ps1, ps2, ps3, psW, psU, psMN, psXU, psAH, psY, psH = PSLOT[slot]
            P_ = SP[slot]
            yop, ldp, wk, arp, etp, ttp, mmp, mkp, mnp, xp = (P_[k_] for k_ in ("yop", "ldp", "wk", "arp", "etp", "ttp", "mmp", "mkp", "mnp", "xp"))
            mbp, nbp, dp, wkr, vbp, w0p, ahp, up, hpp = (P_[k_] for k_ in ("mbp", "nbp", "dp", "wkr", "vbp", "w0p", "ahp", "up", "hpp"))
            m2 = cst.t[:, (C_M2F if d == 0 else C_M2R):(C_M2F if d == 0 else C_M2R) + 512]
            mN = cst.t[:, (C_NF if d == 0 else C_NR):(C_NF if d == 0 else C_NR) + 256]
            id2 = cst.t[:, C_ID2:C_ID2 + 256]
            names = {"r": "rT", "a": "aT", "v": "vT", "lw": "lw%d" % d, "k": "kd%d" % d, "b": "b%d" % d}
            L = {}
            for n, dn in names.items():
                t = ldp[n].get()
                q = "sp"
                dq[0] += 1
                src = scr[dn][g * 128:(g + 1) * 128, c * 128:(c + 1) * 128].rearrange("(h j) t -> j h t", h=2)
                S.dma(t[:], src, reads=[scr[dn]], writes=[t], q=q)
                L[n] = t
            r2, a2, v2, lw2, k2, b2 = L["r"], L["a"], L["v"], L["lw"], L["k"], L["b"]
            if SL < 1:
                return
            yield True
            cum = wk.get()
            for h in range(2):
                S.op("dve", lambda e: e.tensor_tensor_scan(out=cum[:, h, :], data0=ones[0:64, 0:128], data1=lw2[:, h, :], initial=0.0, op0=ALU.mult, op1=ALU.add), reads=[ones, lw2], writes=[cum])
                if FINE: yield True
            if d == 1:
                tmp = wk.get()
                S.op("pool", lambda e: e.tensor_tensor(out=tmp[:], in0=cum[:], in1=lw2[:], op=ALU.subtract), reads=[cum, lw2], writes=[tmp])
                if FINE: yield True
                cr = wk.get()
                for h in range(2):
                    S.op("dve", lambda e: e.tensor_scalar(out=cr[:, h, :], in0=tmp[:, h, :], scalar1=-1.0, scalar2=cum[:, h, 127:128], op0=ALU.mult, op1=ALU.add), reads=[tmp, cum], writes=[cr])
                    if FINE: yield True
                cum = cr
                lastc = 0
            else:
                lastc = 127
            if SL < 2:
                return
            ep = wk.get()
            S.op("act", lambda e: e.activation(out=ep[:], in_=cum[:], func=AF.Exp), reads=[cum], writes=[ep])
            if FINE: yield True
            en = wk.get()
            S.op("act", lambda e: e.activation(out=en[:], in_=cum[:], func=AF.Exp, scale=-1.0), reads=[cum], writes=[en])
            if FINE: yield True
            et = etp.get()
            S.op("act", lambda e: e.activation(out=et[:], in_=cum[:, :, lastc], func=AF.Exp), reads=[cum], writes=[et])
            if FINE: yield True
            eh = wk.get()
            for h in range(2):
                S.op("act", lambda e: e.activation(out=eh[:, h, :], in_=cum[:, h, :], func=AF.Exp, scale=-1.0, bias=cum[:, h, lastc:lastc + 1]), reads=[cum], writes=[eh])
                if FINE: yield True
            if SL < 3:
                return
            AR = arp.get()
            if d == 0:
                S.op("pool", lambda e: e.tensor_tensor(out=AR[:, :, 1:128], in0=a2[:, :, 1:128], in1=ep[:, :, 0:127], op=ALU.mult), reads=[a2, ep], writes=[AR])
                if FINE: yield True
                S.op("act", lambda e: e.activation(out=AR[:, :, 0:1], in_=a2[:, :, 0:1], func=AF.Copy), reads=[a2], writes=[AR])
                if FINE: yield True
            else:
                S.op("pool", lambda e: e.tensor_tensor(out=AR[:, :, 0:127], in0=a2[:, :, 0:127], in1=ep[:, :, 1:128], op=ALU.mult), reads=[a2, ep], writes=[AR])
                if FINE: yield True
                S.op("act", lambda e: e.activation(out=AR[:, :, 127:128], in_=a2[:, :, 127:128], func=AF.Copy), reads=[a2], writes=[AR])
                if FINE: yield True
            S.op("pool", lambda e: e.tensor_tensor(out=AR[:, :, 128:256], in0=r2[:], in1=ep[:], op=ALU.mult), reads=[r2, ep], writes=[AR])
            if FINE: yield True
            Bt = wkr.get()
            S.op("dve", lambda e: e.tensor_tensor(out=Bt[:], in0=b2[:], in1=en[:], op=ALU.mult), reads=[b2, en], writes=[Bt])
            if FINE: yield True
            Kt = wkr.get()
            S.op("pool", lambda e: e.tensor_tensor(out=Kt[:], in0=k2[:], in1=en[:], op=ALU.mult), reads=[k2, en], writes=[Kt])
            if FINE: yield True
            Bh = wkr.get()
            S.op("dve", lambda e: e.tensor_tensor(out=Bh[:], in0=b2[:], in1=eh[:], op=ALU.mult), reads=[b2, eh], writes=[Bh])
            if FINE: yield True
            Kh = wkr.get()
            S.op("pool", lambda e: e.tensor_tensor(out=Kh[:], in0=k2[:], in1=eh[:], op=ALU.mult), reads=[k2, eh], writes=[Kh])
            if FINE: yield True
            if SL < 4:
                return
            yield True
            vb = vbp.get()
            S.op("act", lambda e: e.activation(out=vb[:], in_=v2[:], func=AF.Copy), reads=[v2], writes=[vb])
            if FINE: yield True
            for wi, (src_t, sl) in enumerate([(AR, slice(0, 128)), (Bh, slice(0, 128)), (Kh, slice(0, 128)), (vb, slice(0, 128))]):
                for h in range(2):
                    o_ = (wi * 2 + h) * 64
                    S.op("pe", lambda e: e.transpose(psT[:].bitcast(BF16)[:, o_:o_ + 64], src_t[:, h, sl], ident64), reads=[src_t, identb], writes=[psT])
            TT = ttp.get()
            S.op("act", lambda e: e.activation(out=TT[:], in_=psT[:].bitcast(BF16)[:, 0:512], func=AF.Identity), reads=[psT], writes=[TT])
            if FINE: yield True
            AtT = lambda h: TT[:, h * 64:(h + 1) * 64]
            BhT = lambda h: TT[:, (2 + h) * 64:(3 + h) * 64]
            KhT = lambda h: TT[:, (4 + h) * 64:(5 + h) * 64]
            Vt = lambda h: TT[:, (6 + h) * 64:(7 + h) * 64]
            if SL < 5:
                return
            yield True
            v3 = lambda ap: ap.rearrange("p (h x) -> p h x", h=2)
            MKm = lambda typ: blk.t[:, (typ ^ d) * 1024:(typ ^ d) * 1024 + 1024]
            for h in range(2):
                S.op("pe", lambda e: e.matmul(ps1[:, h * 256:(h + 1) * 256], lhsT=Bt[:, h, :], rhs=AR[:, h, :], start=True, stop=True), reads=[Bt, AR], writes=[ps1])
            Mb = mbp.get()
            S.op("dve", lambda e: e.tensor_tensor(out=Mb[:].rearrange("p w (h x) -> p w h x", h=2), in0=v3(ps1[:])[:, :, 0:128].unsqueeze(1).to_broadcast([128, 3, 2, 128]), in1=MKm(0)[:, 0:768].rearrange("p (w h x) -> p w h x", w=3, h=2), op=ALU.mult), reads=[ps1, blk], writes=[Mb])
            if FINE: yield True
            Mm = mmp.get()
            S.op("dve", lambda e: e.tensor_tensor(out=v3(Mm[:]), in0=v3(ps1[:])[:, :, 128:256], in1=v3(m2)[:, :, 128:256], op=ALU.mult), reads=[ps1, cst], writes=[Mm])
            yield True
            for h in range(2):
                S.op("pe", lambda e: e.matmul(ps2[:, h * 256:(h + 1) * 256], lhsT=Kt[:, h, :], rhs=AR[:, h, :], start=True, stop=True), reads=[Kt, AR], writes=[ps2])
            Mk = mkp.get()
            S.op("dve", lambda e: e.tensor_tensor(out=Mk[:], in0=ps2[:], in1=m2, op=ALU.mult), reads=[ps2, cst], writes=[Mk])
            yield True
            for h in range(2):
                S.op("pe", lambda e: e.matmul(ps3[:, h * 128:(h + 1) * 128], lhsT=AR[:, h, 0:128], rhs=Bt[:, h, :], start=True, stop=True), reads=[AR, Bt], writes=[ps3])
            Nb = nbp.get()
            S.op("dve", lambda e: e.tensor_tensor(out=Nb[:].rearrange("p w (h x) -> p w h x", h=2), in0=v3(ps3[:, 0:256]).unsqueeze(1).to_broadcast([128, 4, 2, 128]), in1=MKm(1).rearrange("p (w h x) -> p w h x", w=4, h=2), op=ALU.mult), reads=[ps3, blk], writes=[Nb])
            if FINE: yield True
            if SL < 6:
                return
            yield True
            v4 = lambda ap: ap.rearrange("p (h m x) -> p h m x", h=2, m=2)
            D = dp.get()
            S.op("pool", lambda e: e.tensor_tensor(out=v4(D[:])[:, :, 0, :], in0=v3(Mb[:, 0, :]), in1=v3(id2), op=ALU.add), reads=[Mb, cst], writes=[D])
            if FINE: yield True
            S.op("pool", lambda e: e.tensor_tensor(out=v4(D[:])[:, :, 1, :], in0=v3(Nb[:, 0, :]), in1=v3(id2), op=ALU.add), reads=[Nb, cst], writes=[D])
            if FINE: yield True
            Db = D
            yield True
            Mp = lambda h: Mb[:, 0, h * 128:(h + 1) * 128]
            Np = lambda h: Nb[:, 0, h * 128:(h + 1) * 128]
            mpb, npb = Mb, Nb

            def d_update(nparts):
                nonlocal D, Db
                Dn_ = dp.get()
                S.op("dve", lambda e: e.tensor_tensor(out=Dn_[:], in0=psXU[:], in1=D[:], op=ALU.add), reads=[psXU, D], writes=[Dn_])
                D = Db = Dn_

            for lev in range(1, 4):
                for h in range(2):
                    S.op("pe", lambda e: e.matmul(psMN[:, h * 256:h * 256 + 128], lhsT=Np(h), rhs=Mp(h), start=True, stop=True), reads=[mpb, npb], writes=[psMN])
                    S.op("pe", lambda e: e.matmul(psMN[:, h * 256 + 128:h * 256 + 256], lhsT=Mp(h), rhs=Np(h), start=True, stop=True), reads=[mpb, npb], writes=[psMN])
                MN = mnp.get()
                S.op("act", lambda e: e.activation(out=MN[:], in_=psMN[:], func=AF.Identity), reads=[psMN], writes=[MN])
                if _os.environ.get("SCAN_Y1", "1") == "1":
                    yield True
                Mp = (lambda MN: (lambda h: MN[:, h * 256:h * 256 + 128]))(MN)
                Np = (lambda MN: (lambda h: MN[:, h * 256 + 128:h * 256 + 256]))(MN)
                mpb = npb = MN
                for h in range(2):
                    S.op("pe", lambda e: e.matmul(psXU[:, h * 256:h * 256 + 128], lhsT=Np(h), rhs=Db[:, h * 256:h * 256 + 128], start=True, stop=True), reads=[MN, Db], writes=[psXU])
                    S.op("pe", lambda e: e.matmul(psXU[:, h * 256 + 128:h * 256 + 256], lhsT=Mp(h), rhs=Db[:, h * 256 + 128:h * 256 + 256], start=True, stop=True), reads=[MN, Db], writes=[psXU])
                d_update(2)
                yield True
            for mi in range(3):
                last = (mi == 2)
                for h in range(2):
                    if not last:
                        S.op("pe", lambda e: e.matmul(psMN[:, h * 256:h * 256 + 128], lhsT=Mb[:, 1 + mi, h * 128:(h + 1) * 128], rhs=Db[:, h * 256 + 128:h * 256 + 256], start=True, stop=True), reads=[Mb, Db], writes=[psMN])
                    S.op("pe", lambda e: e.matmul(psMN[:, h * 256 + 128:h * 256 + 256], lhsT=Nb[:, 1 + mi, h * 128:(h + 1) * 128], rhs=Db[:, h * 256:h * 256 + 128], start=True, stop=True), reads=[Nb, Db], writes=[psMN])
                Z = mnp.get()
                if not last:
                    S.op("act", lambda e: e.activation(out=Z[:], in_=psMN[:], func=AF.Identity), reads=[psMN], writes=[Z])
                    if FINE: yield True
                else:
                    S.op("act", lambda e: e.activation(out=v4(Z[:])[:, :, 1, :], in_=v4(psMN[:])[:, :, 1, :], func=AF.Identity), reads=[psMN], writes=[Z])
                yield True
                for h in range(2):
                    S.op("pe", lambda e: e.matmul(psXU[:, h * 256:h * 256 + 128], lhsT=Db[:, h * 256 + 128:h * 256 + 256], rhs=Z[:, h * 256 + 128:h * 256 + 256], start=True, stop=True), reads=[Db, Z], writes=[psXU])
                    if not last:
                        S.op("pe", lambda e: e.matmul(psXU[:, h * 256 + 128:h * 256 + 256], lhsT=Db[:, h * 256:h * 256 + 128], rhs=Z[:, h * 256:h * 256 + 128], start=True, stop=True), reads=[Db, Z], writes=[psXU])
                if not last:
                    d_update(2)
                else:
                    X = xp.get()
                    S.op("dve", lambda e: e.tensor_tensor(out=v3(X[:]), in0=v4(psXU[:])[:, :, 0, :], in1=v4(D[:])[:, :, 0, :], op=ALU.add), reads=[psXU, D], writes=[X])
                yield True
            yield True
            for h in range(2):
                S.op("pe", lambda e: e.matmul(psW[:, 256 + h * 64:256 + (h + 1) * 64], lhsT=Mk[:, h * 256:h * 256 + 128], rhs=Vt(h), start=True, stop=True), reads=[Mk, TT], writes=[psW])
            W0 = w0p.get()
            S.op("act", lambda e: e.activation(out=W0[:], in_=psW[:, 256:384], func=AF.Identity), reads=[psW], writes=[W0])
            yield True
            for h in range(2):
                S.op("pe", lambda e: e.matmul(psAH[0:64, 256 + h * 128:256 + (h + 1) * 128], lhsT=AtT(h), rhs=X[:, h * 128:(h + 1) * 128], start=True, stop=True), reads=[TT, X], writes=[psAH])
            AH = ahp.get()
            S.op("act", lambda e: e.activation(out=AH[:], in_=psAH[0:64, 256:512], func=AF.Identity), reads=[psAH], writes=[AH])
            if FINE: yield True
            if SL < 9:
                return
            yield True
            Hc = Hs[(d, g)][hcur[(d, g)]]
            Hn = Hs[(d, g)][1 - hcur[(d, g)]]
            Hcr = Hr[(d, g)]
            for h in range(2):
                S.op("pe", lambda e: e.matmul(psU[:, 384 + h * 64:384 + (h + 1) * 64], lhsT=X[:, h * 128:(h + 1) * 128], rhs=W0[:, h * 64:(h + 1) * 64], start=True, stop=False), reads=[X, W0], writes=[psU])
                S.op("pe", lambda e: e.matmul(psU[:, 384 + h * 64:384 + (h + 1) * 64], lhsT=AH[:, h * 128:(h + 1) * 128], rhs=Hcr[:, h, :], start=False, stop=True), reads=[AH, Hcr], writes=[psU])
            U = up.get()
            S.op("dve", lambda e: e.tensor_copy(out=U[:], in_=psU[:, 384:512]), reads=[psU], writes=[U])
            if FINE: yield True
            if SL < 10:
                return
            yield True
            for h in range(2):
                yo = psY[0:64, h * 128:(h + 1) * 128]
                S.op("pe", lambda e: e.matmul(yo, lhsT=Hcr[:, h, :], rhs=AR[:, h, 128:256], start=True, stop=False), reads=[Hcr, AR], writes=[psY])
                S.op("pe", lambda e: e.matmul(yo, lhsT=U[:, h * 64:(h + 1) * 64], rhs=Mm[:, h * 128:(h + 1) * 128], start=False, stop=False), reads=[U, Mm], writes=[psY])
                S.op("pe", lambda e: e.matmul(yo, lhsT=Vt(h), rhs=Mk[:, h * 256 + 128:h * 256 + 256], start=False, stop=True), reads=[TT, Mk], writes=[psY])
            if SL < 10.5:
                return
            yo_ = yop.get()
            S.op("act", lambda e: e.activation(out=yo_[:], in_=psY[0:64, 0:256].rearrange("p (h t) -> p h t", h=2), func=AF.Identity), reads=[psY], writes=[yo_])
            if FINE: yield True
            if SL < 10.7:
                return
            for h in range(2):
                q = "sp"
                dq[0] += 1
                S.dma(scr["ys%d" % d][g * 128 + h * 64:g * 128 + (h + 1) * 64, c * 128:(c + 1) * 128], yo_[:, h, :], reads=[yo_], writes=[scr["ys%d" % d]], q=q)
            if SL < 11:
                return
            yield True
            for h in range(2):
                ho = psH[0:64, 256 + h * 128:256 + (h + 1) * 128]
                S.op("pe", lambda e: e.matmul(ho, lhsT=BhT(h), rhs=U[:, 0:128], start=True, stop=False), reads=[TT, U], writes=[psH])
                S.op("pe", lambda e: e.matmul(ho, lhsT=KhT(h), rhs=TT[:, 384:512], start=False, stop=True), reads=[TT], writes=[psH])
            if SL < 11.5:
                return
            hps = hpp.get()
            S.op("act", lambda e: e.activation(out=hps[:], in_=psH[0:64, 256:512], func=AF.Identity), reads=[psH], writes=[hps])
            if FINE: yield True
            for h in range(2):
                S.op("dve", lambda e: e.scalar_tensor_tensor(out=Hn[:, h, :], in0=Hc[:, h, :], scalar=et[:, h:h + 1], in1=hps[:, h * 192:h * 192 + 64], op0=ALU.mult, op1=ALU.add), reads=[Hc, et, hps], writes=[Hn])
                if FINE: yield True
            hcur[(d, g)] = 1 - hcur[(d, g)]
            hr_ = Hrs[(d, g)][hcur[(d, g)]]
            S.op("act", lambda e: e.activation(out=hr_[:], in_=Hn[:], func=AF.Identity), reads=[Hn], writes=[hr_])
            if FINE: yield True
            Hr[(d, g)] = hr_
            yield True

        from collections import deque
        NCI = int(_os.environ.get('SCAN_NCI', NT))
        todo = deque()
        for ci in range(NCI):
            for d in range(int(_os.environ.get('SCAN_ND', 2))):
                for g in range(int(_os.environ.get('SCAN_NG', 4))):
                    todo.append((d, orders[d][ci], g))
        NSLOT = len(PSLOT)
        active = [None] * NSLOT
        rnd = 0
        STAG = int(_os.environ.get('SCAN_STAG', 22))
        while todo or any(a is not None for a in active):
            rnd += 1
            for k in range(NSLOT):
                if active[k] is None and todo and rnd > k * STAG:
                    d_, c_, g_ = todo.popleft()
                    active[k] = unit(d_, c_, g_, k)
                if active[k] is not None:
                    try:
                        next(active[k])
                    except StopIteration:
                        active[k] = None
        S.barrier()


def run_interleaved(gens, width):
    gens = list(gens)
    active = []
    while gens or active:
        while gens and len(active) < width:
            active.append(gens.pop(0))
        for g_ in list(active):
            try:
                next(g_)
            except StopIteration:
                active.remove(g_)


def out_blocks():
    return [(0, NCTX)] + [(NCTX + 512 * i, NCTX + 512 * (i + 1)) for i in range(8)]


def load_wbf16(nc, S, es, name, w_d, kchunks, ncols, piece=512):
    W = Buf(es.enter_context(nc.sbuf_tensor(name, [128, kchunks, ncols], BF16)))
    wv = w_d.rearrange("(k p) n -> p k n", p=128)
    with ExitStack() as e2:
        stg = Pool(nc, e2, name + "_stg", [128, kchunks, piece], F32, 2)
        i = 0
        for c0 in range(0, ncols, piece):
            c1 = min(ncols, c0 + piece)
            st = stg.get()
            S.dma(st[:, :, 0:c1 - c0], wv[:, :, c0:c1], writes=[st], q=("sp", "pool")[i % 2])
            for k in range(kchunks):
                if (i + k) % 2:
                    S.op("act", lambda e: e.activation(out=W[:, k, c0:c1], in_=st[:, k, 0:c1 - c0], func=AF.Identity), reads=[st], writes=[W])
                else:
                    S.op("dve", lambda e: e.tensor_copy(out=W[:, k, c0:c1], in_=st[:, k, 0:c1 - c0]), reads=[st], writes=[W])
            i += 1
        S.barrier()
    return W


def out_proj_gen(nc, S, pools, Wo, mixblk, G, t0, t1, x_src, x_dst, dst_buf):
    xpool, tpool, pp = pools
    for j in range((t1 - t0) // 128):
        tok = t0 + j * 128
        s = 1 if tok < NCTX else 0
        xt = xpool.get()
        S.dma(xt[:], x_src(tok), writes=[xt], q=("sp", "pool")[j % 2])
        for nh in range(2):
            ps = pp.get()
            for f in range(8):
                S.op("pe", lambda e: e.matmul(ps[:], lhsT=mixblk[:, f, j * 128:(j + 1) * 128], rhs=Wo[:, f, nh * 512:(nh + 1) * 512], start=(f == 0), stop=(f == 7)), reads=[mixblk, Wo], writes=[ps])
            tmp = tpool.get()
            S.op("dve", lambda e: e.tensor_tensor(out=tmp[:], in0=ps[:], in1=G[s][:, nh * 512:(nh + 1) * 512], op=ALU.mult), reads=[ps, G[s]], writes=[tmp])
            S.op("pool", lambda e: e.tensor_tensor(out=xt[:, nh * 512:(nh + 1) * 512], in0=tmp[:], in1=xt[:, nh * 512:(nh + 1) * 512], op=ALU.add), reads=[tmp, xt], writes=[xt])
        dst = x_dst(tok)
        if dst is not None:
            S.dma(dst, xt[:], reads=[xt], writes=[dst_buf], q=("sp", "pool")[(j + 1) % 2])
        yield


def out_proj_block(*a):
    for _ in out_proj_gen(*a):
        pass


def layer0_out(nc, S, colsb, cst, scr, w_out_d, G, src0, bdb):
    def col(name, i=0, n=1):
        o, _ = COLS[name]
        return colsb.t[:, o + i:o + i + n]

    bdm = cst.t[:, 128:256]
    with ExitStack() as es:
        Wo = load_wbf16(nc, S, es, "Wo0", w_out_d, 8, 1024)
        mixp = Pool(nc, es, "mixblk", [128, 8, 512], BF16, 2)
        ldp = Pool(nc, es, "o_ld", [128, 512], F32, 16)
        wkp = Pool(nc, es, "o_wk", [128, 512], F32, 12)
        sqbp = Pool(nc, es, "o_sqb", [128, 512], BF16, 6)
        xpool = Pool(nc, es, "o_x", [128, 1024], F32, 3)
        tpool = Pool(nc, es, "o_t", [128, 512], F32, 2)
        pp = Pool(nc, es, "o_ps", [128, 512], F32, 8, psum=True)
        dq = [0]

        def ld(name, g, t0, t1):
            t = ldp.get()
            q = ("sp", "pool")[dq[0] % 2]
            dq[0] += 1
            S.dma(t[:, 0:t1 - t0], scr[name][g * 128:(g + 1) * 128, t0:t1], reads=[scr[name]], writes=[t], q=q)
            return t

        for (t0, t1) in out_blocks():
            n = t1 - t0
            mixblk = mixp.get()
            S.dma(mixblk[:, 0:4, 0:n], scr["mixT"][0:512, t0:t1].rearrange("(k p) t -> p k t", p=128), reads=[scr["mixT"]], writes=[mixblk])
            def gchain(g, mixblk=mixblk, t0=t0, t1=t1, n=n):
                y0 = ld("ys0", g, t0, t1)
                y1 = ld("ys1", g, t0, t1)
                bon = ld("bonT", g, t0, t1)
                szb = ld("szbT", g, t0, t1)
                ysum = wkp.get()
                S.op("pool", lambda e: e.tensor_tensor(out=ysum[:, 0:n], in0=y0[:, 0:n], in1=y1[:, 0:n], op=ALU.add), reads=[y0, y1], writes=[ysum])
                pm = pp.get()
                S.op("pe", lambda e: e.matmul(pm[:, 0:n], lhsT=bdm, rhs=ysum[:, 0:n], start=True, stop=True), reads=[cst, ysum], writes=[pm])
                yield
                xc = wkp.get()
                S.op("dve", lambda e: e.scalar_tensor_tensor(out=xc[:, 0:n], in0=pm[:, 0:n], scalar=-1.0 / 64, in1=ysum[:, 0:n], op0=ALU.mult, op1=ALU.add), reads=[pm, ysum], writes=[xc])
                yield
                sq = sqbp.get()
                S.op("act", lambda e: e.activation(out=sq[:, 0:n], in_=xc[:, 0:n], func=AF.Square), reads=[xc], writes=[sq])
                yield
                pv = pp.get()
                S.op("pe", lambda e: e.matmul(pv[:, 0:n], lhsT=bdb[:], rhs=sq[:, 0:n], start=True, stop=True), reads=[bdb, sq], writes=[pv])
                yield
                rs = wkp.get()
                S.op("act", lambda e: e.activation(out=rs[:, 0:n], in_=pv[:, 0:n], func=AF.Ln, scale=1.0 / 64, bias=64e-5), reads=[pv], writes=[rs])
                yield
                S.op("act", lambda e: e.activation(out=rs[:, 0:n], in_=rs[:, 0:n], func=AF.Exp, scale=-0.5), reads=[rs], writes=[rs])
                yield
                S.op("dve", lambda e: e.tensor_tensor(out=xc[:, 0:n], in0=xc[:, 0:n], in1=rs[:, 0:n], op=ALU.mult), reads=[xc, rs], writes=[xc])
                yield
                S.op("act", lambda e: e.activation(out=xc[:, 0:n], in_=xc[:, 0:n], func=AF.Identity, scale=col("lnx_w", g), bias=col("lnx_b", g)), reads=[xc, colsb], writes=[xc])
                yield
                S.op("pool", lambda e: e.tensor_tensor(out=xc[:, 0:n], in0=xc[:, 0:n], in1=bon[:, 0:n], op=ALU.add), reads=[xc, bon], writes=[xc])
                S.op("dve", lambda e: e.tensor_tensor(out=mixblk[:, 4 + g, 0:n], in0=xc[:, 0:n], in1=szb[:, 0:n], op=ALU.mult), reads=[xc, szb], writes=[mixblk])
                yield

            run_interleaved([gchain(g) for g in range(4)], 4)
            out_proj_block(nc, S, (xpool, tpool, pp), Wo, mixblk, G, t0, t1,
                           lambda tok: src0(tok // 128), lambda tok: scr["x1"][tok:tok + 128, :], scr["x1"])
        S.barrier()


def rope_apply(nc, S, pools, kr, n, p0, rope_d, permb):
    ropep, misc, wk, krp = pools
    rt = ropep.get()
    S.dma(rt[:, :, 0:n], rope_d[:, :, p0:p0 + n], writes=[rt])
    pp_ = misc.get()
    S.op("pe", lambda e: e.matmul(pp_[0:64, 0:n], lhsT=permb[0:64, 0:64], rhs=kr[0:64, 0:n], start=True, stop=True), reads=[permb, kr], writes=[pp_])
    t1_ = wk.get()
    S.op("dve", lambda e: e.tensor_tensor(out=t1_[0:64, 0:n], in0=pp_[0:64, 0:n], in1=rt[:, 1, 0:n], op=ALU.mult), reads=[pp_, rt], writes=[t1_])
    t2_ = wk.get()
    S.op("pool", lambda e: e.tensor_tensor(out=t2_[0:64, 0:n], in0=kr[0:64, 0:n], in1=rt[:, 0, 0:n], op=ALU.mult), reads=[kr, rt], writes=[t2_])
    kro = krp.get()
    S.op("dve", lambda e: e.tensor_tensor(out=kro[0:64, 0:n], in0=t1_[0:64, 0:n], in1=t2_[0:64, 0:n], op=ALU.add), reads=[t1_, t2_], writes=[kro])
    return kro


def layer1_kv(nc, S, es, hT, hTb, w_in_d, w_kvb_d, colsb, cst, scr, rope_d):
    def col(name, i=0, n=1):
        o, _ = COLS[name]
        return colsb.t[:, o + i:o + i + n]

    W = load_wbf16(nc, S, es, "W1", w_in_d, 8, 1728, piece=432)
    Wkvb = load_wbf16(nc, S, es, "Wkvb", w_kvb_d, 2, 2048, piece=1024)
    with ExitStack() as e2:
        onesb = sb(nc, e2, "onesb1", [128, 128], BF16)
        permb = sb(nc, e2, "permb1", [64, 64], BF16)
        S.op("pool", lambda e: e.memset(onesb[:], 1.0), writes=[onesb])
        S.op("dve", lambda e: e.tensor_copy(out=permb[:], in_=cst[0:64, C_PERM:C_PERM + 64]), reads=[cst], writes=[permb])
        pp = Pool(nc, e2, "kv_ps", [128, 512], F32, 7, psum=True)
        sqp = Pool(nc, e2, "kv_sq", [128, 512], BF16, 5)
        wk = Pool(nc, e2, "kv_wk", [128, 512], F32, 7)
        kvnp = Pool(nc, e2, "kv_kvn", [128, 2, 512], BF16, 2)
        ktp = Pool(nc, e2, "kv_kt", [128, 512], BF16, 4)
        vtp = Pool(nc, e2, "kv_vt", [128, 512], BF16, 3)
        krp = Pool(nc, e2, "kv_kr", [64, 512], BF16, 3)
        qnp = Pool(nc, e2, "kv_qn", [128, 3, 512], BF16, 2)
        szp = Pool(nc, e2, "kv_sz", [128, 512], F32, 3)
        ropep = Pool(nc, e2, "kv_rope", [64, 2, 512], F32, 2)
        dq = [0]

        def st(dst_buf, dst_ap, src_ap, src_buf):
            q = ("sp", "pool")[dq[0] % 2]
            dq[0] += 1
            S.dma(dst_ap, src_ap, reads=[src_buf], writes=[dst_buf], q=q)

        def rms(pss, rows, n, count):
            rs = wk.get()
            S.op("act", lambda e: e.activation(out=rs[0:rows, 0:n], in_=pss[0:rows, 0:n], func=AF.Ln, scale=1.0 / count, bias=1e-6), reads=[pss], writes=[rs])
            S.op("act", lambda e: e.activation(out=rs[0:rows, 0:n], in_=rs[0:rows, 0:n], func=AF.Exp, scale=-0.5), reads=[rs], writes=[rs])
            return rs

        for (t0, t1) in out_blocks():
            n = t1 - t0
            tiles_rd = hTb[t0 // 128:t1 // 128]

            def mm(c0, c1):
                ps = pp.get()
                for k in range(8):
                    S.op("pe", lambda e: e.matmul(ps[0:c1 - c0, 0:n], lhsT=W[:, k, c0:c1], rhs=hT[:, k, t0:t1], start=(k == 0), stop=(k == 7)), reads=[W] + tiles_rd, writes=[ps])
                return ps

            def sumsq(plist, rows):
                pss = pp.get()
                for i, p_ in enumerate(plist):
                    sq = sqp.get()
                    S.op("act", lambda e: e.activation(out=sq[0:rows, 0:n], in_=p_[0:rows, 0:n], func=AF.Square), reads=[p_], writes=[sq])
                    S.op("pe", lambda e: e.matmul(pss[0:rows, 0:n], lhsT=onesb[0:rows, 0:rows], rhs=sq[0:rows, 0:n], start=(i == 0), stop=(i == len(plist) - 1)), reads=[onesb, sq], writes=[pss])
                return pss

            pkv = [mm(384 + 128 * i, 384 + 128 * (i + 1)) for i in range(2)]
            rs = rms(sumsq(pkv, 128), 128, n, 256)
            kvn = kvnp.get()
            for i in range(2):
                S.op("dve", lambda e: e.scalar_tensor_tensor(out=kvn[:, i, 0:n], in0=pkv[i][:, 0:n], scalar=col("kv_a_norm", i), in1=rs[:, 0:n], op0=ALU.mult, op1=ALU.mult), reads=[pkv[i], rs, colsb], writes=[kvn])
            def kchain(h, kvn=kvn, n=n, t0=t0, t1=t1):
                pk = pp.get()
                for i in range(2):
                    S.op("pe", lambda e: e.matmul(pk[:, 0:n], lhsT=Wkvb[:, i, h * 256:h * 256 + 128], rhs=kvn[:, i, 0:n], start=(i == 0), stop=(i == 1)), reads=[Wkvb, kvn], writes=[pk])
                yield
                sq = sqp.get()
                S.op("act", lambda e: e.activation(out=sq[:, 0:n], in_=pk[:, 0:n], func=AF.Square), reads=[pk], writes=[sq])
                yield
                pss = pp.get()
                S.op("pe", lambda e: e.matmul(pss[:, 0:n], lhsT=onesb[:], rhs=sq[:, 0:n], start=True, stop=True), reads=[onesb, sq], writes=[pss])
                yield
                rs = wk.get()
                S.op("act", lambda e: e.activation(out=rs[:, 0:n], in_=pss[:, 0:n], func=AF.Ln, scale=1.0 / 128, bias=1e-6), reads=[pss], writes=[rs])
                yield
                S.op("act", lambda e: e.activation(out=rs[:, 0:n], in_=rs[:, 0:n], func=AF.Exp, scale=-0.5), reads=[rs], writes=[rs])
                yield
                kt = ktp.get()
                S.op("dve", lambda e: e.scalar_tensor_tensor(out=kt[:, 0:n], in0=pk[:, 0:n], scalar=col("gk_nope"), in1=rs[:, 0:n], op0=ALU.mult, op1=ALU.mult), reads=[pk, rs, colsb], writes=[kt])
                st(scr["KTd"], scr["KTd"][h * 128:(h + 1) * 128, t0:t1], kt[:, 0:n], kt)
                yield

            run_interleaved([kchain(h) for h in range(8)], 3)
            for j in range(n // 128):
                for vh in range(2):
                    pv = pp.get()
                    for i in range(2):
                        rhs = Wkvb[:, i, :].rearrange("p (h x) -> p h x", h=8)[:, vh * 4:(vh + 1) * 4, 128:256]
                        S.op("pe", lambda e: e.matmul(pv[:].rearrange("p (h x) -> p h x", h=4), lhsT=kvn[:, i, j * 128:(j + 1) * 128], rhs=rhs, start=(i == 0), stop=(i == 1)), reads=[Wkvb, kvn], writes=[pv])
                    vt = vtp.get()
                    S.op("act", lambda e: e.activation(out=vt[:], in_=pv[:], func=AF.Identity), reads=[pv], writes=[vt])
                    st(scr["Vd"], scr["Vd"][t0 + j * 128:t0 + (j + 1) * 128, vh * 512:(vh + 1) * 512], vt[:], vt)
            pr = mm(640, 704)
            rs = rms(sumsq([pr], 64), 64, n, 64)
            kr = krp.get()
            S.op("dve", lambda e: e.scalar_tensor_tensor(out=kr[0:64, 0:n], in0=pr[0:64, 0:n], scalar=col("gk_rope")[0:64, :], in1=rs[0:64, 0:n], op0=ALU.mult, op1=ALU.mult), reads=[pr, rs, colsb], writes=[kr])
            if t0 >= NCTX:
                kr = rope_apply(nc, S, (ropep, pp, wk, krp), kr, n, t0 - NCTX, rope_d, permb)
            st(scr["KRd"], scr["KRd"][0:64, t0:t1], kr[0:64, 0:n], kr)
            if t0 >= NCTX:
                l0 = t0 - NCTX
                pq = [mm(128 * i, 128 * (i + 1)) for i in range(3)]
                rs = rms(sumsq(pq, 128), 128, n, 384)
                qn = qnp.get()
                for i in range(3):
                    S.op("dve", lambda e: e.scalar_tensor_tensor(out=qn[:, i, 0:n], in0=pq[i][:, 0:n], scalar=col("q_a_norm", i), in1=rs[:, 0:n], op0=ALU.mult, op1=ALU.mult), reads=[pq[i], rs, colsb], writes=[qn])
                    st(scr["QNd"], scr["QNd"][i * 128:(i + 1) * 128, l0:l0 + n], qn[:, i, 0:n], qn)
                for c in range(8):
                    pz = mm(704 + 128 * c, 704 + 128 * (c + 1))
                    sz = szp.get()
                    S.op("act", lambda e: e.activation(out=sz[:, 0:n], in_=pz[:, 0:n], func=AF.Silu), reads=[pz], writes=[sz])
                    st(scr["SZd"], scr["SZd"][c * 128:(c + 1) * 128, l0:l0 + n], sz[:, 0:n], sz)
        S.barrier()


SM_SCALE = 192.0 ** -0.5


def layer1_attn(nc, S, scr, w_qb_d, w_o_d, colsb, cst, G, rope_d, out_d, out_buf):
    with ExitStack() as es:
        Wqb = load_wbf16(nc, S, es, "Wqb", w_qb_d, 3, 1536, piece=768)
        Wo = load_wbf16(nc, S, es, "Wo1", w_o_d, 8, 1024)
        KR = sb(nc, es, "KR", [128, T], BF16)
        S.op("pool", lambda e: e.memset(KR[64:128, :], 0.0), writes=[KR])
        S.dma(KR[0:64, :], scr["KRd"][:, :], reads=[scr["KRd"]], writes=[KR])
        gq = sb(nc, es, "gq", [128, 2])
        go, _ = COLS["gq_nope"]
        S.op("dve", lambda e: e.tensor_scalar(out=gq[:], in0=colsb[:, go:go + 2], scalar1=SM_SCALE, scalar2=None, op0=ALU.mult), reads=[colsb], writes=[gq])
        onesb = sb(nc, es, "onesb2", [128, 128], BF16)
        permb = sb(nc, es, "permb2", [64, 64], BF16)
        S.op("pool", lambda e: e.memset(onesb[:], 1.0), writes=[onesb])
        S.op("dve", lambda e: e.tensor_copy(out=permb[:], in_=cst[0:64, C_PERM:C_PERM + 64]), reads=[cst], writes=[permb])
        qnp = Pool(nc, es, "at_qn", [128, 3, 512], BF16, 2)
        szp = Pool(nc, es, "at_sz", [128, 8, 512], F32, 2)
        mixp = Pool(nc, es, "at_mix", [128, 8, 512], BF16, 2)
        kthp = Pool(nc, es, "at_kt", [128, T], BF16, 2)
        vhp = Pool(nc, es, "at_v", [128, NT, 128], BF16, 2)
        qntp = Pool(nc, es, "at_QN", [128, 512], BF16, 2)
        krp = Pool(nc, es, "at_QR", [128, 512], BF16, 4)
        for b_ in krp.bufs:
            S.op("pool", lambda e: e.memset(b_[64:128, :], 0.0), writes=[b_])
        ptp = Pool(nc, es, "at_PT", [128, 512], BF16, 6)
        sqp = Pool(nc, es, "at_sq", [128, 512], BF16, 3)
        wk = Pool(nc, es, "at_wk", [128, 512], F32, 8)
        ropep = Pool(nc, es, "at_rope", [64, 2, 512], F32, 2)
        accp = Pool(nc, es, "at_acc", [128, 512], F32, 4)
        xpool = Pool(nc, es, "at_x", [128, 1024], F32, 3)
        tpool = Pool(nc, es, "at_t", [128, 512], F32, 2)
        pS = Pool(nc, es, "at_pS", [128, 512], F32, 3, psum=True)
        pOp = Pool(nc, es, "at_pO", [128, 512], F32, 2, psum=True)
        pRp = Pool(nc, es, "at_pR", [128, 512], F32, 1, psum=True)
        misc = Pool(nc, es, "at_pm", [128, 512], F32, 2, psum=True)

        def rms(pss, rows, count):
            rs = wk.get()
            S.op("act", lambda e: e.activation(out=rs[0:rows, :], in_=pss[0:rows, :], func=AF.Sqrt, scale=1.0 / count, bias=1e-6), reads=[pss], writes=[rs])
            S.op("dve", lambda e: e.reciprocal(out=rs[0:rows, :], in_=rs[0:rows, :]), reads=[rs], writes=[rs])
            return rs

        def sumsq(p_, rows):
            sq = sqp.get()
            S.op("act", lambda e: e.activation(out=sq[0:rows, :], in_=p_[0:rows, :], func=AF.Square), reads=[p_], writes=[sq])
            pss = misc.get()
            S.op("pe", lambda e: e.matmul(pss[0:rows, :], lhsT=onesb[0:rows, 0:rows], rhs=sq[0:rows, :], start=True, stop=True), reads=[onesb, sq], writes=[pss])
            return pss

        import os as _os
        NQB = int(_os.environ.get("ATT_NQB", 8))
        blocks = {}

        def block_setup(qb):
            q0 = qb * 512
            qn = qnp.get()
            for i in range(3):
                S.dma(qn[:, i, :], scr["QNd"][i * 128:(i + 1) * 128, q0:q0 + 512], reads=[scr["QNd"]], writes=[qn], q=("sp", "pool")[i % 2])
            SZ = szp.get()
            S.dma(SZ[:], scr["SZd"][:, q0:q0 + 512].rearrange("(k p) t -> p k t", p=128), reads=[scr["SZd"]], writes=[SZ])
            mixblk = mixp.get()
            blocks[qb] = (qn, SZ, mixblk)

        def prep(qb, h):
            if h == 0:
                block_setup(qb)
            qn = blocks[qb][0]
            q0 = qb * 512
            kth = kthp.get()
            S.dma(kth[:], scr["KTd"][h * 128:(h + 1) * 128, :], reads=[scr["KTd"]], writes=[kth], q="sp")
            vh = vhp.get()
            S.dma(vh[:], scr["Vd"][:, h * 128:(h + 1) * 128].rearrange("(kt p) d -> p kt d", p=128), reads=[scr["Vd"]], writes=[vh], q="pool")
            yield None
            pqn = misc.get()
            for kc in range(3):
                S.op("pe", lambda e: e.matmul(pqn[:], lhsT=Wqb[:, kc, h * 192:h * 192 + 128], rhs=qn[:, kc, :], start=(kc == 0), stop=(kc == 2)), reads=[Wqb, qn], writes=[pqn])
            yield None
            sq = sqp.get()
            S.op("act", lambda e: e.activation(out=sq[:], in_=pqn[:], func=AF.Square), reads=[pqn], writes=[sq])
            yield None
            pss = misc.get()
            S.op("pe", lambda e: e.matmul(pss[:], lhsT=onesb[:], rhs=sq[:], start=True, stop=True), reads=[onesb, sq], writes=[pss])
            yield None
            rs = wk.get()
            S.op("act", lambda e: e.activation(out=rs[:], in_=pss[:], func=AF.Ln, scale=1.0 / 128, bias=1e-6), reads=[pss], writes=[rs])
            S.op("act", lambda e: e.activation(out=rs[:], in_=rs[:], func=AF.Exp, scale=-0.5), reads=[rs], writes=[rs])
            yield None
            QN = qntp.get()
            S.op("dve", lambda e: e.scalar_tensor_tensor(out=QN[:], in0=pqn[:], scalar=gq[:, 0:1], in1=rs[:], op0=ALU.mult, op1=ALU.mult), reads=[pqn, rs, gq], writes=[QN])
            yield None
            pqr = misc.get()
            for kc in range(3):
                S.op("pe", lambda e: e.matmul(pqr[0:64, :], lhsT=Wqb[:, kc, h * 192 + 128:h * 192 + 192], rhs=qn[:, kc, :], start=(kc == 0), stop=(kc == 2)), reads=[Wqb, qn], writes=[pqr])
            yield None
            sq2 = sqp.get()
            S.op("act", lambda e: e.activation(out=sq2[0:64, :], in_=pqr[0:64, :], func=AF.Square), reads=[pqr], writes=[sq2])
            yield None
            pss2 = misc.get()
            S.op("pe", lambda e: e.matmul(pss2[0:64, :], lhsT=onesb[0:64, 0:64], rhs=sq2[0:64, :], start=True, stop=True), reads=[onesb, sq2], writes=[pss2])
            yield None
            rs2 = wk.get()
            S.op("act", lambda e: e.activation(out=rs2[0:64, :], in_=pss2[0:64, :], func=AF.Ln, scale=1.0 / 64, bias=1e-6), reads=[pss2], writes=[rs2])
            S.op("act", lambda e: e.activation(out=rs2[0:64, :], in_=rs2[0:64, :], func=AF.Exp, scale=-0.5), reads=[rs2], writes=[rs2])
            yield None
            qr0 = krp.get()
            S.op("dve", lambda e: e.scalar_tensor_tensor(out=qr0[0:64, :], in0=pqr[0:64, :], scalar=gq[0:64, 1:2], in1=rs2[0:64, :], op0=ALU.mult, op1=ALU.mult), reads=[pqr, rs2, gq], writes=[qr0])
            yield None
            rt = ropep.get()
            S.dma(rt[:], rope_d[:, :, q0:q0 + 512], writes=[rt])
            pp_ = misc.get()
            S.op("pe", lambda e: e.matmul(pp_[0:64, :], lhsT=permb[0:64, 0:64], rhs=qr0[0:64, :], start=True, stop=True), reads=[permb, qr0], writes=[pp_])
            yield None
            t1_ = wk.get()
            S.op("dve", lambda e: e.tensor_tensor(out=t1_[0:64, :], in0=pp_[0:64, :], in1=rt[:, 1, :], op=ALU.mult), reads=[pp_, rt], writes=[t1_])
            t2_ = wk.get()
            S.op("pool", lambda e: e.tensor_tensor(out=t2_[0:64, :], in0=qr0[0:64, :], in1=rt[:, 0, :], op=ALU.mult), reads=[qr0, rt], writes=[t2_])
            yield None
            QR = krp.get()
            S.op("dve", lambda e: e.tensor_tensor(out=QR[0:64, :], in0=t1_[0:64, :], in1=t2_[0:64, :], op=ALU.add), reads=[t1_, t2_], writes=[QR])
            yield (kth, vh, QN, QR)


        def run_all(gen):
            r = None
            for r in gen:
                pass
            return r

        items = [(qb, h) for qb in range(NQB) for h in range(8)]
        pending_out = [None]
        nxt = run_all(prep(*items[0]))
        for idx, (qb, h) in enumerate(items):
            kth, vh, QN, QR = nxt
            qn, SZ, mixblk = blocks[qb]
            tok0 = NCTX + qb * 512
            gen = prep(*items[idx + 1]) if idx + 1 < len(items) else iter(())
            nxt = None
            gen_done = [False]
            pO = pOp.get()
            pR = pRp.get()

            def scores(kt):
                ps = pS.get()
                S.op("pe", lambda e: e.matmul(ps[:], lhsT=kth[:, kt * 128:(kt + 1) * 128], rhs=QN[:], start=True, stop=False), reads=[kth, QN], writes=[ps])
                S.op("pe", lambda e: e.matmul(ps[:], lhsT=KR[:, kt * 128:(kt + 1) * 128], rhs=QR[:, :], start=False, stop=True), reads=[KR, QR], writes=[ps])
                return ps

            AHEAD = 2
            psq = [scores(k_) for k_ in range(AHEAD)]
            for kt in range(NT):
                ps = psq.pop(0)
                if kt + AHEAD < NT:
                    psq.append(scores(kt + AHEAD))
                PT = ptp.get()
                S.op("act", lambda e: e.activation(out=PT[:], in_=ps[:], func=AF.Exp), reads=[ps], writes=[PT])
                S.op("pe", lambda e: e.matmul(pO[:], lhsT=vh[:, kt, :], rhs=PT[:], start=(kt == 0), stop=(kt == NT - 1)), reads=[vh, PT], writes=[pO])
                S.op("pe", lambda e: e.matmul(pR[:], lhsT=onesb[:], rhs=PT[:], start=(kt == 0), stop=(kt == NT - 1)), reads=[onesb, PT], writes=[pR])
                if kt % 2 == 1 and not gen_done[0]:
                    r_ = next(gen, "done")
                    if r_ == "done":
                        gen_done[0] = True
                    elif r_ is not None:
                        nxt = r_
                        gen_done[0] = True
                elif gen_done[0] and pending_out[0] is not None:
                    if next(pending_out[0], "done") == "done":
                        pending_out[0] = None
            for r_ in gen:
                if r_ is not None:
                    nxt = r_
            rinv = wk.get()
            S.op("act", lambda e: e.activation(out=rinv[:], in_=pR[:], func=AF.Ln), reads=[pR], writes=[rinv])
            S.op("act", lambda e: e.activation(out=rinv[:], in_=rinv[:], func=AF.Exp, scale=-1.0), reads=[rinv], writes=[rinv])
            o = wk.get()
            S.op("dve", lambda e: e.tensor_tensor(out=o[:], in0=pO[:], in1=rinv[:], op=ALU.mult), reads=[pO, rinv], writes=[o])
            S.op("pool", lambda e: e.tensor_tensor(out=mixblk[:, h, :], in0=o[:], in1=SZ[:, h, :], op=ALU.mult), reads=[o, SZ], writes=[mixblk])
            if h == 7:
                if pending_out[0] is not None:
                    for _ in pending_out[0]:
                        pass
                pending_out[0] = out_proj_gen(nc, S, (xpool, tpool, misc), Wo, mixblk, G, tok0, tok0 + 512,
                                              lambda tok: scr["x1"][tok:tok + 128, :], lambda tok: out_d[tok - NCTX:tok - NCTX + 128, :], out_buf)
                if _os.environ.get("ATT_DEFER", "1") == "0":
                    for _ in pending_out[0]:
                        pass
                    pending_out[0] = None
        if pending_out[0] is not None:
            for _ in pending_out[0]:
                pass
        S.barrier()


def make_in_maps(inp):
    consts = build_consts()
    rope = build_rope()
    lora = np.ascontiguousarray(np.concatenate([inp["ev_w2"][0], inp["ev_a2"][0]], axis=1).transpose(1, 0, 2))
    maps = []
    for b in range(8):
        maps.append({
            "x": np.ascontiguousarray(inp["x"][b]),
            "ctx": np.ascontiguousarray(inp["ctx"][b]),
            "cols": build_cols(inp, b),
            "consts": consts,
            "ada_w": np.ascontiguousarray(inp["ada_w"]),
            "ev_w_in": np.ascontiguousarray(inp["ev_w_in"][0]),
            "ev_lora": lora,
            "ev_w_out": np.ascontiguousarray(inp["ev_w_out"][0]),
            "od_w_in": np.ascontiguousarray(inp["od_w_in"][0]),
            "od_w_qb": np.ascontiguousarray(inp["od_w_qb"][0]),
            "od_w_kvb": np.ascontiguousarray(inp["od_w_kvb"][0]),
            "od_w_o": np.ascontiguousarray(inp["od_w_o"][0]),
            "rope": rope,
        })
    return maps


def kernel(**inp):
    inp = {k: np.asarray(v) for k, v in inp.items()}
    nc = build()
    res = run_bass_kernel_spmd(nc, make_in_maps(inp), core_ids=list(range(8)))
    return np.stack([res.results[b]["out"] for b in range(8)], axis=0).astype(np.float32)
```
